# Optimizing a Trainium2 kernel written in Bass

```python
import math
import jax, jax.numpy as jnp
from jax import lax
import numpy as np

D_MODEL = 2048
BATCH = 2
SEQ = 8192
DEPTH = 1

D_MIX = D_MODEL
D_A = D_MIX // 2
D_B = D_MIX - D_A
N_HEADS_A = 8
HD_A = D_A // N_HEADS_A
N_GROUPS_B = 8
CHUNK = 128
CONV_W = 3
EMB_DIM = 33
N_BANDS = (EMB_DIM - 1) // 2
FILTER_ORDER = 64
DECAY_TARGET = 1e-2
FAST_DECAY_PCT = 0.3
SLOW_DECAY_PCT = 1.5
D_FF = 5504
D_IN = 2 * D_A + 3 * D_B
EPS = 1e-6

kernel_name = "hybrid_sgu_hyena_convglu_encoder"


def rms_norm(x, g):
    xf = x.astype(jnp.float32)
    y = xf * lax.rsqrt(jnp.mean(xf * xf, axis=-1, keepdims=True) + EPS)
    return (y * g.astype(jnp.float32)).astype(x.dtype)


def dwconv3(x, w, b):
    xp = jnp.pad(x, ((0, 0), (1, 1), (0, 0)))
    return xp[:, :-2] * w[0] + xp[:, 1:-1] * w[1] + xp[:, 2:] * w[2] + b


def positional_features(L):
    t = jnp.linspace(0.0, 1.0, L, dtype=jnp.float32)[:, None]
    w = (2.0 * math.pi / L) * jnp.arange(L, dtype=jnp.float32)[:, None]
    f = jnp.linspace(1e-4, N_BANDS - 1, N_BANDS, dtype=jnp.float32)[None]
    z = jnp.concatenate([t, jnp.cos(f * w), -jnp.sin(f * w)], axis=-1)
    return z, t


def hyena_filter(L, w1, b1, w2, b2, w3, b3, freq, wout):
    f32 = jnp.float32
    z, t = positional_features(L)
    fr = freq.astype(f32)
    h = jnp.sin(fr * (z @ w1.astype(f32) + b1.astype(f32)))
    h = jnp.sin(fr * (h @ w2.astype(f32) + b2.astype(f32)))
    h = jnp.sin(fr * (h @ w3.astype(f32) + b3.astype(f32)))
    h = h @ wout.astype(f32)
    max_decay = math.log(DECAY_TARGET) / FAST_DECAY_PCT
    min_decay = math.log(DECAY_TARGET) / SLOW_DECAY_PCT
    deltas = jnp.abs(jnp.linspace(min_decay, max_decay, D_B, dtype=f32))
    window = jnp.exp(-t * jnp.concatenate([deltas, deltas])[None])
    h = h * window
    h_fwd, h_bwd = h[:, :D_B], h[:, D_B:]
    k = jnp.concatenate([h_fwd, jnp.zeros((1, D_B), f32), h_bwd[:0:-1]], axis=0)
    l1 = jnp.sum(jnp.abs(k), axis=0, keepdims=True)
    return k / (l1 + EPS)


def bidir_long_conv(v, k):
    L = v.shape[1]
    V = jnp.fft.rfft(v.astype(jnp.float32), n=2 * L, axis=1)
    K = jnp.fft.rfft(k, n=2 * L, axis=0)
    return jnp.fft.irfft(V * K[None], n=2 * L, axis=1)[:, :L]


def setup_inputs(seed: int = 0) -> dict:
    key = jax.random.key(seed)
    ks = jax.random.split(key, 32)

    def nrm(k, shape, scale):
        return jax.random.normal(k, shape, jnp.float32) * scale

    def gain(k, shape):
        return 1.0 + 0.02 * jax.random.normal(k, shape, jnp.float32)

    Dp = DEPTH
    return {
        "x": nrm(ks[0], (BATCH, SEQ, D_MODEL), 1.0),
        "norm1_g": gain(ks[1], (Dp, D_MODEL)),
        "w_in": nrm(ks[2], (Dp, D_MODEL, D_IN), D_MODEL ** -0.5),
        "sgu_norm_g": gain(ks[3], (Dp, D_A)),
        "sgu_w": nrm(ks[4], (Dp, N_HEADS_A, CHUNK, CHUNK), CHUNK ** -0.5),
        "sgu_b": gain(ks[5], (Dp, N_HEADS_A, CHUNK)),
        "hy_conv_w": nrm(ks[6], (Dp, CONV_W, 3 * D_B), CONV_W ** -0.5),
        "hy_conv_b": nrm(ks[7], (Dp, 3 * D_B), 0.02),
        "hy_f_w1": nrm(ks[8], (Dp, EMB_DIM, FILTER_ORDER), EMB_DIM ** -0.5),
        "hy_f_b1": nrm(ks[9], (Dp, FILTER_ORDER), 0.1),
        "hy_f_w2": nrm(ks[10], (Dp, FILTER_ORDER, FILTER_ORDER), FILTER_ORDER ** -0.5),
        "hy_f_b2": nrm(ks[11], (Dp, FILTER_ORDER), 0.1),
        "hy_f_w3": nrm(ks[12], (Dp, FILTER_ORDER, FILTER_ORDER), FILTER_ORDER ** -0.5),
        "hy_f_b3": nrm(ks[13], (Dp, FILTER_ORDER), 0.1),
        "hy_f_freq": gain(ks[14], (Dp, FILTER_ORDER)),
        "hy_f_wout": nrm(ks[15], (Dp, FILTER_ORDER, 2 * D_B), FILTER_ORDER ** -0.5),
        "hy_d_skip": nrm(ks[16], (Dp, D_B), 1.0),
        "outnorm_a_g": gain(ks[17], (Dp, D_A)),
        "outnorm_b_g": gain(ks[18], (Dp, D_B)),
        "w_out": nrm(ks[19], (Dp, D_MIX, D_MODEL), D_MIX ** -0.5),
        "norm2_g": gain(ks[20], (Dp, D_MODEL)),
        "ffn_w_up": nrm(ks[21], (Dp, D_MODEL, 2 * D_FF), D_MODEL ** -0.5),
        "ffn_dw_w": nrm(ks[22], (Dp, CONV_W, D_FF), CONV_W ** -0.5),
        "ffn_dw_b": nrm(ks[23], (Dp, D_FF), 0.02),
        "ffn_w_down": nrm(ks[24], (Dp, D_FF, D_MODEL), D_FF ** -0.5),
        "final_g": gain(ks[25], (D_MODEL,)),
    }


def reference(x, norm1_g, w_in, sgu_norm_g, sgu_w, sgu_b, hy_conv_w, hy_conv_b,
              hy_f_w1, hy_f_b1, hy_f_w2, hy_f_b2, hy_f_w3, hy_f_b3, hy_f_freq,
              hy_f_wout, hy_d_skip, outnorm_a_g, outnorm_b_g, w_out, norm2_g,
              ffn_w_up, ffn_dw_w, ffn_dw_b, ffn_w_down, final_g):
    B, L, _ = x.shape
    n_chunks = L // CHUNK
    for l in range(DEPTH):
        h = rms_norm(x, norm1_g[l])
        p = h @ w_in[l]
        a_u = p[..., :D_A]
        a_v = p[..., D_A:2 * D_A]
        b_in = p[..., 2 * D_A:]

        zu = jax.nn.gelu(a_u, approximate=False)
        zv = rms_norm(jax.nn.gelu(a_v, approximate=False), sgu_norm_g[l])
        zv = zv.reshape(B, n_chunks, CHUNK, N_HEADS_A, HD_A)
        s = jnp.einsum('hij,bcjhd->bcihd', sgu_w[l], zv)
        s = s + sgu_b[l].T[None, None, :, :, None]
        y_a = zu * s.reshape(B, L, D_A)

        b_c = dwconv3(b_in, hy_conv_w[l], hy_conv_b[l])
        bx0 = b_c[..., :D_B]
        bx1 = b_c[..., D_B:2 * D_B]
        bv = b_c[..., 2 * D_B:]
        k = hyena_filter(L, hy_f_w1[l], hy_f_b1[l], hy_f_w2[l], hy_f_b2[l],
                         hy_f_w3[l], hy_f_b3[l], hy_f_freq[l], hy_f_wout[l])
        u = bv * bx1
        yc = bidir_long_conv(u, k).astype(x.dtype) + u * hy_d_skip[l]
        y_b = bx0 * yc

        merged = jnp.concatenate([rms_norm(y_a, outnorm_a_g[l]),
                                  rms_norm(y_b, outnorm_b_g[l])], axis=-1)
        x = x + merged @ w_out[l]

        h = rms_norm(x, norm2_g[l])
        gv = h @ ffn_w_up[l]
        g = dwconv3(gv[..., :D_FF], ffn_dw_w[l], ffn_dw_b[l])
        val = gv[..., D_FF:]
        x = x + (jax.nn.gelu(g, approximate=False) * val) @ ffn_w_down[l]
    return rms_norm(x, final_g)
```

```python
import contextlib
import math
import numpy as np
import concourse.bass as bass
import concourse.mybir as mybir
from concourse.bass_utils import run_bass_kernel_spmd

F32 = mybir.dt.float32
BF16 = mybir.dt.bfloat16
I32 = mybir.dt.int32
ALU = mybir.AluOpType
AF = mybir.ActivationFunctionType

NCORES = 8
D = 2048
L = 8192
DA = 1024
DB = 1024
DFF = 5504
DIN = 5120
KD = D // 128
KF = DFF // 128
TOK = 2048
HALO = 128
TE = TOK + 2 * HALO
TM = 256
NMT = TE // TM
TF = 512
NFT = TOK // TF
EPS = 1e-6

ENGS = ("pe", "act", "dve", "pool", "sp")
N_DMA_SEMS = 40


class Prog:
    def __init__(self, nc):
        self.nc = nc
        self.stack = contextlib.ExitStack()
        self.ops = {e: [] for e in ENGS}
        self.cnt = {e: 0 for e in ENGS}
        self.res = {}
        self.known = {e: {} for e in ENGS}
        self.sems = {}
        for e in ENGS:
            self.sems[e] = self.stack.enter_context(nc.semaphore("s_" + e))
        self.dma_cum = {}
        self.dma_rr = {}
        for q in ("sp", "act", "pool"):
            n = N_DMA_SEMS if q != "act" else 4
            self.dma_rr[q] = [0, n]
            for k in range(n):
                self.dma_cum[(q, k)] = 0
                self.sems[("d", q, k)] = self.stack.enter_context(nc.semaphore("s_d%s%d" % (q, k)))

    def sb(self, name, shape, dt):
        return self.stack.enter_context(self.nc.sbuf_tensor(name, list(shape), dt))

    def ps(self, name, shape, dt=F32):
        return self.stack.enter_context(self.nc.psum_tensor(name, list(shape), dt))

    def _deps(self, eng, reads, writes):
        evs = []
        for k in reads:
            r = self.res.get(k)
            if r and r["w"] is not None:
                evs.append(r["w"])
        for k in writes:
            r = self.res.get(k)
            if r:
                if r["w"] is not None:
                    evs.append(r["w"])
                evs.extend(r["r"].items())
        waits = {}
        for (s, v) in evs:
            if eng == "pe" and s == "pe":
                continue
            if self.known[eng].get(s, 0) >= v:
                continue
            if waits.get(s, 0) < v:
                waits[s] = v
        for s, v in waits.items():
            self.known[eng][s] = v
        return list(waits.items())

    def _record(self, ev, reads, writes):
        for k in reads:
            r = self.res.setdefault(k, {"w": None, "r": {}})
            if r["r"].get(ev[0], 0) < ev[1]:
                r["r"][ev[0]] = ev[1]
        for k in writes:
            self.res[k] = {"w": ev, "r": {}}

    def op(self, eng, fn, reads=(), writes=(), inc=True):
        waits = self._deps(eng, reads, writes)
        if inc:
            self.cnt[eng] += 1
            ev = (eng, self.cnt[eng])
        else:
            ev = (eng, self.cnt[eng] + 1)
        self.ops[eng].append(("c", fn, waits, inc))
        self._record(ev, reads, writes)

    def dma(self, q, fn, reads=(), writes=(), inc=16):
        k = self.dma_rr[q][0]
        self.dma_rr[q][0] = (k + 1) % self.dma_rr[q][1]
        s = ("d", q, k)
        waits = dict(self._deps(q, reads, writes))
        cum = self.dma_cum[(q, k)]
        if cum > 0 and self.known[q].get(s, 0) < cum:
            waits[s] = cum
            self.known[q][s] = cum
        self.dma_cum[(q, k)] = cum + inc
        ev = (s, cum + inc)
        self.ops[q].append(("d", fn, list(waits.items()), (s, inc)))
        self._record(ev, reads, writes)

    def full_barrier(self):
        evs = [(e, self.cnt[e]) for e in ENGS if self.cnt[e] > 0]
        evs += [(("d", q_, k), v) for (q_, k), v in self.dma_cum.items() if v > 0]
        for e in ENGS:
            waits = []
            for s_, v in evs:
                if self.known[e].get(s_, 0) < v:
                    waits.append((s_, v))
                    self.known[e][s_] = v
            self.ops[e].append(("w", None, waits, False))
        self.res = {}

    def barrier_wait(self, eng, keys):
        waits = self._deps(eng, list(keys), list(keys))
        self.ops[eng].append(("w", None, waits, False))

    def emit(self):
        nc = self.nc
        handles = {"pe": "tensor", "act": "scalar", "dve": "vector", "pool": "gpsimd", "sp": "sync"}
        with nc.Block() as block:
            for e in ENGS:
                ops = self.ops[e]
                if not ops:
                    continue

                def body(eng, ops=ops, e=e):
                    for kind, fn, waits, inc in ops:
                        for s, v in waits:
                            eng.wait_ge(self.sems[s], v)
                        if kind == "w":
                            continue
                        ins = fn(eng)
                        if kind == "c":
                            if inc:
                                ins.then_inc(self.sems[e], 1)
                        else:
                            s, n = inc
                            ins.then_inc(self.sems[s], n)

                getattr(block, handles[e])(body)

    def close(self):
        self.stack.close()


class Arena:
    def __init__(self, t, n):
        self.t, self.n, self.off = t, n, 0

    def reset(self):
        self.off = 0

    def get(self, shape):
        n = int(np.prod(shape[1:]))
        assert self.off + n <= self.n, (self.off, n, self.n)
        ap = self.t[:, self.off:self.off + n]
        self.off += n
        if len(shape) == 3:
            ap = ap.rearrange("p (a b) -> p a b", a=shape[1])
        return ap


def build(nmt=NMT, nft=NFT, kf=KF, debug=False):
    nc = bass.Bass("TRN2", target_bir_lowering=False)

    def din(name, shape, dt=F32):
        return nc.dram_tensor(name, list(shape), dt, kind="ExternalInput").ap()

    xTm = din("xTm", [D, TE])
    mask = din("mask", [128, TE])
    w_in = din("w_in", [DIN // 128, 128, KD, 128])
    w_out = din("w_out", [D // 128, 128, KD, 128])
    w_up = din("w_up", [2 * DFF // 128, 128, KD, 128])
    w_down = din("w_down", [D // 128, 128, KF, 128])
    g1 = din("g1", [128, KD])
    g2 = din("g2", [128, KD])
    gf = din("gf", [128, KD])
    ga = din("ga", [128, 8])
    gb = din("gb", [128, 8])
    gsgu = din("gsgu", [128, DA])
    bsr = din("bsr", [128, 8 * 128])
    wsT = din("wsT", [128, 8 * 128])
    dww = din("dww", [128, KF * 3])
    dwb = din("dwb", [128, KF])
    xTb = din("xTb", [D, (L // TOK) * (TOK + 2)])
    zq = din("zq", [33, 2 * L])
    ttq = din("ttq", [128, 2 * L])
    m2q = din("m2q", [128, 2 * L])
    hw3d = din("hw3d", [64, 128])
    hfr2 = din("hfr2", [128, 1])
    hb32 = din("hb32", [128, 1])
    hwo2 = din("hwo2", [128, DB])
    negd = din("negd", [128, 8])
    hw1 = din("hw1", [33, 64])
    hw2 = din("hw2", [64, 64])
    hw3 = din("hw3", [64, 64])
    hb = din("hb", [64, 3])
    hfr = din("hfr", [64, 1])
    cw = din("cw", [128, 24 * 3])
    cb = din("cb", [128, 24])
    dsk = din("dsk", [128, 8])
    yT = nc.dram_tensor("yT", [D, TOK], F32, kind="ExternalOutput").ap()
    ud = nc.dram_tensor("ud", [DB, L], F32, kind="Internal").ap()
    uo = nc.dram_tensor("uo", [DB, TE], F32, kind="Internal").ap()
    b0 = nc.dram_tensor("b0", [DB, TE], F32, kind="Internal").ap()
    kd = nc.dram_tensor("kd", [DB, 2 * L], F32, kind="Internal").ap()
    ycd = nc.dram_tensor("ycd", [DB, 17 * 128], F32, kind="Internal").ap()
    fT1r = din("fT1r", [128, 512]); fT1i = din("fT1i", [128, 512])
    fT2r = din("fT2r", [128, 512]); fT2i = din("fT2i", [128, 512])
    fF1 = din("fF1", [128, 128])
    fF1d = din("fF1d", [64, 128])
    fF2r = din("fF2r", [128, 128]); fF2i = din("fF2i", [128, 128]); fF2n = din("fF2n", [128, 128])
    fG1 = din("fG1", [128, 256]); fG2 = din("fG2", [128, 256])
    fHc = din("fHc", [128, 32]); fHs = din("fHs", [128, 32])
    x1s = nc.dram_tensor("x1s", [D, TE], F32, kind="Internal").ap()
    ybs = nc.dram_tensor("ybs", [DB, TE], F32, kind="Internal").ap()

    P = Prog(nc)
    ones = P.sb("ones", [128, 128], F32)
    zero = P.sb("zero", [128, TM], F32)
    g1_s = P.sb("g1_s", [128, KD], F32)
    g2_s = P.sb("g2_s", [128, KD], F32)
    gf_s = P.sb("gf_s", [128, KD], F32)
    ga_s = P.sb("ga_s", [128, 8], F32)
    gb_s = P.sb("gb_s", [128, 8], F32)
    gsgu_s = P.sb("gsgu_s", [128, DA], F32)
    bsr_s = P.sb("bsr_s", [128, 8, 128], F32)
    wsT_s = P.sb("wsT_s", [128, 8, 128], BF16)
    dww_s = P.sb("dww_s", [128, KF, 3], F32)
    dwb_s = P.sb("dwb_s", [128, KF], F32)
    mask_s = P.sb("mask_s", [128, TF + 2], F32)
    NA32, NA16 = 19800, 52200
    a32 = Arena(P.sb("arena32", [128, NA32], F32), NA32)
    a16 = Arena(P.sb("arena16", [128, NA16], BF16), NA16)

    eps_s = P.sb("eps_s", [128, 1], F32)
    P.op("dve", lambda e: e.memset(eps_s[:], EPS), writes=["eps"])
    P.op("dve", lambda e: e.memset(ones[:], 1.0), writes=["ones"])
    P.op("dve", lambda e: e.memset(zero[:], 0.0), writes=["zero"])
    for (t, src, nm) in ((g1_s, g1, "g1"), (g2_s, g2, "g2"), (gf_s, gf, "gf"), (ga_s, ga, "ga"),
                         (gb_s, gb, "gb"), (gsgu_s, gsgu, "gsgu"), (dwb_s, dwb, "dwb")):
        P.dma("sp", lambda e, t=t, src=src: e.dma_start(out=t[:], in_=src), writes=[nm])
    P.dma("sp", lambda e: e.dma_start(out=bsr_s[:].rearrange("p h i -> p (h i)"), in_=bsr), writes=["bsr"])
    P.dma("sp", lambda e: e.dma_start(out=dww_s[:].rearrange("p k c -> p (k c)"), in_=dww), writes=["dww"])
    P.dma("pool", lambda e: e.dma_start(out=wsT_s[:].rearrange("p h i -> p (h i)"), in_=wsT), writes=["wsT"])

    NPS = 6
    pdb = [P.ps("pd%d" % i, [128, 1024]) for i in range(4)]
    psb = [pdb[i // 2][:, (i % 2) * 512:(i % 2 + 1) * 512] for i in range(8)]
    ps_st, ps_h = psb[6], psb[7]
    bank_n = [NPS]
    bank_p = [0]

    def next_ps():
        i = bank_p[0] % bank_n[0]
        bank_p[0] = i + 1
        return psb[i], "ps%d" % i

    def next_pd():
        i = bank_p[0] % bank_n[0]
        if i % 2:
            i = (i + 1) % bank_n[0]
        bank_p[0] = i + 2
        return pdb[i // 2][:, :], ["ps%d" % i, "ps%d" % (i + 1)]

    rstd = P.sb("rstd", [128, 520], F32)

    sqb = [P.sb("sqb%d" % i, [128, 520], F32) for i in range(2)]
    sq_rr = [0]

    def stats(src, nk, n, dim, srckeys, outkey="rstd", pieces=None, dst=None):
        pieces = pieces or [(0, n)]
        R = rstd if dst is None else dst
        for (a, b) in pieces:
            for k in range(nk):
                i = sq_rr[0]
                sq_rr[0] = 1 - i
                P.op("act", lambda e, i=i, k=k, a=a, b=b: e.activation(out=sqb[i][:, 0:b - a], in_=src[:, k, a:b], func=AF.Square),
                     reads=srckeys, writes=["sqb%d" % i])
                P.op("pe", lambda e, i=i, k=k, a=a, b=b: e.matmul(ps_st[:, 0:b - a], lhsT=ones[:], rhs=sqb[i][:, 0:b - a],
                                                                  start=(k == 0), stop=(k == nk - 1)),
                     reads=["sqb%d" % i, "ones"], writes=["ps6"])
            P.op("act", lambda e, a=a, b=b: e.activation(out=R[:, a:b], in_=ps_st[:, 0:b - a], func=AF.Sqrt,
                                                         scale=1.0 / dim, bias=eps_s[:, 0:1]),
                 reads=["ps6", "eps"], writes=[outkey])
        P.op("dve", lambda e: e.reciprocal(out=R[:, 0:n], in_=R[:, 0:n]), reads=[outkey], writes=[outkey])

    def wblk_ap(w, col):
        assert col % 128 == 0
        return w[col // 128]
    xTm_v = xTm.rearrange("(k p) t -> p k t", p=128)
    x1s_v = x1s.rearrange("(k p) t -> p k t", p=128)
    ybs_v = ybs.rearrange("(k p) t -> p k t", p=128)
    yT_v = yT.rearrange("(k p) t -> p k t", p=128)
    ybz_keys = []
    for c in range(DB // 128):
        for (a_, b_) in ((0, HALO - 1), (HALO + TOK + 1, TE)):
            key = "ybz_%d_%d" % (c, a_)
            ybz_keys.append(key)
            P.dma("sp", lambda e, c=c, a_=a_, b_=b_: e.dma_start(out=ybs[c * 128:(c + 1) * 128, a_:b_], in_=zero[:, 0:b_ - a_]),
                  reads=["zero"], writes=[key])

    NJ = TOK + 2
    E0 = HALO - 1
    TWO_PI = 2.0 * math.pi
    MAGIC = 12582912.0
    hw1_s = P.sb("hw1_s", [33, 64], F32)
    hw2_s = P.sb("hw2_s", [64, 64], F32)
    hw3_s = P.sb("hw3_s", [64, 64], F32)
    hb_s = P.sb("hb_s", [64, 3], F32)
    hfr_s = P.sb("hfr_s", [64, 1], F32)
    hfb_s = P.sb("hfb_s", [64, 3], F32)
    hwo_s = P.sb("hwo_s", [128, DB], BF16)
    hw3d_s = P.sb("hw3d_s", [64, 128], F32)
    hfr2_s = P.sb("hfr2_s", [128, 1], F32)
    hfb2_s = P.sb("hfb2_s", [128, 1], F32)
    cw_s = P.sb("cw_s", [128, 24, 3], F32)
    cb_s = P.sb("cb_s", [128, 24], F32)
    dsk_s = P.sb("dsk_s", [128, 8], F32)
    negd_s = P.sb("negd_s", [128, 8], F32)
    l1c = P.sb("l1c", [128, 40], F32)
    for (t, src, nm) in ((hw1_s, hw1, "hw1"), (hw2_s, hw2, "hw2"), (hw3_s, hw3, "hw3"), (hb_s, hb, "hb"), (hfr_s, hfr, "hfr"),
                         (cb_s, cb, "cb"), (dsk_s, dsk, "dsk"), (negd_s, negd, "negd")):
        P.dma("sp", lambda e, t=t, src=src: e.dma_start(out=t[:], in_=src), writes=[nm])
    P.dma("sp", lambda e: e.dma_start(out=cw_s[:].rearrange("p k c -> p (k c)"), in_=cw), writes=["cw"])
    P.dma("pool", lambda e: e.dma_start(out=hwo_s[:], in_=hwo2), writes=["hwo"])
    P.dma("sp", lambda e: e.dma_start(out=hw3d_s[:], in_=hw3d), writes=["hw3d"])
    P.dma("sp", lambda e: e.dma_start(out=hfr2_s[:], in_=hfr2), writes=["hfr2"])
    P.dma("sp", lambda e: e.dma_start(out=hfb2_s[:], in_=hb32), writes=["hfb2"])
    P.op("dve", lambda e: e.tensor_scalar(out=hfb2_s[:], in0=hfb2_s[:], scalar1=hfr2_s[:, 0:1], scalar2=None, op0=ALU.mult),
         reads=["hfb2", "hfr2"], writes=["hfb2"])
    P.op("dve", lambda e: e.tensor_scalar(out=hfb_s[:], in0=hb_s[:], scalar1=hfr_s[:, 0:1], scalar2=None, op0=ALU.mult),
         reads=["hb", "hfr"], writes=["hfb"])

    a16f = Arena(a16.t[:, :].bitcast(F32), NA16 // 2)

    def conv3(o, xin, kidx, W, keys_in, key_out):
        P.op("dve", lambda e: e.tensor_scalar(out=o[:, 0:W], in0=xin[:, 0:W], scalar1=cw_s[:, kidx, 1:2], scalar2=cb_s[:, kidx:kidx + 1],
                                              op0=ALU.mult, op1=ALU.add), reads=keys_in + ["cw", "cb"], writes=[key_out])
        P.op("dve", lambda e: e.scalar_tensor_tensor(out=o[:, 1:W], in0=xin[:, 0:W - 1], scalar=cw_s[:, kidx, 0:1], in1=o[:, 1:W],
                                                     op0=ALU.mult, op1=ALU.add), reads=keys_in + ["cw", key_out], writes=[key_out])
        P.op("dve", lambda e: e.scalar_tensor_tensor(out=o[:, 0:W - 1], in0=xin[:, 1:W], scalar=cw_s[:, kidx, 2:3], in1=o[:, 0:W - 1],
                                                     op0=ALU.mult, op1=ALU.add), reads=keys_in + ["cw", key_out], writes=[key_out])

    xTb_v = xTb.rearrange("(k p) t -> p k t", p=128)
    P.full_barrier()
    a32.reset(); a16.reset()
    WS = 2304
    xch = [a32.get([128, KD, 256]) for _ in range(2)]
    prb = [a32.get([128, WS]) for _ in range(2)]
    cvb = [a32.get([128, WS]) for _ in range(2)]
    hTs = a16.get([128, KD, WS])
    rstdS = a32.get([128, WS])
    wbl = [a16.get([128, KD, 128]) for _ in range(4)]
    w_rr = [0]
    x_rr = [0]

    def build_hT(src_v, col0, ncols):
        c = 0
        while c < ncols:
            w = min(256, ncols - c)
            i = x_rr[0]; x_rr[0] = 1 - i
            xt, xk = xch[i], "xch%d" % i
            P.dma("sp", lambda e, xt=xt, c=c, w=w: e.dma_start(out=xt[:, :, 0:w], in_=src_v[:, :, col0 + c:col0 + c + w]), writes=[xk])
            for k in range(KD):
                P.op("dve", lambda e, k=k, xt=xt, c=c, w=w: e.tensor_scalar(
                    out=hTs[:, k, c:c + w], in0=xt[:, k, 0:w], scalar1=g1_s[:, k:k + 1], scalar2=None, op0=ALU.mult),
                    reads=[xk, "g1"], writes=["hTs"])
            stats(xt, KD, w, D, [xk], outkey="rstdS_%d" % c, dst=rstdS[:, c:c + w])
            c += w

    def apply_rstd(buf, key, ncols):
        ks = ["rstdS_%d" % c_ for c_ in range(0, ncols, 256)]
        P.op("dve", lambda e: e.tensor_tensor(out=buf[:, 0:ncols], in0=buf[:, 0:ncols], in1=rstdS[:, 0:ncols], op=ALU.mult),
             reads=[key] + ks, writes=[key])

    def proj_block(ncols, wcol, dst, dkey):
        i = w_rr[0]; w_rr[0] = (i + 1) % 4
        P.dma("pool", lambda e: e.dma_start(out=wbl[i], in_=wblk_ap(w_in, wcol)), writes=["wbl%d" % i])
        c = 0
        while c < ncols:
            w = min(512, ncols - c)
            pt, pk = next_ps()
            for k in range(KD):
                P.op("pe", lambda e, k=k, pt=pt, c=c, w=w: e.matmul(pt[:, 0:w], lhsT=wbl[i][:, k, :], rhs=hTs[:, k, c:c + w],
                                                                    start=(k == 0), stop=(k == KD - 1)),
                     reads=["wbl%d" % i, "hTs"], writes=[pk], inc=(k == KD - 1))
            P.op("act", lambda e, pt=pt, c=c, w=w: e.copy(out=dst[:, c:c + w], in_=pt[:, 0:w]), reads=[pk], writes=[dkey])
            c += w

    for st in range(L // TOK):
        build_hT(xTb_v, st * (TOK + 2), TOK + 2)
        for cc in range(8):
            proj_block(TOK + 2, 3 * DB + cc * 128, prb[0], "prb0")
            proj_block(TOK + 2, 4 * DB + cc * 128, prb[1], "prb1")
            apply_rstd(prb[0], "prb0", TOK + 2)
            apply_rstd(prb[1], "prb1", TOK + 2)
            conv3(cvb[0], prb[0], 8 + cc, TOK + 2, ["prb0"], "cvb0")
            conv3(cvb[1], prb[1], 16 + cc, TOK + 2, ["prb1"], "cvb1")
            P.op("dve", lambda e: e.tensor_tensor(out=cvb[0][:, 1:TOK + 1], in0=cvb[0][:, 1:TOK + 1], in1=cvb[1][:, 1:TOK + 1], op=ALU.mult),
                 reads=["cvb0", "cvb1"], writes=["cvb0"])
            P.dma("sp", lambda e, cc=cc, st=st: e.dma_start(out=ud[cc * 128:(cc + 1) * 128, st * TOK:(st + 1) * TOK],
                                                            in_=cvb[0][:, 1:TOK + 1]), reads=["cvb0"], writes=["ud_%d_%d" % (cc, st)])
    build_hT(xTm_v, 0, TE)
    for cc in range(8):
        proj_block(TE, 2 * DB + cc * 128, prb[0], "prb0")
        apply_rstd(prb[0], "prb0", TE)
        conv3(cvb[0], prb[0], cc, TE, ["prb0"], "cvb0")
        P.dma("sp", lambda e, cc=cc: e.dma_start(out=b0[cc * 128:(cc + 1) * 128, :], in_=cvb[0][:, 0:TE]), reads=["cvb0"], writes=["b0"])
    P.full_barrier()

    NFFT = 2 * L
    a32.reset(); a16.reset()
    t512 = [a32.get([128, 512]) for _ in range(3)]
    kch = [a32.get([128, 512]) for _ in range(2)]
    stg2 = [a32.get([128, 512]) for _ in range(2)]
    accA = a32.get([128, NJ]); b0t = a32.get([128, NJ])
    za = [a32.get([128, 512]) for _ in range(2)]
    zr = a32.get([128, 512])

    def bf512():
        return a32.get([128, 256]).bitcast(BF16)

    EV = {nm: [(bf512(), bf512()) for _ in range(2)] for nm in ("A", "B", "C", "D")}
    TMP = [[bf512() for _ in range(4)] for _ in range(2)]
    tmp_rr = [0]
    h3T = a16.get([128, 2 * L])
    Ub = [a16.get([128, 32, 128]) for _ in range(2)]
    Kb = [a16.get([128, 32, 128]) for _ in range(2)]
    AfrB = [a16.get([128, 512]) for _ in range(2)]; AfiB = [a16.get([128, 512]) for _ in range(2)]
    AdrB = [a16.get([128, 512]) for _ in range(2)]; AdiB = [a16.get([128, 512]) for _ in range(2)]
    YrB = [a16.get([128, 8, 64]) for _ in range(2)]; YiB = [a16.get([128, 8, 64]) for _ in range(2)]
    ZrB = [a16.get([128, 512]) for _ in range(2)]; ZiB = [a16.get([128, 512]) for _ in range(2)]
    KrB = [a16.get([128, 512]) for _ in range(2)]; KiB = [a16.get([128, 512]) for _ in range(2)]
    T1r8 = a16.get([128, 512]); T1i8 = a16.get([128, 512])
    T2r4 = a16.get([128, 512]); T2i4 = a16.get([128, 512])
    F1m = a16.get([128, 128])
    F1d = a16.get([128, 128])
    uot = a16.get([128, 2 * NJ]).bitcast(F32)
    F2r_m = a16.get([128, 128]); F2i_m = a16.get([128, 128]); F2n_m = a16.get([128, 128])
    G1m = a16.get([128, 256]); G2m = a16.get([128, 256])
    Hcm = a16.get([128, 32]); Hsm = a16.get([128, 32])
    for (dst_, src_, nm) in ((T1r8, fT1r, "T1r"), (T1i8, fT1i, "T1i"), (T2r4, fT2r, "T2r"), (T2i4, fT2i, "T2i"), (F1m, fF1, "F1m"), (F1d[0:64, :], fF1d, "F1d"), (F2r_m, fF2r, "F2r"), (F2i_m, fF2i, "F2i"), (F2n_m, fF2n, "F2n"),
                             (G1m, fG1, "G1"), (G2m, fG2, "G2"), (Hcm, fHc, "Hc"), (Hsm, fHs, "Hs")):
        P.dma("pool", lambda e, dst_=dst_, src_=src_: e.dma_start(out=dst_, in_=src_), writes=[nm])

    NPC = 2 * L // 512
    zaS = [za, [a32.get([128, 512]) for _ in range(2)]]
    zrS = [zr, a32.get([128, 512])]
    zinS = [t512[0], t512[1]]
    m2S = [t512[2], kch[0]]
    for pc0 in range(0, NPC, 2):
        st_ = []
        for s_i in range(2):
            pc = pc0 + s_i
            zc, zk = zinS[s_i], "zin%d" % s_i
            P.dma("sp", lambda e, pc=pc, zc=zc: e.dma_start(out=zc[0:33, :], in_=zq[:, pc * 512:(pc + 1) * 512]), writes=[zk])
            P.dma("sp", lambda e, pc=pc, s_i=s_i: e.dma_start(out=m2S[s_i], in_=m2q[:, pc * 512:(pc + 1) * 512]), writes=["m2_%d" % s_i])
            st_.append([zc, zk, 33])
        for li, wl in enumerate((hw1_s, hw2_s, hw3d_s)):
            nr = 64 if li < 2 else 128
            sc1 = hfr_s[:, 0:1] if li < 2 else hfr2_s[:, 0:1]
            sc2 = hfb_s[:, li:li + 1] if li < 2 else hfb2_s[:, 0:1]
            pts = []
            for s_i in range(2):
                cur, curk, kdim = st_[s_i]
                pt, pk = next_ps()
                pts.append((pt, pk))
                P.op("pe", lambda e, pt=pt, cur=cur, kdim=kdim, wl=wl, nr=nr: e.matmul(pt[0:nr, :], lhsT=wl[0:kdim, :], rhs=cur[0:kdim, :],
                                                                         start=True, stop=True),
                     reads=[curk, "hw%d" % (li + 1), "hw3d"], writes=[pk])
            aa = [(zaS[s_i][li % 2], "za%d_%d" % (s_i, li % 2)) for s_i in range(2)]
            zz = [(zrS[s_i], "zr%d" % s_i) for s_i in range(2)]
            for s_i in range(2):
                (pt, pk), (a_, ak) = pts[s_i], aa[s_i]
                P.op("dve", lambda e, pt=pt, a_=a_, nr=nr, sc1=sc1, sc2=sc2: e.tensor_scalar(out=a_[0:nr, :], in0=pt[0:nr, :], scalar1=sc1, scalar2=sc2,
                                                                    op0=ALU.mult, op1=ALU.add),
                     reads=[pk, "hfr", "hfb", "hfr2", "hfb2"], writes=[ak])
            for s_i in range(2):
                (a_, ak), (z_, zk_) = aa[s_i], zz[s_i]
                P.op("dve", lambda e, a_=a_, z_=z_, nr=nr: e.tensor_scalar(out=z_[0:nr, :], in0=a_[0:nr, :], scalar1=1.0 / TWO_PI, scalar2=MAGIC,
                                                                    op0=ALU.mult, op1=ALU.add), reads=[ak], writes=[zk_])
            for s_i in range(2):
                (z_, zk_) = zz[s_i]
                P.op("dve", lambda e, z_=z_, nr=nr: e.tensor_scalar(out=z_[0:nr, :], in0=z_[0:nr, :], scalar1=MAGIC, scalar2=-TWO_PI,
                                                             op0=ALU.subtract, op1=ALU.mult), reads=[zk_], writes=[zk_])
            for s_i in range(2):
                (a_, ak), (z_, zk_) = aa[s_i], zz[s_i]
                P.op("dve", lambda e, a_=a_, z_=z_, nr=nr: e.tensor_tensor(out=a_[0:nr, :], in0=a_[0:nr, :], in1=z_[0:nr, :], op=ALU.add),
                     reads=[ak, zk_], writes=[ak])
            for s_i in range(2):
                (a_, ak) = aa[s_i]
                P.op("dve", lambda e, a_=a_, nr=nr: e.tensor_scalar(out=a_[0:nr, :], in0=a_[0:nr, :], scalar1=-math.pi, scalar2=math.pi,
                                                             op0=ALU.max, op1=ALU.min), reads=[ak], writes=[ak])
            for s_i in range(2):
                (a_, ak) = aa[s_i]
                P.op("act", lambda e, a_=a_, nr=nr: e.activation(out=a_[0:nr, :], in_=a_[0:nr, :], func=AF.Sin), reads=[ak], writes=[ak])
                st_[s_i] = [a_, ak, 64]
            if li == 2:
                for s_i in range(2):
                    (a_, ak) = aa[s_i]
                    pc = pc0 + s_i
                    P.op("dve", lambda e, a_=a_, pc=pc, s_i=s_i: e.tensor_tensor(out=h3T[:, pc * 512:(pc + 1) * 512], in0=a_[:, :], in1=m2S[s_i],
                                                                                 op=ALU.mult),
                         reads=[ak, "m2_%d" % s_i], writes=["h3T"])
    P.full_barrier()

    ud_a = ud.rearrange("c (a r) -> a c r", r=128)
    kd_a = kd.rearrange("c (a r) -> a c r", r=128)
    yc_a = ycd.rearrange("c (a r) -> a c r", r=128)

    def cmul(o_re, o_im, p_re, p_im, t_re, t_im, rk, wk):
        i = tmp_rr[0]
        tmp_rr[0] = 1 - i
        t1, t2, t3, t4 = TMP[i]
        k1, k2, k3, k4 = ["tmp%d_%d" % (i, j) for j in range(4)]
        P.op("dve", lambda e: e.tensor_tensor(out=t1, in0=p_re, in1=t_re, op=ALU.mult), reads=rk, writes=[k1])
        P.op("dve", lambda e: e.tensor_tensor(out=t2, in0=p_im, in1=t_im, op=ALU.mult), reads=rk, writes=[k2])
        P.op("dve", lambda e: e.tensor_tensor(out=t3, in0=p_re, in1=t_im, op=ALU.mult), reads=rk, writes=[k3])
        P.op("dve", lambda e: e.tensor_tensor(out=t4, in0=p_im, in1=t_re, op=ALU.mult), reads=rk, writes=[k4])
        P.op("dve", lambda e: e.tensor_tensor(out=o_re, in0=t1, in1=t2, op=ALU.subtract), reads=[k1, k2], writes=wk[0:1])
        P.op("dve", lambda e: e.tensor_tensor(out=o_im, in0=t3, in1=t4, op=ALU.add), reads=[k3, k4], writes=wk[1:2])

    def evac(stage, gi, src_re, src_im, srck, shape3=None):
        er, ei = EV[stage][gi]
        kr, ki = "ev%s%d_r" % (stage, gi), "ev%s%d_i" % (stage, gi)
        o_r = er if shape3 is None else v3(er, *shape3)
        o_i = ei if shape3 is None else v3(ei, *shape3)
        P.op("act", lambda e: e.copy(out=o_r, in_=src_re), reads=srck, writes=[kr])
        P.op("act", lambda e: e.copy(out=o_i, in_=src_im), reads=srck, writes=[ki])
        return er, ei, [kr, ki]

    def v3(ap2d, a, b):
        return ap2d.rearrange("p (a b) -> p a b", a=a)

    def fwd(src, nrow, srck, A_re, A_im, akeys, via_pool=None):
        pd_, pdk = next_pd()
        for j in range(8):
            P.op("pe", lambda e, j=j: e.matmul(pd_[:, j * 128:(j + 1) * 128], lhsT=src[0:nrow, j, :], rhs=F1m[0:nrow, :],
                                               start=True, stop=True), reads=[srck, "F1m"], writes=pdk, inc=(j == 7))
        if via_pool is None:
            pv = pd_.rearrange("p (c t k) -> p c t k", c=8, t=2)
            cmul(v3(A_re, 8, 64), v3(A_im, 8, 64), pv[:, :, 0, :], pv[:, :, 1, :], v3(T1r8, 8, 64), v3(T1i8, 8, 64),
                 pdk + ["T1r", "T1i"], akeys)
        else:
            sf, sfk = via_pool
            P.op("act", lambda e: e.copy(out=sf, in_=pd_), reads=pdk, writes=[sfk])
            pv = sf.rearrange("p (c t k) -> p c t k", c=8, t=2)
            cmul(v3(A_re, 8, 64), v3(A_im, 8, 64), pv[:, :, 0, :], pv[:, :, 1, :], v3(T1r8, 8, 64), v3(T1i8, 8, 64),
                 [sfk, "T1r", "T1i"], akeys, eng="pool")
        xr, xrk = next_ps()
        xi, xik = next_ps()
        P.op("pe", lambda e: e.matmul(xr[:, :], lhsT=F2r_m, rhs=A_re, start=True, stop=False), reads=["F2r", akeys[0]], writes=[xrk], inc=False)
        P.op("pe", lambda e: e.matmul(xr[:, :], lhsT=F2n_m, rhs=A_im, start=False, stop=True), reads=["F2n", akeys[1]], writes=[xrk])
        P.op("pe", lambda e: e.matmul(xi[:, :], lhsT=F2i_m, rhs=A_re, start=True, stop=False), reads=["F2i", akeys[0]], writes=[xik], inc=False)
        P.op("pe", lambda e: e.matmul(xi[:, :], lhsT=F2r_m, rhs=A_im, start=False, stop=True), reads=["F2r", akeys[1]], writes=[xik])
        return xr, xi, [xrk, xik]

    l1cB = [l1c, P.sb("l1c2", [128, 40], F32)]

    kch.append(a32.get([128, 512]))

    def taps_slice(cc, pcs):
        l1 = l1cB[cc % 2]
        l1k = "l1c_%d" % (cc % 2)
        pcs = list(pcs)
        for g0 in range(0, len(pcs), 3):
            grp_ = pcs[g0:g0 + 3]
            pfs = []
            for pc in grp_:
                j = pc % 3
                sl = slice(pc * 512, (pc + 1) * 512)
                P.dma("sp", lambda e, j=j, sl=sl: e.dma_start(out=t512[j], in_=ttq[:, sl]), writes=["t512_%d" % j])
            for pc in grp_:
                j = pc % 3
                sl = slice(pc * 512, (pc + 1) * 512)
                pf, pfk = next_ps()
                pfs.append((pf, pfk))
                P.op("pe", lambda e, pf=pf, sl=sl: e.matmul(pf[:, :], lhsT=hwo_s[:, cc * 128:(cc + 1) * 128], rhs=h3T[:, sl],
                                                            start=True, stop=True), reads=["hwo", "h3T"], writes=[pfk])
                P.op("act", lambda e, j=j: e.activation(out=t512[j], in_=t512[j], func=AF.Exp, scale=negd_s[:, cc:cc + 1]),
                     reads=["t512_%d" % j, "negd"], writes=["t512_%d" % j])
            for pc, (pf, pfk) in zip(grp_, pfs):
                j = pc % 3
                P.op("dve", lambda e, pf=pf, j=j: e.tensor_tensor(out=kch[j], in0=pf[:, :], in1=t512[j], op=ALU.mult),
                     reads=[pfk, "t512_%d" % j], writes=["kch%d" % j])
            for pc in grp_:
                j = pc % 3
                sl = slice(pc * 512, (pc + 1) * 512)
                P.op("act", lambda e, j=j, pc=pc: e.activation(out=za[0], in_=kch[j], func=AF.Abs, accum_out=l1[:, pc:pc + 1]),
                     reads=["kch%d" % j], writes=["za0", l1k])
                P.dma("pool", lambda e, j=j, sl=sl: e.dma_start(out=kd[cc * 128:(cc + 1) * 128, sl], in_=kch[j]), reads=["kch%d" % j],
                      writes=["kd_%d_%d" % (cc, pc)])

    def taps_finish(cc):
        l1 = l1cB[cc % 2]
        l1k = "l1c_%d" % (cc % 2)
        P.op("dve", lambda e: e.tensor_reduce(out=l1[:, 32:33], in_=l1[:, 0:NPC], axis=mybir.AxisListType.X, op=ALU.add),
             reads=[l1k], writes=[l1k])
        P.op("dve", lambda e: e.tensor_scalar(out=l1[:, 32:33], in0=l1[:, 32:33], scalar1=EPS, scalar2=None, op0=ALU.add),
             reads=[l1k], writes=[l1k])
        P.op("dve", lambda e: e.reciprocal(out=l1[:, 33:34], in_=l1[:, 32:33]), reads=[l1k], writes=[l1k])

    def load_sub(cc, sc):
        slot = sc % 2
        cbase = cc * 128 + sc * 32
        P.dma("pool", lambda e: e.dma_start(out=Ub[slot][0:64, :, :], in_=ud_a[:, cbase:cbase + 32, :]),
              reads=["ud_%d_%d" % (cc, st_) for st_ in range(L // TOK)], writes=["Ub%d" % slot])
        P.dma("pool", lambda e: e.dma_start(out=Kb[slot], in_=kd_a[:, cbase:cbase + 32, :]),
              reads=["kd_%d_%d" % (cc, pc_) for pc_ in range(NPC)], writes=["Kb%d" % slot])

    class Grp:
        pass

    def mk(t):
        G_ = Grp()
        G_.cc, rem = divmod(t, 16)
        G_.sc, G_.g = divmod(rem, 4)
        G_.slot = G_.sc % 2
        G_.c0 = G_.cc * 128 + G_.sc * 32 + G_.g * 8
        gi = t % 2
        G_.gi = gi
        G_.sfx = "_%d" % gi
        G_.Afr, G_.Afi, G_.Adr, G_.Adi = AfrB[gi], AfiB[gi], AdrB[gi], AdiB[gi]
        G_.Yr, G_.Yi, G_.Zr, G_.Zi, G_.Kr, G_.Ki = YrB[gi], YiB[gi], ZrB[gi], ZiB[gi], KrB[gi], KiB[gi]
        return G_

    def s1(src, nrow, srck):
        pd_, pdk = next_pd()
        fm, fk = (F1m, "F1m") if nrow == 128 else (F1d, "F1d")
        for j in range(8):
            P.op("pe", lambda e, j=j: e.matmul(pd_[:, j * 128:(j + 1) * 128], lhsT=src[0:nrow, j, :], rhs=fm[0:nrow, :],
                                               start=True, stop=True), reads=[srck, fk], writes=pdk, inc=(j == 7))
        return pd_, pdk

    def s2(A_re, A_im, akeys):
        xr, xrk = next_ps()
        xi, xik = next_ps()
        P.op("pe", lambda e: e.matmul(xr[:, :], lhsT=F2r_m, rhs=A_re, start=True, stop=False), reads=["F2r", akeys[0]], writes=[xrk], inc=False)
        P.op("pe", lambda e: e.matmul(xr[:, :], lhsT=F2n_m, rhs=A_im, start=False, stop=True), reads=["F2n", akeys[1]], writes=[xrk])
        P.op("pe", lambda e: e.matmul(xi[:, :], lhsT=F2i_m, rhs=A_re, start=True, stop=False), reads=["F2i", akeys[0]], writes=[xik], inc=False)
        P.op("pe", lambda e: e.matmul(xi[:, :], lhsT=F2r_m, rhs=A_im, start=False, stop=True), reads=["F2r", akeys[1]], writes=[xik])
        return xr, xi, [xrk, xik]

    def stA(G_):
        pd_, pdk = s1(Kb[G_.slot][:, G_.g * 8:(G_.g + 1) * 8, :], 128, "Kb%d" % G_.slot)
        pv = pd_.rearrange("p (c t k) -> p c t k", c=8, t=2)
        er, ei, ek = evac("A", G_.gi, pv[:, :, 0, :], pv[:, :, 1, :], pdk, (8, 64))
        cmul(G_.Afr, G_.Afi, er, ei, T1r8, T1i8, ek + ["T1r", "T1i"], ["Afr" + G_.sfx, "Afi" + G_.sfx])

    def stB(G_):
        xr, xi, xk = s2(G_.Afr, G_.Afi, ["Afr" + G_.sfx, "Afi" + G_.sfx])
        P.op("act", lambda e: e.copy(out=G_.Kr, in_=xr[:, :]), reads=xk[0:1], writes=["Kr" + G_.sfx])
        P.op("act", lambda e: e.copy(out=G_.Ki, in_=xi[:, :]), reads=xk[1:2], writes=["Ki" + G_.sfx])
        pd_, pdk = s1(Ub[G_.slot][:, G_.g * 8:(G_.g + 1) * 8, :], 64, "Ub%d" % G_.slot)
        pv = pd_.rearrange("p (c t k) -> p c t k", c=8, t=2)
        er, ei, ek = evac("B", G_.gi, pv[:, :, 0, :], pv[:, :, 1, :], pdk, (8, 64))
        cmul(G_.Adr, G_.Adi, er, ei, T1r8, T1i8, ek + ["T1r", "T1i"], ["Adr" + G_.sfx, "Adi" + G_.sfx])

    def stC(G_):
        xr, xi, xk = s2(G_.Adr, G_.Adi, ["Adr" + G_.sfx, "Adi" + G_.sfx])
        er, ei, ek = evac("C", G_.gi, xr[:, :], xi[:, :], xk)
        cmul(G_.Yr.rearrange("p c k -> p (c k)"), G_.Yi.rearrange("p c k -> p (c k)"), er, ei, G_.Kr, G_.Ki,
             ek + ["Kr" + G_.sfx, "Ki" + G_.sfx], ["Yr" + G_.sfx, "Yi" + G_.sfx])

    def stD(G_):
        pz, pzk = next_pd()
        for j in range(8):
            p_, hf = divmod(j, 2)
            o_ = pz[hf * 64:(hf + 1) * 64, p_ * 256:(p_ + 1) * 256]
            P.op("pe", lambda e, o_=o_, j=j, hf=hf: e.matmul(o_, lhsT=G_.Yr[:, j, :], rhs=G1m, start=True, stop=False,
                                                             tile_position=(0, hf * 64)),
                 reads=["Yr" + G_.sfx, "G1"], writes=pzk, inc=False)
            P.op("pe", lambda e, o_=o_, j=j, hf=hf: e.matmul(o_, lhsT=G_.Yi[:, j, :], rhs=G2m, start=False, stop=True,
                                                             tile_position=(0, hf * 64)),
                 reads=["Yi" + G_.sfx, "G2"], writes=pzk, inc=(j == 7))
        zv_ = pz.rearrange("p (c t r) -> p c t r", c=4, t=2)
        er, ei, ek = evac("D", G_.gi, zv_[:, :, 0, :], zv_[:, :, 1, :], pzk, (4, 128))
        cmul(G_.Zr, G_.Zi, er, ei, T2r4, T2i4, ek + ["T2r", "T2i"], ["Zr" + G_.sfx, "Zi" + G_.sfx])

    def stE(G_):
        for hf in range(2):
            py, pyk = next_ps()
            lo = hf * 64
            P.op("pe", lambda e, py=py, lo=lo: e.matmul(py[0:32, :], lhsT=Hcm[lo:lo + 64, :], rhs=G_.Zr[lo:lo + 64, :],
                                                        start=True, stop=False), reads=["Hc", "Zr" + G_.sfx], writes=[pyk], inc=False)
            P.op("pe", lambda e, py=py, lo=lo: e.matmul(py[0:32, :], lhsT=Hsm[lo:lo + 64, :], rhs=G_.Zi[lo:lo + 64, :],
                                                        start=False, stop=True), reads=["Hs", "Zi" + G_.sfx], writes=[pyk])
            sg = stg2[hf]
            P.op("act", lambda e, py=py, sg=sg: e.copy(out=sg[0:32, :], in_=py[0:32, :]), reads=[pyk], writes=["stg2_%d" % hf])
            P.dma("pool", lambda e, sg=sg, hf=hf: e.dma_start(
                out=yc_a[0:17, G_.c0 + hf:G_.c0 + 8:2, :], in_=sg[0:17, :].rearrange("p (c r) -> p c r", c=4)),
                reads=["stg2_%d" % hf], writes=["ycd_%d_%d" % (G_.c0, hf)])

    def combine(cc):
        l1 = l1cB[cc % 2]
        l1k = "l1c_%d" % (cc % 2)
        P.dma("sp", lambda e: e.dma_start(out=accA, in_=ycd[cc * 128:(cc + 1) * 128, 0:NJ]),
              reads=["ycd_%d_%d" % (cc * 128 + g8 * 8, hf) for g8 in range(16) for hf in range(2)], writes=["accA"])
        P.dma("sp", lambda e: e.dma_start(out=b0t, in_=b0[cc * 128:(cc + 1) * 128, E0:E0 + NJ]), reads=["b0"], writes=["b0t"])
        udk = ["ud_%d_%d" % (cc, st_) for st_ in range(L // TOK)]
        P.dma("sp", lambda e: e.dma_start(out=uot[:, 0:1], in_=ud[cc * 128:(cc + 1) * 128, L - 1:L], allow_slow_non_contiguous=True), reads=udk, writes=["uot"])
        P.dma("sp", lambda e: e.dma_start(out=uot[:, 1:NJ], in_=ud[cc * 128:(cc + 1) * 128, 0:NJ - 1]), reads=udk, writes=["uot2"])
        P.op("dve", lambda e: e.tensor_scalar(out=accA, in0=accA, scalar1=l1[:, 33:34], scalar2=None, op0=ALU.mult),
             reads=["accA", l1k], writes=["accA"])
        P.op("dve", lambda e: e.scalar_tensor_tensor(out=accA, in0=uot, scalar=dsk_s[:, cc:cc + 1], in1=accA,
                                                     op0=ALU.mult, op1=ALU.add), reads=["uot", "uot2", "dsk", "accA"], writes=["accA"])
        P.op("dve", lambda e: e.tensor_tensor(out=accA, in0=accA, in1=b0t, op=ALU.mult), reads=["accA", "b0t"], writes=["accA"])
        P.dma("sp", lambda e: e.dma_start(out=ybs[cc * 128:(cc + 1) * 128, E0:E0 + NJ], in_=accA), reads=["accA"], writes=["ybs_%d" % cc])

    NG = 128
    bank_n[0] = 8
    taps_slice(0, range(NPC))
    taps_finish(0)
    load_sub(0, 0)
    grp = {}
    for t in range(NG + 4):
        if t < NG:
            grp[t] = mk(t)
            stA(grp[t])
        if 0 <= t - 1 < NG:
            stB(grp[t - 1])
        if 0 <= t - 2 < NG:
            stC(grp[t - 2])
        if 0 <= t - 3 < NG:
            stD(grp[t - 3])
        if 0 <= t - 4 < NG:
            stE(grp[t - 4])
            if (t - 4) % 16 == 15:
                combine((t - 4) // 16)
            del grp[t - 4]
        if t < NG:
            cc, rem = divmod(t, 16)
            if cc + 1 < 8 and rem < 11:
                lo_ = rem * 3
                hi_ = min(lo_ + 3, NPC)
                taps_slice(cc + 1, range(lo_, hi_))
                if rem == 10:
                    taps_finish(cc + 1)
            if rem % 4 == 0:
                nt = t + 4
                if nt < NG:
                    load_sub(nt // 16, (nt % 16) // 4)
    P.full_barrier()
    bank_n[0] = NPS
    bank_p[0] = 0
    a32.reset(); a16.reset()

    xTB = [a32.get([128, KD, TM]) for _ in range(3)]
    zu = a32.get([128, 8, TM])
    gav = a32.get([128, DA])
    ssq = a32.get([128, 2])
    ya = a32.get([128, 8, TM])
    yb = a32.get([128, 8, TM])
    stmp = a32.get([128, 128])
    hTB2 = [a16.get([128, KD, TM]) for _ in range(2)]
    wblk = [a16.get([128, KD, 128]) for i in range(4)]
    wv = [a16.get([128, KD, 512]) for i in range(2)]
    NRES = 6
    wres = [a16.get([128, KD, 128]) for i in range(NRES)]
    zv = a16.get([128, DA])
    junk = a16.get([128, DA])
    mT = a16.get([128, KD, TM])
    wb_rr = [0]

    def load_wblk(src_ap):
        i = wb_rr[0]
        wb_rr[0] = (i + 1) % 4
        P.dma("pool", lambda e: e.dma_start(out=wblk[i][:], in_=src_ap), writes=["wblk%d" % i])
        return wblk[i], "wblk%d" % i

    def load_wv():
        for half in range(2):
            for j in range(4):
                P.dma("pool", lambda e, half=half, j=j: e.dma_start(out=wv[half][:, :, j * 128:(j + 1) * 128],
                                                                    in_=wblk_ap(w_in, DA + half * 512 + j * 128)),
                      writes=["wv%d_%d" % (half, j)])

    def mF1a(m):
        xT, xk = xTB[m % 3], "xT%d" % (m % 3)
        P.dma("sp", lambda e: e.dma_start(out=xT, in_=xTm_v[:, :, m * TM:(m + 1) * TM]), writes=[xk])

    def mF1b(m):
        xT, xk = xTB[m % 3], "xT%d" % (m % 3)
        hT, hk = hTB2[m % 2], "hT%d" % (m % 2)
        stats(xT, KD, TM, D, [xk])
        for k in range(KD):
            P.op("dve", lambda e, k=k: e.scalar_tensor_tensor(out=hT[:, k, :], in0=xT[:, k, :], scalar=g1_s[:, k:k + 1],
                                                              in1=rstd[:, 0:TM], op0=ALU.mult, op1=ALU.mult),
                 reads=[xk, "g1", "rstd"], writes=[hk])

    def mF2(m):
        hT, hk = hTB2[m % 2], "hT%d" % (m % 2)
        for h in range(8):
            if h < NRES:
                wt, wk = wres[h], "wres%d" % h
            else:
                wt, wk = load_wblk(wblk_ap(w_in, h * 128))
            pt, pk = next_ps()
            for k in range(KD):
                P.op("pe", lambda e, k=k, wt=wt, pt=pt: e.matmul(pt[:, 0:TM], lhsT=wt[:, k, :], rhs=hT[:, k, :],
                                                                 start=(k == 0), stop=(k == KD - 1)),
                     reads=[wk, hk], writes=[pk], inc=(k == KD - 1))
            P.op("act", lambda e, h=h, pt=pt: e.activation(out=zu[:, h, :], in_=pt[:, 0:TM], func=AF.Gelu),
                 reads=[pk], writes=["zu"])
        for tb in range(TM // 128):
            for half in range(2):
                pt, pk = next_ps()
                for k in range(KD):
                    P.op("pe", lambda e, k=k, pt=pt, tb=tb, half=half: e.matmul(
                        pt[:, :], lhsT=hT[:, k, tb * 128:(tb + 1) * 128], rhs=wv[half][:, k, :],
                        start=(k == 0), stop=(k == KD - 1)),
                        reads=["wv%d_%d" % (half, j_) for j_ in range(4)] + [hk], writes=[pk], inc=(k == KD - 1))
                P.op("act", lambda e, pt=pt, half=half: e.activation(out=gav[:, half * 512:(half + 1) * 512], in_=pt[:, :],
                                                                     func=AF.Gelu), reads=[pk], writes=["gav"])
            P.op("act", lambda e: e.activation(out=junk, in_=gav, func=AF.Square, accum_out=ssq[:, 0:1]),
                 reads=["gav"], writes=["junk", "ssq"])
            P.op("act", lambda e: e.activation(out=ssq[:, 1:2], in_=ssq[:, 0:1], func=AF.Sqrt, scale=1.0 / DA, bias=eps_s[:, 0:1]),
                 reads=["ssq", "eps"], writes=["ssq"])
            P.op("dve", lambda e: e.reciprocal(out=ssq[:, 1:2], in_=ssq[:, 1:2]), reads=["ssq"], writes=["ssq"])
            P.op("dve", lambda e: e.scalar_tensor_tensor(out=zv, in0=gav, scalar=ssq[:, 1:2], in1=gsgu_s[:],
                                                         op0=ALU.mult, op1=ALU.mult),
                 reads=["gav", "ssq", "gsgu"], writes=["zv"])
            for h in range(8):
                P.op("pe", lambda e, h=h: e.matmul(ps_h[:, 0:128], lhsT=zv[:, h * 128:(h + 1) * 128], rhs=wsT_s[:, h, :],
                                                   start=True, stop=True), reads=["zv", "wsT"], writes=["ps7"])
                P.op("dve", lambda e, h=h: e.tensor_tensor(out=stmp, in0=ps_h[:, 0:128], in1=bsr_s[:, h, :], op=ALU.add),
                     reads=["ps7", "bsr"], writes=["stmp"])
                P.op("dve", lambda e, h=h, tb=tb: e.tensor_tensor(out=ya[:, h, tb * 128:(tb + 1) * 128], in0=stmp,
                                                                  in1=zu[:, h, tb * 128:(tb + 1) * 128], op=ALU.mult),
                     reads=["stmp", "zu"], writes=["ya"])

    def mB1a(m):
        P.dma("sp", lambda e: e.dma_start(out=yb, in_=ybs_v[:, :, m * TM:(m + 1) * TM]), reads=["ybs_%d" % c_ for c_ in range(8)] + ybz_keys, writes=["yb"])

    def mB1b(m):
        stats(ya, 8, TM, DA, ["ya"])
        for h in range(8):
            P.op("dve", lambda e, h=h: e.scalar_tensor_tensor(out=mT[:, h, :], in0=ya[:, h, :], scalar=ga_s[:, h:h + 1],
                                                              in1=rstd[:, 0:TM], op0=ALU.mult, op1=ALU.mult),
                 reads=["ya", "ga", "rstd"], writes=["mT"])
        stats(yb, 8, TM, DB, ["yb"])
        for h in range(8):
            P.op("dve", lambda e, h=h: e.scalar_tensor_tensor(out=mT[:, 8 + h, :], in0=yb[:, h, :], scalar=gb_s[:, h:h + 1],
                                                              in1=rstd[:, 0:TM], op0=ALU.mult, op1=ALU.mult),
                 reads=["yb", "gb", "rstd"], writes=["mT"])

    def mB2(m):
        xT, xk = xTB[m % 3], "xT%d" % (m % 3)
        for ob in range(KD):
            wt, wk = load_wblk(wblk_ap(w_out, ob * 128))
            pt, pk = next_ps()
            for k in range(KD):
                P.op("pe", lambda e, k=k, wt=wt, pt=pt: e.matmul(pt[:, 0:TM], lhsT=wt[:, k, :], rhs=mT[:, k, :],
                                                                 start=(k == 0), stop=(k == KD - 1)),
                     reads=[wk, "mT"], writes=[pk], inc=(k == KD - 1))
            P.op("dve", lambda e, ob=ob, pt=pt: e.tensor_tensor(out=xT[:, ob, :], in0=pt[:, 0:TM], in1=xT[:, ob, :], op=ALU.add),
                 reads=[pk, xk], writes=[xk])
        P.dma("sp", lambda e: e.dma_start(out=x1s_v[:, :, m * TM:(m + 1) * TM], in_=xT), reads=[xk], writes=["x1s"])

    load_wv()
    for h in range(NRES):
        P.dma("pool", lambda e, h=h: e.dma_start(out=wres[h], in_=wblk_ap(w_in, h * 128)), writes=["wres%d" % h])
    mF1a(0)
    mF1b(0)
    for it in range(nmt + 1):
        if it + 1 < nmt:
            mF1a(it + 1)
        if it < nmt:
            mB1a(it)
            mF2(it)
        if it - 1 >= 0:
            mB2(it - 1)
        if it + 1 < nmt:
            mF1b(it + 1)
        if it < nmt:
            mB1b(it)

    TH = TF + 2
    P.full_barrier()
    a32.reset()
    a16.reset()
    x1fB = [a32.get([128, KD, TH]) for _ in range(2)]
    gp = a32.get([128, TH])
    gc = a32.get([128, TF])
    ge = a32.get([128, TF])
    h2 = a16.get([128, KD, TH])
    act = a16.get([128, KF, TF])
    wg = [a16.get([128, KD, 128]) for i in range(2)]
    wu = [a16.get([128, KD, 128]) for i in range(2)]
    wd = [a16.get([128, KF, 128]) for i in range(2)]
    yT_keys = []

    def fPr(ft):
        x1f, xk = x1fB[ft % 2], "x1f%d" % (ft % 2)
        c0 = HALO + ft * TF - 1
        P.dma("sp", lambda e: e.dma_start(out=x1f, in_=x1s_v[:, :, c0:c0 + TH]), reads=["x1s"], writes=[xk])
        P.dma("sp", lambda e: e.dma_start(out=mask_s[:, 0:TH], in_=mask[:, c0:c0 + TH]), writes=["mask"])
        for cix in (0, TH - 1):
            P.op("dve", lambda e, cix=cix: e.tensor_scalar(out=x1f[:, :, cix:cix + 1], in0=x1f[:, :, cix:cix + 1],
                                                           scalar1=mask_s[:, cix:cix + 1], scalar2=None, op0=ALU.mult),
                 reads=[xk, "mask"], writes=[xk])
        stats(x1f, KD, TH, D, [xk], pieces=[(0, 512), (512, TH)])
        for k in range(KD):
            P.op("dve", lambda e, k=k: e.scalar_tensor_tensor(out=h2[:, k, :], in0=x1f[:, k, :], scalar=g2_s[:, k:k + 1],
                                                              in1=rstd[:, 0:TH], op0=ALU.mult, op1=ALU.mult),
                 reads=[xk, "g2", "rstd"], writes=["h2"])

    def fU(ft, fbs):
        for fb in fbs:
            i = fb % 2
            P.dma("pool", lambda e, i=i, fb=fb: e.dma_start(out=wg[i][:], in_=wblk_ap(w_up, fb * 128)),
                  writes=["wg%d" % i])
            P.dma("pool", lambda e, i=i, fb=fb: e.dma_start(out=wu[i][:], in_=wblk_ap(w_up, DFF + fb * 128)),
                  writes=["wu%d" % i])
            pg, pgk = next_ps()
            pv, pvk = next_ps()
            for k in range(KD):
                P.op("pe", lambda e, k=k, i=i, pg=pg: e.matmul(pg[:, 0:TF], lhsT=wg[i][:, k, :], rhs=h2[:, k, 1:TF + 1],
                                                               start=(k == 0), stop=(k == KD - 1)),
                     reads=["wg%d" % i, "h2"], writes=[pgk], inc=(k == KD - 1))
            for k in range(KD):
                P.op("pe", lambda e, k=k, i=i: e.matmul(ps_h[:, 0:2], lhsT=wg[i][:, k, :], rhs=h2[:, k, 0:TH:TF + 1],
                                                        start=(k == 0), stop=(k == KD - 1)),
                     reads=["wg%d" % i, "h2"], writes=["ps7"], inc=(k == KD - 1))
            for k in range(KD):
                P.op("pe", lambda e, k=k, i=i, pv=pv: e.matmul(pv[:, 0:TF], lhsT=wu[i][:, k, :], rhs=h2[:, k, 1:TF + 1],
                                                               start=(k == 0), stop=(k == KD - 1)),
                     reads=["wu%d" % i, "h2"], writes=[pvk], inc=(k == KD - 1))
            P.op("act", lambda e, pg=pg: e.copy(out=gp[:, 1:TF + 1], in_=pg[:, 0:TF]), reads=[pgk], writes=["gp"])
            P.op("act", lambda e: e.copy(out=gp[:, 0:TH:TF + 1], in_=ps_h[:, 0:2]), reads=["ps7"], writes=["gp"])
            P.op("dve", lambda e, fb=fb: e.tensor_scalar(out=gc, in0=gp[:, 1:TF + 1], scalar1=dww_s[:, fb, 1:2],
                                                         scalar2=dwb_s[:, fb:fb + 1], op0=ALU.mult, op1=ALU.add),
                 reads=["gp", "dww", "dwb"], writes=["gc"])
            P.op("dve", lambda e, fb=fb: e.scalar_tensor_tensor(out=gc, in0=gp[:, 0:TF], scalar=dww_s[:, fb, 0:1],
                                                                in1=gc, op0=ALU.mult, op1=ALU.add),
                 reads=["gp", "dww", "gc"], writes=["gc"])
            P.op("dve", lambda e, fb=fb: e.scalar_tensor_tensor(out=gc, in0=gp[:, 2:TF + 2], scalar=dww_s[:, fb, 2:3],
                                                                in1=gc, op0=ALU.mult, op1=ALU.add),
                 reads=["gp", "dww", "gc"], writes=["gc"])
            P.op("act", lambda e: e.activation(out=ge, in_=gc, func=AF.Gelu), reads=["gc"], writes=["ge"])
            P.op("dve", lambda e, fb=fb, pv=pv: e.tensor_tensor(out=act[:, fb, :], in0=pv[:, 0:TF], in1=ge, op=ALU.mult),
                 reads=[pvk, "ge"], writes=["act%d" % fb])

    def fD(ft, obs):
        x1f, xk = x1fB[ft % 2], "x1f%d" % (ft % 2)
        for ob in obs:
            i = ob % 2
            P.dma("pool", lambda e, i=i, ob=ob: e.dma_start(out=wd[i][:, 0:kf, :], in_=w_down[ob][:, 0:kf, :]),
                  writes=["wd%d" % i])
            pt, pk = next_ps()
            for k in range(kf):
                P.op("pe", lambda e, k=k, i=i, pt=pt: e.matmul(pt[:, 0:TF], lhsT=wd[i][:, k, :], rhs=act[:, k, :],
                                                               start=(k == 0), stop=(k == kf - 1)),
                     reads=["wd%d" % i, "act%d" % k], writes=[pk], inc=(k == kf - 1))
            P.op("dve", lambda e, ob=ob, pt=pt: e.tensor_tensor(out=x1f[:, ob, 1:TF + 1], in0=pt[:, 0:TF], in1=x1f[:, ob, 1:TF + 1],
                                                                op=ALU.add), reads=[pk, xk], writes=[xk])

    def fN(ft):
        x1f, xk = x1fB[ft % 2], "x1f%d" % (ft % 2)
        stats(x1f[:, :, 1:TF + 1], KD, TF, D, [xk])
        for k in range(KD):
            P.op("dve", lambda e, k=k: e.scalar_tensor_tensor(out=x1f[:, k, 1:TF + 1], in0=x1f[:, k, 1:TF + 1], scalar=gf_s[:, k:k + 1],
                                                              in1=rstd[:, 0:TF], op0=ALU.mult, op1=ALU.mult),
                 reads=[xk, "gf", "rstd"], writes=[xk])
        yT_keys.append("yT%d" % ft)
        P.dma("sp", lambda e: e.dma_start(out=yT_v[:, :, ft * TF:(ft + 1) * TF], in_=x1f[:, :, 1:TF + 1]), reads=[xk], writes=["yT%d" % ft])

    NU0 = min(4, kf)
    fPr(0)
    fU(0, range(0, NU0))
    for ft in range(nft):
        fU(ft, range(NU0, kf))
        fD(ft, range(0, 4))
        if ft + 1 < nft:
            fPr(ft + 1)
        fD(ft, range(4, KD))
        if ft + 1 < nft:
            fU(ft + 1, range(0, NU0))
        fN(ft)

    P.barrier_wait("sp", yT_keys)
    P.emit()
    P.close()
    return nc


def _tile_w(w):
    w = np.asarray(w, np.float32)
    K, C = w.shape
    return np.ascontiguousarray(w.reshape(K // 128, 128, C // 128, 128).transpose(2, 1, 0, 3))


def _pk(v, nk):
    return np.ascontiguousarray(np.asarray(v, np.float32).reshape(nk, 128).T)


def prep_inputs(inputs):
    x = np.asarray(inputs["x"], np.float32)
    shared = {
        "w_in": _tile_w(inputs["w_in"][0]),
        "w_out": _tile_w(inputs["w_out"][0]),
        "w_up": _tile_w(inputs["ffn_w_up"][0]),
        "w_down": _tile_w(inputs["ffn_w_down"][0]),
        "g1": _pk(inputs["norm1_g"][0], KD),
        "g2": _pk(inputs["norm2_g"][0], KD),
        "gf": _pk(inputs["final_g"], KD),
        "ga": _pk(inputs["outnorm_a_g"][0], 8),
        "gb": _pk(inputs["outnorm_b_g"][0], 8),
        "gsgu": np.ascontiguousarray(np.broadcast_to(np.asarray(inputs["sgu_norm_g"][0], np.float32)[None, :], (128, DA))),
        "bsr": np.ascontiguousarray(np.broadcast_to(np.asarray(inputs["sgu_b"][0], np.float32).reshape(1, 8 * 128), (128, 8 * 128))),
        "wsT": np.ascontiguousarray(np.transpose(np.asarray(inputs["sgu_w"][0], np.float32), (2, 0, 1)).reshape(128, 8 * 128)),
        "dww": np.ascontiguousarray(np.transpose(np.asarray(inputs["ffn_dw_w"][0], np.float32).reshape(3, KF, 128), (2, 1, 0)).reshape(128, KF * 3)),
        "dwb": _pk(inputs["ffn_dw_b"][0], KF),
    }
    t_lin = np.linspace(0.0, 1.0, L, dtype=np.float32)
    wv_ = (np.float32(2.0 * math.pi / L) * np.arange(L, dtype=np.float32))[:, None]
    fb_ = np.linspace(1e-4, 15, 16, dtype=np.float32)[None]
    zfull = np.concatenate([t_lin[:, None], np.cos(fb_ * wv_), -np.sin(fb_ * wv_)], axis=-1).astype(np.float32)
    max_decay = math.log(1e-2) / 0.3
    min_decay = math.log(1e-2) / 1.5
    deltas = np.abs(np.linspace(min_decay, max_decay, DB, dtype=np.float32))
    shared.update({
        "negd": _pk(-deltas, 8),
        "hw1": np.ascontiguousarray(inputs["hy_f_w1"][0], dtype=np.float32),
        "hw2": np.ascontiguousarray(inputs["hy_f_w2"][0], dtype=np.float32),
        "hw3": np.ascontiguousarray(inputs["hy_f_w3"][0], dtype=np.float32),
        "hb": np.ascontiguousarray(np.stack([inputs["hy_f_b1"][0], inputs["hy_f_b2"][0], inputs["hy_f_b3"][0]], axis=1), dtype=np.float32),
        "hfr": np.ascontiguousarray(np.asarray(inputs["hy_f_freq"][0], np.float32).reshape(64, 1)),
        "hwo2": np.ascontiguousarray(np.concatenate([inputs["hy_f_wout"][0][:, :DB], inputs["hy_f_wout"][0][:, DB:]], axis=0), dtype=np.float32),
        "hw3d": np.ascontiguousarray(np.concatenate([inputs["hy_f_w3"][0], inputs["hy_f_w3"][0]], axis=1), dtype=np.float32),
        "hfr2": np.ascontiguousarray(np.concatenate([inputs["hy_f_freq"][0], inputs["hy_f_freq"][0]]).reshape(128, 1), dtype=np.float32),
        "hb32": np.ascontiguousarray(np.concatenate([inputs["hy_f_b3"][0], inputs["hy_f_b3"][0]]).reshape(128, 1), dtype=np.float32),
        "cw": np.ascontiguousarray(np.transpose(np.asarray(inputs["hy_conv_w"][0], np.float32).reshape(3, 24, 128), (2, 1, 0)).reshape(128, 72)),
        "cb": _pk(inputs["hy_conv_b"][0], 24),
        "dsk": _pk(inputs["hy_d_skip"][0], 8),
    })
    NF = 2 * L
    ar = np.arange(128, dtype=np.float64)
    kA = np.arange(64, dtype=np.float64) + 0.5
    th1 = 2 * np.pi * np.outer(ar, kA) / 128.0
    thT = 2 * np.pi * np.outer(ar, kA) / NF
    th2 = 2 * np.pi * np.outer(ar, ar) / 128.0
    hc = np.zeros((64, 32)); hs = np.zeros((64, 32))
    hc[:, :17] = (2.0 / NF) * np.cos(th1[:17].T)
    hs[:, :17] = -(2.0 / NF) * np.sin(th1[:17].T)
    f32c = lambda a_: np.ascontiguousarray(a_, dtype=np.float32)
    shared.update({
        "fF1": f32c(np.concatenate([np.cos(th1), -np.sin(th1)], axis=1)),
        "fT1r": f32c(np.tile(np.cos(thT), (1, 8))), "fT1i": f32c(np.tile(-np.sin(thT), (1, 8))),
        "fF2r": f32c(np.cos(th2)), "fF2i": f32c(-np.sin(th2)), "fF2n": f32c(np.sin(th2)),
        "fG1": f32c(np.concatenate([np.cos(th2), np.sin(th2)], axis=1)),
        "fG2": f32c(np.concatenate([-np.sin(th2), np.cos(th2)], axis=1)),
        "fT2r": f32c(np.tile(np.concatenate([np.cos(thT.T), np.cos(thT.T)], axis=0), (1, 4))),
        "fT2i": f32c(np.tile(np.concatenate([np.sin(thT.T), np.sin(thT.T)], axis=0), (1, 4))),
        "fHc": f32c(np.concatenate([hc, hc], axis=0)), "fHs": f32c(np.concatenate([hs, hs], axis=0)),
    })
    xpad = [np.concatenate([np.zeros((1, D), np.float32), x[b], np.zeros((1, D), np.float32)], axis=0) for b in range(2)]

    def xTb_for(b, q):
        tiles_ = []
        for i in range(4):
            qi = (q + i) % 4
            tiles_.append(xpad[b][TOK * qi:TOK * qi + TOK + 2])
        return np.ascontiguousarray(np.concatenate(tiles_, axis=0).T)

    F1full = np.concatenate([np.cos(th1), -np.sin(th1)], axis=1)
    rot = {}
    for q in range(4):
        s0 = TOK * q - 1
        n = (s0 + np.arange(2 * L)) % (2 * L)
        fwd = n < L
        bwd = n > L
        pos = np.where(fwd, n, np.where(bwd, 2 * L - n, 0))
        sgn = np.where(np.arange(2 * L) < L, 1.0, -1.0).astype(np.float32)
        rot[q] = {
            "zq": np.ascontiguousarray(zfull[pos].T),
            "ttq": np.ascontiguousarray(np.broadcast_to(t_lin[pos][None, :], (128, 2 * L))),
            "m2q": np.ascontiguousarray(np.concatenate([np.broadcast_to((fwd.astype(np.float32) * sgn)[None, :], (64, 2 * L)),
                                                        np.broadcast_to((bwd.astype(np.float32) * sgn)[None, :], (64, 2 * L))], axis=0)),
        }
    in_maps = []
    for c in range(NCORES):
        b, q = divmod(c, 4)
        lo = TOK * q - HALO
        xe = np.zeros((TE, D), np.float32)
        m = np.zeros((TE,), np.float32)
        s0 = max(lo, 0)
        s1 = min(lo + TE, L)
        xe[s0 - lo:s1 - lo] = x[b, s0:s1]
        m[s0 - lo:s1 - lo] = 1.0
        d = dict(shared)
        d.update(rot[q])
        d["xTb"] = xTb_for(b, q)
        a_s = (np.arange(64) + 16 * q) % 64
        d["fF1d"] = np.ascontiguousarray(F1full[a_s], dtype=np.float32)
        d["xTm"] = np.ascontiguousarray(xe.T)
        d["mask"] = np.ascontiguousarray(np.broadcast_to(m[None, :], (128, TE)))
        in_maps.append(d)
    return in_maps


def kernel(**inputs):
    in_maps = prep_inputs(inputs)
    nc = build()
    res = run_bass_kernel_spmd(nc, in_maps, core_ids=list(range(NCORES)))
    out = np.empty((2, L, D), np.float32)
    for c in range(NCORES):
        b, q = divmod(c, 4)
        out[b, TOK * q:TOK * (q + 1), :] = res.results[c]["yT"].T
    return out
```

```python
import contextlib
import math
import numpy as np
import concourse.bass as bass
import concourse.mybir as mybir
from concourse.bass_utils import run_bass_kernel_spmd

F32 = mybir.dt.float32
BF16 = mybir.dt.bfloat16
I32 = mybir.dt.int32
ALU = mybir.AluOpType
AF = mybir.ActivationFunctionType

NCORES = 8
D = 2048
L = 8192
DA = 1024
DB = 1024
DFF = 5504
DIN = 5120
KD = D // 128
KF = DFF // 128
TOK = 2048
HALO = 128
TE = TOK + 2 * HALO
TM = 256
NMT = TE // TM
TF = 512
NFT = TOK // TF
EPS = 1e-6

ENGS = ("pe", "act", "dve", "pool", "sp")
N_DMA_SEMS = 40


class Prog:
    def __init__(self, nc):
        self.nc = nc
        self.stack = contextlib.ExitStack()
        self.ops = {e: [] for e in ENGS}
        self.cnt = {e: 0 for e in ENGS}
        self.res = {}
        self.known = {e: {} for e in ENGS}
        self.sems = {}
        for e in ENGS:
            self.sems[e] = self.stack.enter_context(nc.semaphore("s_" + e))
        self.dma_cum = {}
        self.dma_rr = {}
        for q in ("sp", "act", "pool"):
            n = N_DMA_SEMS if q != "act" else 4
            self.dma_rr[q] = [0, n]
            for k in range(n):
                self.dma_cum[(q, k)] = 0
                self.sems[("d", q, k)] = self.stack.enter_context(nc.semaphore("s_d%s%d" % (q, k)))

    def sb(self, name, shape, dt):
        return self.stack.enter_context(self.nc.sbuf_tensor(name, list(shape), dt))

    def ps(self, name, shape, dt=F32):
        return self.stack.enter_context(self.nc.psum_tensor(name, list(shape), dt))

    def _deps(self, eng, reads, writes):
        evs = []
        for k in reads:
            r = self.res.get(k)
            if r and r["w"] is not None:
                evs.append(r["w"])
        for k in writes:
            r = self.res.get(k)
            if r:
                if r["w"] is not None:
                    evs.append(r["w"])
                evs.extend(r["r"].items())
        waits = {}
        for (s, v) in evs:
            if eng == "pe" and s == "pe":
                continue
            if self.known[eng].get(s, 0) >= v:
                continue
            if waits.get(s, 0) < v:
                waits[s] = v
        for s, v in waits.items():
            self.known[eng][s] = v
        return list(waits.items())

    def _record(self, ev, reads, writes):
        for k in reads:
            r = self.res.setdefault(k, {"w": None, "r": {}})
            if r["r"].get(ev[0], 0) < ev[1]:
                r["r"][ev[0]] = ev[1]
        for k in writes:
            self.res[k] = {"w": ev, "r": {}}

    def op(self, eng, fn, reads=(), writes=(), inc=True):
        waits = self._deps(eng, reads, writes)
        if inc:
            self.cnt[eng] += 1
            ev = (eng, self.cnt[eng])
        else:
            ev = (eng, self.cnt[eng] + 1)
        self.ops[eng].append(("c", fn, waits, inc))
        self._record(ev, reads, writes)

    def dma(self, q, fn, reads=(), writes=(), inc=16):
        k = self.dma_rr[q][0]
        self.dma_rr[q][0] = (k + 1) % self.dma_rr[q][1]
        s = ("d", q, k)
        waits = dict(self._deps(q, reads, writes))
        cum = self.dma_cum[(q, k)]
        if cum > 0 and self.known[q].get(s, 0) < cum:
            waits[s] = cum
            self.known[q][s] = cum
        self.dma_cum[(q, k)] = cum + inc
        ev = (s, cum + inc)
        self.ops[q].append(("d", fn, list(waits.items()), (s, inc)))
        self._record(ev, reads, writes)

    def full_barrier(self):
        evs = [(e, self.cnt[e]) for e in ENGS if self.cnt[e] > 0]
        evs += [(("d", q_, k), v) for (q_, k), v in self.dma_cum.items() if v > 0]
        for e in ENGS:
            waits = []
            for s_, v in evs:
                if self.known[e].get(s_, 0) < v:
                    waits.append((s_, v))
                    self.known[e][s_] = v
            self.ops[e].append(("w", None, waits, False))
        self.res = {}

    def barrier_wait(self, eng, keys):
        waits = self._deps(eng, list(keys), list(keys))
        self.ops[eng].append(("w", None, waits, False))

    def emit(self):
        nc = self.nc
        handles = {"pe": "tensor", "act": "scalar", "dve": "vector", "pool": "gpsimd", "sp": "sync"}
        with nc.Block() as block:
            for e in ENGS:
                ops = self.ops[e]
                if not ops:
                    continue

                def body(eng, ops=ops, e=e):
                    for kind, fn, waits, inc in ops:
                        for s, v in waits:
                            eng.wait_ge(self.sems[s], v)
                        if kind == "w":
                            continue
                        ins = fn(eng)
                        if kind == "c":
                            if inc:
                                ins.then_inc(self.sems[e], 1)
                        else:
                            s, n = inc
                            ins.then_inc(self.sems[s], n)

                getattr(block, handles[e])(body)

    def close(self):
        self.stack.close()


class Arena:
    def __init__(self, t, n):
        self.t, self.n, self.off = t, n, 0

    def reset(self):
        self.off = 0

    def get(self, shape):
        n = int(np.prod(shape[1:]))
        assert self.off + n <= self.n, (self.off, n, self.n)
        ap = self.t[:, self.off:self.off + n]
        self.off += n
        if len(shape) == 3:
            ap = ap.rearrange("p (a b) -> p a b", a=shape[1])
        return ap


def build(nmt=NMT, nft=NFT, kf=KF, debug=False):
    nc = bass.Bass("TRN2", target_bir_lowering=False)

    def din(name, shape, dt=F32):
        return nc.dram_tensor(name, list(shape), dt, kind="ExternalInput").ap()

    xTm = din("xTm", [D, TE])
    mask = din("mask", [128, TE])
    w_in = din("w_in", [DIN // 128, 128, KD, 128])
    w_out = din("w_out", [D // 128, 128, KD, 128])
    w_up = din("w_up", [2 * DFF // 128, 128, KD, 128])
    w_down = din("w_down", [D // 128, 128, KF, 128])
    g1 = din("g1", [128, KD])
    g2 = din("g2", [128, KD])
    gf = din("gf", [128, KD])
    ga = din("ga", [128, 8])
    gb = din("gb", [128, 8])
    gsgu = din("gsgu", [128, DA])
    bsr = din("bsr", [128, 8 * 128])
    wsT = din("wsT", [128, 8 * 128])
    dww = din("dww", [128, KF * 3])
    dwb = din("dwb", [128, KF])
    xTb = din("xTb", [D, (L // TOK) * (TOK + 2)])
    zq = din("zq", [33, 2 * L])
    ttq = din("ttq", [128, 2 * L])
    m2q = din("m2q", [128, 2 * L])
    hw3d = din("hw3d", [64, 128])
    hfr2 = din("hfr2", [128, 1])
    hb32 = din("hb32", [128, 1])
    hwo2 = din("hwo2", [128, DB])
    negd = din("negd", [128, 8])
    hw1 = din("hw1", [33, 64])
    hw2 = din("hw2", [64, 64])
    hw3 = din("hw3", [64, 64])
    hb = din("hb", [64, 3])
    hfr = din("hfr", [64, 1])
    cw = din("cw", [128, 24 * 3])
    cb = din("cb", [128, 24])
    dsk = din("dsk", [128, 8])
    yT = nc.dram_tensor("yT", [D, TOK], F32, kind="ExternalOutput").ap()
    ud = nc.dram_tensor("ud", [DB, L], F32, kind="Internal").ap()
    uo = nc.dram_tensor("uo", [DB, TE], F32, kind="Internal").ap()
    b0 = nc.dram_tensor("b0", [DB, TE], F32, kind="Internal").ap()
    kd = nc.dram_tensor("kd", [DB, 2 * L], F32, kind="Internal").ap()
    ycd = nc.dram_tensor("ycd", [DB, 17 * 128], F32, kind="Internal").ap()
    fT1r = din("fT1r", [128, 512]); fT1i = din("fT1i", [128, 512])
    fT2r = din("fT2r", [128, 512]); fT2i = din("fT2i", [128, 512])
    fF1 = din("fF1", [128, 128])
    fF1d = din("fF1d", [64, 128])
    fF2r = din("fF2r", [128, 128]); fF2i = din("fF2i", [128, 128]); fF2n = din("fF2n", [128, 128])
    fG1 = din("fG1", [128, 256]); fG2 = din("fG2", [128, 256])
    fHc = din("fHc", [128, 32]); fHs = din("fHs", [128, 32])
    x1s = nc.dram_tensor("x1s", [D, TE], F32, kind="Internal").ap()
    ybs = nc.dram_tensor("ybs", [DB, TE], F32, kind="Internal").ap()

    P = Prog(nc)
    ones = P.sb("ones", [128, 128], F32)
    zero = P.sb("zero", [128, TM], F32)
    g1_s = P.sb("g1_s", [128, KD], F32)
    g2_s = P.sb("g2_s", [128, KD], F32)
    gf_s = P.sb("gf_s", [128, KD], F32)
    ga_s = P.sb("ga_s", [128, 8], F32)
    gb_s = P.sb("gb_s", [128, 8], F32)
    gsgu_s = P.sb("gsgu_s", [128, DA], F32)
    bsr_s = P.sb("bsr_s", [128, 8, 128], F32)
    wsT_s = P.sb("wsT_s", [128, 8, 128], BF16)
    dww_s = P.sb("dww_s", [128, KF, 3], F32)
    dwb_s = P.sb("dwb_s", [128, KF], F32)
    mask_s = P.sb("mask_s", [128, TF + 2], F32)
    NA32, NA16 = 19800, 52200
    a32 = Arena(P.sb("arena32", [128, NA32], F32), NA32)
    a16 = Arena(P.sb("arena16", [128, NA16], BF16), NA16)

    eps_s = P.sb("eps_s", [128, 1], F32)
    P.op("dve", lambda e: e.memset(eps_s[:], EPS), writes=["eps"])
    P.op("dve", lambda e: e.memset(ones[:], 1.0), writes=["ones"])
    P.op("dve", lambda e: e.memset(zero[:], 0.0), writes=["zero"])
    for (t, src, nm) in ((g1_s, g1, "g1"), (g2_s, g2, "g2"), (gf_s, gf, "gf"), (ga_s, ga, "ga"),
                         (gb_s, gb, "gb"), (gsgu_s, gsgu, "gsgu"), (dwb_s, dwb, "dwb")):
        P.dma("sp", lambda e, t=t, src=src: e.dma_start(out=t[:], in_=src), writes=[nm])
    P.dma("sp", lambda e: e.dma_start(out=bsr_s[:].rearrange("p h i -> p (h i)"), in_=bsr), writes=["bsr"])
    P.dma("sp", lambda e: e.dma_start(out=dww_s[:].rearrange("p k c -> p (k c)"), in_=dww), writes=["dww"])
    P.dma("pool", lambda e: e.dma_start(out=wsT_s[:].rearrange("p h i -> p (h i)"), in_=wsT), writes=["wsT"])

    NPS = 6
    pdb = [P.ps("pd%d" % i, [128, 1024]) for i in range(4)]
    psb = [pdb[i // 2][:, (i % 2) * 512:(i % 2 + 1) * 512] for i in range(8)]
    ps_st, ps_h = psb[6], psb[7]
    bank_n = [NPS]
    bank_p = [0]

    def next_ps():
        i = bank_p[0] % bank_n[0]
        bank_p[0] = i + 1
        return psb[i], "ps%d" % i

    def next_pd():
        i = bank_p[0] % bank_n[0]
        if i % 2:
            i = (i + 1) % bank_n[0]
        bank_p[0] = i + 2
        return pdb[i // 2][:, :], ["ps%d" % i, "ps%d" % (i + 1)]

    rstd = P.sb("rstd", [128, 520], F32)

    sqb = [P.sb("sqb%d" % i, [128, 520], F32) for i in range(2)]
    sq_rr = [0]

    def stats(src, nk, n, dim, srckeys, outkey="rstd", pieces=None, dst=None):
        pieces = pieces or [(0, n)]
        R = rstd if dst is None else dst
        for (a, b) in pieces:
            for k in range(nk):
                i = sq_rr[0]
                sq_rr[0] = 1 - i
                P.op("act", lambda e, i=i, k=k, a=a, b=b: e.activation(out=sqb[i][:, 0:b - a], in_=src[:, k, a:b], func=AF.Square),
                     reads=srckeys, writes=["sqb%d" % i])
                P.op("pe", lambda e, i=i, k=k, a=a, b=b: e.matmul(ps_st[:, 0:b - a], lhsT=ones[:], rhs=sqb[i][:, 0:b - a],
                                                                  start=(k == 0), stop=(k == nk - 1)),
                     reads=["sqb%d" % i, "ones"], writes=["ps6"])
            P.op("act", lambda e, a=a, b=b: e.activation(out=R[:, a:b], in_=ps_st[:, 0:b - a], func=AF.Sqrt,
                                                         scale=1.0 / dim, bias=eps_s[:, 0:1]),
                 reads=["ps6", "eps"], writes=[outkey])
        P.op("dve", lambda e: e.reciprocal(out=R[:, 0:n], in_=R[:, 0:n]), reads=[outkey], writes=[outkey])

    def wblk_ap(w, col):
        assert col % 128 == 0
        return w[col // 128]
    xTm_v = xTm.rearrange("(k p) t -> p k t", p=128)
    x1s_v = x1s.rearrange("(k p) t -> p k t", p=128)
    ybs_v = ybs.rearrange("(k p) t -> p k t", p=128)
    yT_v = yT.rearrange("(k p) t -> p k t", p=128)
    ybz_keys = []
    for c in range(DB // 128):
        for (a_, b_) in ((0, HALO - 1), (HALO + TOK + 1, TE)):
            key = "ybz_%d_%d" % (c, a_)
            ybz_keys.append(key)
            P.dma("sp", lambda e, c=c, a_=a_, b_=b_: e.dma_start(out=ybs[c * 128:(c + 1) * 128, a_:b_], in_=zero[:, 0:b_ - a_]),
                  reads=["zero"], writes=[key])

    NJ = TOK + 2
    E0 = HALO - 1
    TWO_PI = 2.0 * math.pi
    MAGIC = 12582912.0
    hw1_s = P.sb("hw1_s", [33, 64], F32)
    hw2_s = P.sb("hw2_s", [64, 64], F32)
    hw3_s = P.sb("hw3_s", [64, 64], F32)
    hb_s = P.sb("hb_s", [64, 3], F32)
    hfr_s = P.sb("hfr_s", [64, 1], F32)
    hfb_s = P.sb("hfb_s", [64, 3], F32)
    hwo_s = P.sb("hwo_s", [128, DB], BF16)
    hw3d_s = P.sb("hw3d_s", [64, 128], F32)
    hfr2_s = P.sb("hfr2_s", [128, 1], F32)
    hfb2_s = P.sb("hfb2_s", [128, 1], F32)
    cw_s = P.sb("cw_s", [128, 24, 3], F32)
    cb_s = P.sb("cb_s", [128, 24], F32)
    dsk_s = P.sb("dsk_s", [128, 8], F32)
    negd_s = P.sb("negd_s", [128, 8], F32)
    l1c = P.sb("l1c", [128, 40], F32)
    for (t, src, nm) in ((hw1_s, hw1, "hw1"), (hw2_s, hw2, "hw2"), (hw3_s, hw3, "hw3"), (hb_s, hb, "hb"), (hfr_s, hfr, "hfr"),
                         (cb_s, cb, "cb"), (dsk_s, dsk, "dsk"), (negd_s, negd, "negd")):
        P.dma("sp", lambda e, t=t, src=src: e.dma_start(out=t[:], in_=src), writes=[nm])
    P.dma("sp", lambda e: e.dma_start(out=cw_s[:].rearrange("p k c -> p (k c)"), in_=cw), writes=["cw"])
    P.dma("pool", lambda e: e.dma_start(out=hwo_s[:], in_=hwo2), writes=["hwo"])
    P.dma("sp", lambda e: e.dma_start(out=hw3d_s[:], in_=hw3d), writes=["hw3d"])
    P.dma("sp", lambda e: e.dma_start(out=hfr2_s[:], in_=hfr2), writes=["hfr2"])
    P.dma("sp", lambda e: e.dma_start(out=hfb2_s[:], in_=hb32), writes=["hfb2"])
    P.op("dve", lambda e: e.tensor_scalar(out=hfb2_s[:], in0=hfb2_s[:], scalar1=hfr2_s[:, 0:1], scalar2=None, op0=ALU.mult),
         reads=["hfb2", "hfr2"], writes=["hfb2"])
    P.op("dve", lambda e: e.tensor_scalar(out=hfb_s[:], in0=hb_s[:], scalar1=hfr_s[:, 0:1], scalar2=None, op0=ALU.mult),
         reads=["hb", "hfr"], writes=["hfb"])

    a16f = Arena(a16.t[:, :].bitcast(F32), NA16 // 2)

    def conv3(o, xin, kidx, W, keys_in, key_out):
        P.op("dve", lambda e: e.tensor_scalar(out=o[:, 0:W], in0=xin[:, 0:W], scalar1=cw_s[:, kidx, 1:2], scalar2=cb_s[:, kidx:kidx + 1],
                                              op0=ALU.mult, op1=ALU.add), reads=keys_in + ["cw", "cb"], writes=[key_out])
        P.op("dve", lambda e: e.scalar_tensor_tensor(out=o[:, 1:W], in0=xin[:, 0:W - 1], scalar=cw_s[:, kidx, 0:1], in1=o[:, 1:W],
                                                     op0=ALU.mult, op1=ALU.add), reads=keys_in + ["cw", key_out], writes=[key_out])
        P.op("dve", lambda e: e.scalar_tensor_tensor(out=o[:, 0:W - 1], in0=xin[:, 1:W], scalar=cw_s[:, kidx, 2:3], in1=o[:, 0:W - 1],
                                                     op0=ALU.mult, op1=ALU.add), reads=keys_in + ["cw", key_out], writes=[key_out])

    xTb_v = xTb.rearrange("(k p) t -> p k t", p=128)
    P.full_barrier()
    a32.reset(); a16.reset()
    WS = 2304
    xch = [a32.get([128, KD, 256]) for _ in range(2)]
    prb = [a32.get([128, WS]) for _ in range(2)]
    cvb = [a32.get([128, WS]) for _ in range(2)]
    hTs = a16.get([128, KD, WS])
    rstdS = a32.get([128, WS])
    wbl = [a16.get([128, KD, 128]) for _ in range(4)]
    w_rr = [0]
    x_rr = [0]

    def build_hT(src_v, col0, ncols):
        c = 0
        while c < ncols:
            w = min(256, ncols - c)
            i = x_rr[0]; x_rr[0] = 1 - i
            xt, xk = xch[i], "xch%d" % i
            P.dma("sp", lambda e, xt=xt, c=c, w=w: e.dma_start(out=xt[:, :, 0:w], in_=src_v[:, :, col0 + c:col0 + c + w]), writes=[xk])
            for k in range(KD):
                P.op("dve", lambda e, k=k, xt=xt, c=c, w=w: e.tensor_scalar(
                    out=hTs[:, k, c:c + w], in0=xt[:, k, 0:w], scalar1=g1_s[:, k:k + 1], scalar2=None, op0=ALU.mult),
                    reads=[xk, "g1"], writes=["hTs"])
            stats(xt, KD, w, D, [xk], outkey="rstdS_%d" % c, dst=rstdS[:, c:c + w])
            c += w

    def apply_rstd(buf, key, ncols):
        ks = ["rstdS_%d" % c_ for c_ in range(0, ncols, 256)]
        P.op("dve", lambda e: e.tensor_tensor(out=buf[:, 0:ncols], in0=buf[:, 0:ncols], in1=rstdS[:, 0:ncols], op=ALU.mult),
             reads=[key] + ks, writes=[key])

    def proj_block(ncols, wcol, dst, dkey):
        i = w_rr[0]; w_rr[0] = (i + 1) % 4
        P.dma("pool", lambda e: e.dma_start(out=wbl[i], in_=wblk_ap(w_in, wcol)), writes=["wbl%d" % i])
        c = 0
        while c < ncols:
            w = min(512, ncols - c)
            pt, pk = next_ps()
            for k in range(KD):
                P.op("pe", lambda e, k=k, pt=pt, c=c, w=w: e.matmul(pt[:, 0:w], lhsT=wbl[i][:, k, :], rhs=hTs[:, k, c:c + w],
                                                                    start=(k == 0), stop=(k == KD - 1)),
                     reads=["wbl%d" % i, "hTs"], writes=[pk], inc=(k == KD - 1))
            P.op("act", lambda e, pt=pt, c=c, w=w: e.copy(out=dst[:, c:c + w], in_=pt[:, 0:w]), reads=[pk], writes=[dkey])
            c += w

    for st in range(L // TOK):
        build_hT(xTb_v, st * (TOK + 2), TOK + 2)
        for cc in range(8):
            proj_block(TOK + 2, 3 * DB + cc * 128, prb[0], "prb0")
            proj_block(TOK + 2, 4 * DB + cc * 128, prb[1], "prb1")
            apply_rstd(prb[0], "prb0", TOK + 2)
            apply_rstd(prb[1], "prb1", TOK + 2)
            conv3(cvb[0], prb[0], 8 + cc, TOK + 2, ["prb0"], "cvb0")
            conv3(cvb[1], prb[1], 16 + cc, TOK + 2, ["prb1"], "cvb1")
            P.op("dve", lambda e: e.tensor_tensor(out=cvb[0][:, 1:TOK + 1], in0=cvb[0][:, 1:TOK + 1], in1=cvb[1][:, 1:TOK + 1], op=ALU.mult),
                 reads=["cvb0", "cvb1"], writes=["cvb0"])
            P.dma("sp", lambda e, cc=cc, st=st: e.dma_start(out=ud[cc * 128:(cc + 1) * 128, st * TOK:(st + 1) * TOK],
                                                            in_=cvb[0][:, 1:TOK + 1]), reads=["cvb0"], writes=["ud_%d_%d" % (cc, st)])
    build_hT(xTm_v, 0, TE)
    for cc in range(8):
        proj_block(TE, 2 * DB + cc * 128, prb[0], "prb0")
        apply_rstd(prb[0], "prb0", TE)
        conv3(cvb[0], prb[0], cc, TE, ["prb0"], "cvb0")
        P.dma("sp", lambda e, cc=cc: e.dma_start(out=b0[cc * 128:(cc + 1) * 128, :], in_=cvb[0][:, 0:TE]), reads=["cvb0"], writes=["b0"])
    P.full_barrier()

    NFFT = 2 * L
    a32.reset(); a16.reset()
    t512 = [a32.get([128, 512]) for _ in range(3)]
    kch = [a32.get([128, 512]) for _ in range(2)]
    stg2 = [a32.get([128, 512]) for _ in range(2)]
    accA = a32.get([128, NJ]); b0t = a32.get([128, NJ])
    za = [a32.get([128, 512]) for _ in range(2)]
    zr = a32.get([128, 512])

    def bf512():
        return a32.get([128, 256]).bitcast(BF16)

    EV = {nm: [(bf512(), bf512()) for _ in range(2)] for nm in ("A", "B", "C", "D")}
    TMP = [[bf512() for _ in range(4)] for _ in range(2)]
    tmp_rr = [0]
    h3T = a16.get([128, 2 * L])
    Ub = [a16.get([128, 32, 128]) for _ in range(2)]
    Kb = [a16.get([128, 32, 128]) for _ in range(2)]
    AfrB = [a16.get([128, 512]) for _ in range(2)]; AfiB = [a16.get([128, 512]) for _ in range(2)]
    AdrB = [a16.get([128, 512]) for _ in range(2)]; AdiB = [a16.get([128, 512]) for _ in range(2)]
    YrB = [a16.get([128, 8, 64]) for _ in range(2)]; YiB = [a16.get([128, 8, 64]) for _ in range(2)]
    ZrB = [a16.get([128, 512]) for _ in range(2)]; ZiB = [a16.get([128, 512]) for _ in range(2)]
    KrB = [a16.get([128, 512]) for _ in range(2)]; KiB = [a16.get([128, 512]) for _ in range(2)]
    T1r8 = a16.get([128, 512]); T1i8 = a16.get([128, 512])
    T2r4 = a16.get([128, 512]); T2i4 = a16.get([128, 512])
    F1m = a16.get([128, 128])
    F1d = a16.get([128, 128])
    uot = a16.get([128, 2 * NJ]).bitcast(F32)
    F2r_m = a16.get([128, 128]); F2i_m = a16.get([128, 128]); F2n_m = a16.get([128, 128])
    G1m = a16.get([128, 256]); G2m = a16.get([128, 256])
    Hcm = a16.get([128, 32]); Hsm = a16.get([128, 32])
    for (dst_, src_, nm) in ((T1r8, fT1r, "T1r"), (T1i8, fT1i, "T1i"), (T2r4, fT2r, "T2r"), (T2i4, fT2i, "T2i"), (F1m, fF1, "F1m"), (F1d[0:64, :], fF1d, "F1d"), (F2r_m, fF2r, "F2r"), (F2i_m, fF2i, "F2i"), (F2n_m, fF2n, "F2n"),
                             (G1m, fG1, "G1"), (G2m, fG2, "G2"), (Hcm, fHc, "Hc"), (Hsm, fHs, "Hs")):
        P.dma("pool", lambda e, dst_=dst_, src_=src_: e.dma_start(out=dst_, in_=src_), writes=[nm])

    NPC = 2 * L // 512
    zaS = [za, [a32.get([128, 512]) for _ in range(2)]]
    zrS = [zr, a32.get([128, 512])]
    zinS = [t512[0], t512[1]]
    m2S = [t512[2], kch[0]]
    for pc0 in range(0, NPC, 2):
        st_ = []
        for s_i in range(2):
            pc = pc0 + s_i
            zc, zk = zinS[s_i], "zin%d" % s_i
            P.dma("sp", lambda e, pc=pc, zc=zc: e.dma_start(out=zc[0:33, :], in_=zq[:, pc * 512:(pc + 1) * 512]), writes=[zk])
            P.dma("sp", lambda e, pc=pc, s_i=s_i: e.dma_start(out=m2S[s_i], in_=m2q[:, pc * 512:(pc + 1) * 512]), writes=["m2_%d" % s_i])
            st_.append([zc, zk, 33])
        for li, wl in enumerate((hw1_s, hw2_s, hw3d_s)):
            nr = 64 if li < 2 else 128
            sc1 = hfr_s[:, 0:1] if li < 2 else hfr2_s[:, 0:1]
            sc2 = hfb_s[:, li:li + 1] if li < 2 else hfb2_s[:, 0:1]
            pts = []
            for s_i in range(2):
                cur, curk, kdim = st_[s_i]
                pt, pk = next_ps()
                pts.append((pt, pk))
                P.op("pe", lambda e, pt=pt, cur=cur, kdim=kdim, wl=wl, nr=nr: e.matmul(pt[0:nr, :], lhsT=wl[0:kdim, :], rhs=cur[0:kdim, :],
                                                                         start=True, stop=True),
                     reads=[curk, "hw%d" % (li + 1), "hw3d"], writes=[pk])
            aa = [(zaS[s_i][li % 2], "za%d_%d" % (s_i, li % 2)) for s_i in range(2)]
            zz = [(zrS[s_i], "zr%d" % s_i) for s_i in range(2)]
            for s_i in range(2):
                (pt, pk), (a_, ak) = pts[s_i], aa[s_i]
                P.op("dve", lambda e, pt=pt, a_=a_, nr=nr, sc1=sc1, sc2=sc2: e.tensor_scalar(out=a_[0:nr, :], in0=pt[0:nr, :], scalar1=sc1, scalar2=sc2,
                                                                    op0=ALU.mult, op1=ALU.add),
                     reads=[pk, "hfr", "hfb", "hfr2", "hfb2"], writes=[ak])
            for s_i in range(2):
                (a_, ak), (z_, zk_) = aa[s_i], zz[s_i]
                P.op("dve", lambda e, a_=a_, z_=z_, nr=nr: e.tensor_scalar(out=z_[0:nr, :], in0=a_[0:nr, :], scalar1=1.0 / TWO_PI, scalar2=MAGIC,
                                                                    op0=ALU.mult, op1=ALU.add), reads=[ak], writes=[zk_])
            for s_i in range(2):
                (z_, zk_) = zz[s_i]
                P.op("dve", lambda e, z_=z_, nr=nr: e.tensor_scalar(out=z_[0:nr, :], in0=z_[0:nr, :], scalar1=MAGIC, scalar2=-TWO_PI,
                                                             op0=ALU.subtract, op1=ALU.mult), reads=[zk_], writes=[zk_])
            for s_i in range(2):
                (a_, ak), (z_, zk_) = aa[s_i], zz[s_i]
                P.op("dve", lambda e, a_=a_, z_=z_, nr=nr: e.tensor_tensor(out=a_[0:nr, :], in0=a_[0:nr, :], in1=z_[0:nr, :], op=ALU.add),
                     reads=[ak, zk_], writes=[ak])
            for s_i in range(2):
                (a_, ak) = aa[s_i]
                P.op("dve", lambda e, a_=a_, nr=nr: e.tensor_scalar(out=a_[0:nr, :], in0=a_[0:nr, :], scalar1=-math.pi, scalar2=math.pi,
                                                             op0=ALU.max, op1=ALU.min), reads=[ak], writes=[ak])
            for s_i in range(2):
                (a_, ak) = aa[s_i]
                P.op("act", lambda e, a_=a_, nr=nr: e.activation(out=a_[0:nr, :], in_=a_[0:nr, :], func=AF.Sin), reads=[ak], writes=[ak])
                st_[s_i] = [a_, ak, 64]
            if li == 2:
                for s_i in range(2):
                    (a_, ak) = aa[s_i]
                    pc = pc0 + s_i
                    P.op("dve", lambda e, a_=a_, pc=pc, s_i=s_i: e.tensor_tensor(out=h3T[:, pc * 512:(pc + 1) * 512], in0=a_[:, :], in1=m2S[s_i],
                                                                                 op=ALU.mult),
                         reads=[ak, "m2_%d" % s_i], writes=["h3T"])
    P.full_barrier()

    ud_a = ud.rearrange("c (a r) -> a c r", r=128)
    kd_a = kd.rearrange("c (a r) -> a c r", r=128)
    yc_a = ycd.rearrange("c (a r) -> a c r", r=128)

    def cmul(o_re, o_im, p_re, p_im, t_re, t_im, rk, wk):
        i = tmp_rr[0]
        tmp_rr[0] = 1 - i
        t1, t2, t3, t4 = TMP[i]
        k1, k2, k3, k4 = ["tmp%d_%d" % (i, j) for j in range(4)]
        P.op("dve", lambda e: e.tensor_tensor(out=t1, in0=p_re, in1=t_re, op=ALU.mult), reads=rk, writes=[k1])
        P.op("dve", lambda e: e.tensor_tensor(out=t2, in0=p_im, in1=t_im, op=ALU.mult), reads=rk, writes=[k2])
        P.op("dve", lambda e: e.tensor_tensor(out=t3, in0=p_re, in1=t_im, op=ALU.mult), reads=rk, writes=[k3])
        P.op("dve", lambda e: e.tensor_tensor(out=t4, in0=p_im, in1=t_re, op=ALU.mult), reads=rk, writes=[k4])
        P.op("dve", lambda e: e.tensor_tensor(out=o_re, in0=t1, in1=t2, op=ALU.subtract), reads=[k1, k2], writes=wk[0:1])
        P.op("dve", lambda e: e.tensor_tensor(out=o_im, in0=t3, in1=t4, op=ALU.add), reads=[k3, k4], writes=wk[1:2])

    def evac(stage, gi, src_re, src_im, srck, shape3=None):
        er, ei = EV[stage][gi]
        kr, ki = "ev%s%d_r" % (stage, gi), "ev%s%d_i" % (stage, gi)
        o_r = er if shape3 is None else v3(er, *shape3)
        o_i = ei if shape3 is None else v3(ei, *shape3)
        P.op("act", lambda e: e.copy(out=o_r, in_=src_re), reads=srck, writes=[kr])
        P.op("act", lambda e: e.copy(out=o_i, in_=src_im), reads=srck, writes=[ki])
        return er, ei, [kr, ki]

    def v3(ap2d, a, b):
        return ap2d.rearrange("p (a b) -> p a b", a=a)

    def fwd(src, nrow, srck, A_re, A_im, akeys, via_pool=None):
        pd_, pdk = next_pd()
        for j in range(8):
            P.op("pe", lambda e, j=j: e.matmul(pd_[:, j * 128:(j + 1) * 128], lhsT=src[0:nrow, j, :], rhs=F1m[0:nrow, :],
                                               start=True, stop=True), reads=[srck, "F1m"], writes=pdk, inc=(j == 7))
        if via_pool is None:
            pv = pd_.rearrange("p (c t k) -> p c t k", c=8, t=2)
            cmul(v3(A_re, 8, 64), v3(A_im, 8, 64), pv[:, :, 0, :], pv[:, :, 1, :], v3(T1r8, 8, 64), v3(T1i8, 8, 64),
                 pdk + ["T1r", "T1i"], akeys)
        else:
            sf, sfk = via_pool
            P.op("act", lambda e: e.copy(out=sf, in_=pd_), reads=pdk, writes=[sfk])
            pv = sf.rearrange("p (c t k) -> p c t k", c=8, t=2)
            cmul(v3(A_re, 8, 64), v3(A_im, 8, 64), pv[:, :, 0, :], pv[:, :, 1, :], v3(T1r8, 8, 64), v3(T1i8, 8, 64),
                 [sfk, "T1r", "T1i"], akeys, eng="pool")
        xr, xrk = next_ps()
        xi, xik = next_ps()
        P.op("pe", lambda e: e.matmul(xr[:, :], lhsT=F2r_m, rhs=A_re, start=True, stop=False), reads=["F2r", akeys[0]], writes=[xrk], inc=False)
        P.op("pe", lambda e: e.matmul(xr[:, :], lhsT=F2n_m, rhs=A_im, start=False, stop=True), reads=["F2n", akeys[1]], writes=[xrk])
        P.op("pe", lambda e: e.matmul(xi[:, :], lhsT=F2i_m, rhs=A_re, start=True, stop=False), reads=["F2i", akeys[0]], writes=[xik], inc=False)
        P.op("pe", lambda e: e.matmul(xi[:, :], lhsT=F2r_m, rhs=A_im, start=False, stop=True), reads=["F2r", akeys[1]], writes=[xik])
        return xr, xi, [xrk, xik]

    l1cB = [l1c, P.sb("l1c2", [128, 40], F32)]

    kch.append(a32.get([128, 512]))

    def taps_slice(cc, pcs):
        l1 = l1cB[cc % 2]
        l1k = "l1c_%d" % (cc % 2)
        pcs = list(pcs)
        for g0 in range(0, len(pcs), 3):
            grp_ = pcs[g0:g0 + 3]
            pfs = []
            for pc in grp_:
                j = pc % 3
                sl = slice(pc * 512, (pc + 1) * 512)
                P.dma("sp", lambda e, j=j, sl=sl: e.dma_start(out=t512[j], in_=ttq[:, sl]), writes=["t512_%d" % j])
            for pc in grp_:
                j = pc % 3
                sl = slice(pc * 512, (pc + 1) * 512)
                pf, pfk = next_ps()
                pfs.append((pf, pfk))
                P.op("pe", lambda e, pf=pf, sl=sl: e.matmul(pf[:, :], lhsT=hwo_s[:, cc * 128:(cc + 1) * 128], rhs=h3T[:, sl],
                                                            start=True, stop=True), reads=["hwo", "h3T"], writes=[pfk])
                P.op("act", lambda e, j=j: e.activation(out=t512[j], in_=t512[j], func=AF.Exp, scale=negd_s[:, cc:cc + 1]),
                     reads=["t512_%d" % j, "negd"], writes=["t512_%d" % j])
            for pc, (pf, pfk) in zip(grp_, pfs):
                j = pc % 3
                P.op("dve", lambda e, pf=pf, j=j: e.tensor_tensor(out=kch[j], in0=pf[:, :], in1=t512[j], op=ALU.mult),
                     reads=[pfk, "t512_%d" % j], writes=["kch%d" % j])
            for pc in grp_:
                j = pc % 3
                sl = slice(pc * 512, (pc + 1) * 512)
                P.op("act", lambda e, j=j, pc=pc: e.activation(out=za[0], in_=kch[j], func=AF.Abs, accum_out=l1[:, pc:pc + 1]),
                     reads=["kch%d" % j], writes=["za0", l1k])
                P.dma("pool", lambda e, j=j, sl=sl: e.dma_start(out=kd[cc * 128:(cc + 1) * 128, sl], in_=kch[j]), reads=["kch%d" % j],
                      writes=["kd_%d_%d" % (cc, pc)])

    def taps_finish(cc):
        l1 = l1cB[cc % 2]
        l1k = "l1c_%d" % (cc % 2)
        P.op("dve", lambda e: e.tensor_reduce(out=l1[:, 32:33], in_=l1[:, 0:NPC], axis=mybir.AxisListType.X, op=ALU.add),
             reads=[l1k], writes=[l1k])
        P.op("dve", lambda e: e.tensor_scalar(out=l1[:, 32:33], in0=l1[:, 32:33], scalar1=EPS, scalar2=None, op0=ALU.add),
             reads=[l1k], writes=[l1k])
        P.op("dve", lambda e: e.reciprocal(out=l1[:, 33:34], in_=l1[:, 32:33]), reads=[l1k], writes=[l1k])

    def load_sub(cc, sc):
        slot = sc % 2
        cbase = cc * 128 + sc * 32
        P.dma("pool", lambda e: e.dma_start(out=Ub[slot][0:64, :, :], in_=ud_a[:, cbase:cbase + 32, :]),
              reads=["ud_%d_%d" % (cc, st_) for st_ in range(L // TOK)], writes=["Ub%d" % slot])
        P.dma("pool", lambda e: e.dma_start(out=Kb[slot], in_=kd_a[:, cbase:cbase + 32, :]),
              reads=["kd_%d_%d" % (cc, pc_) for pc_ in range(NPC)], writes=["Kb%d" % slot])

    class Grp:
        pass

    def mk(t):
        G_ = Grp()
        G_.cc, rem = divmod(t, 16)
        G_.sc, G_.g = divmod(rem, 4)
        G_.slot = G_.sc % 2
        G_.c0 = G_.cc * 128 + G_.sc * 32 + G_.g * 8
        gi = t % 2
        G_.gi = gi
        G_.sfx = "_%d" % gi
        G_.Afr, G_.Afi, G_.Adr, G_.Adi = AfrB[gi], AfiB[gi], AdrB[gi], AdiB[gi]
        G_.Yr, G_.Yi, G_.Zr, G_.Zi, G_.Kr, G_.Ki = YrB[gi], YiB[gi], ZrB[gi], ZiB[gi], KrB[gi], KiB[gi]
        return G_

    def s1(src, nrow, srck):
        pd_, pdk = next_pd()
        fm, fk = (F1m, "F1m") if nrow == 128 else (F1d, "F1d")
        for j in range(8):
            P.op("pe", lambda e, j=j: e.matmul(pd_[:, j * 128:(j + 1) * 128], lhsT=src[0:nrow, j, :], rhs=fm[0:nrow, :],
                                               start=True, stop=True), reads=[srck, fk], writes=pdk, inc=(j == 7))
        return pd_, pdk

    def s2(A_re, A_im, akeys):
        xr, xrk = next_ps()
        xi, xik = next_ps()
        P.op("pe", lambda e: e.matmul(xr[:, :], lhsT=F2r_m, rhs=A_re, start=True, stop=False), reads=["F2r", akeys[0]], writes=[xrk], inc=False)
        P.op("pe", lambda e: e.matmul(xr[:, :], lhsT=F2n_m, rhs=A_im, start=False, stop=True), reads=["F2n", akeys[1]], writes=[xrk])
        P.op("pe", lambda e: e.matmul(xi[:, :], lhsT=F2i_m, rhs=A_re, start=True, stop=False), reads=["F2i", akeys[0]], writes=[xik], inc=False)
        P.op("pe", lambda e: e.matmul(xi[:, :], lhsT=F2r_m, rhs=A_im, start=False, stop=True), reads=["F2r", akeys[1]], writes=[xik])
        return xr, xi, [xrk, xik]

    def stA(G_):
        pd_, pdk = s1(Kb[G_.slot][:, G_.g * 8:(G_.g + 1) * 8, :], 128, "Kb%d" % G_.slot)
        pv = pd_.rearrange("p (c t k) -> p c t k", c=8, t=2)
        er, ei, ek = evac("A", G_.gi, pv[:, :, 0, :], pv[:, :, 1, :], pdk, (8, 64))
        cmul(G_.Afr, G_.Afi, er, ei, T1r8, T1i8, ek + ["T1r", "T1i"], ["Afr" + G_.sfx, "Afi" + G_.sfx])

    def stB(G_):
        xr, xi, xk = s2(G_.Afr, G_.Afi, ["Afr" + G_.sfx, "Afi" + G_.sfx])
        P.op("act", lambda e: e.copy(out=G_.Kr, in_=xr[:, :]), reads=xk[0:1], writes=["Kr" + G_.sfx])
        P.op("act", lambda e: e.copy(out=G_.Ki, in_=xi[:, :]), reads=xk[1:2], writes=["Ki" + G_.sfx])
        pd_, pdk = s1(Ub[G_.slot][:, G_.g * 8:(G_.g + 1) * 8, :], 64, "Ub%d" % G_.slot)
        pv = pd_.rearrange("p (c t k) -> p c t k", c=8, t=2)
        er, ei, ek = evac("B", G_.gi, pv[:, :, 0, :], pv[:, :, 1, :], pdk, (8, 64))
        cmul(G_.Adr, G_.Adi, er, ei, T1r8, T1i8, ek + ["T1r", "T1i"], ["Adr" + G_.sfx, "Adi" + G_.sfx])

    def stC(G_):
        xr, xi, xk = s2(G_.Adr, G_.Adi, ["Adr" + G_.sfx, "Adi" + G_.sfx])
        er, ei, ek = evac("C", G_.gi, xr[:, :], xi[:, :], xk)
        cmul(G_.Yr.rearrange("p c k -> p (c k)"), G_.Yi.rearrange("p c k -> p (c k)"), er, ei, G_.Kr, G_.Ki,
             ek + ["Kr" + G_.sfx, "Ki" + G_.sfx], ["Yr" + G_.sfx, "Yi" + G_.sfx])

    def stD(G_):
        pz, pzk = next_pd()
        for j in range(8):
            p_, hf = divmod(j, 2)
            o_ = pz[hf * 64:(hf + 1) * 64, p_ * 256:(p_ + 1) * 256]
            P.op("pe", lambda e, o_=o_, j=j, hf=hf: e.matmul(o_, lhsT=G_.Yr[:, j, :], rhs=G1m, start=True, stop=False,
                                                             tile_position=(0, hf * 64)),
                 reads=["Yr" + G_.sfx, "G1"], writes=pzk, inc=False)
            P.op("pe", lambda e, o_=o_, j=j, hf=hf: e.matmul(o_, lhsT=G_.Yi[:, j, :], rhs=G2m, start=False, stop=True,
                                                             tile_position=(0, hf * 64)),
                 reads=["Yi" + G_.sfx, "G2"], writes=pzk, inc=(j == 7))
        zv_ = pz.rearrange("p (c t r) -> p c t r", c=4, t=2)
        er, ei, ek = evac("D", G_.gi, zv_[:, :, 0, :], zv_[:, :, 1, :], pzk, (4, 128))
        cmul(G_.Zr, G_.Zi, er, ei, T2r4, T2i4, ek + ["T2r", "T2i"], ["Zr" + G_.sfx, "Zi" + G_.sfx])

    def stE(G_):
        for hf in range(2):
            py, pyk = next_ps()
            lo = hf * 64
            P.op("pe", lambda e, py=py, lo=lo: e.matmul(py[0:32, :], lhsT=Hcm[lo:lo + 64, :], rhs=G_.Zr[lo:lo + 64, :],
                                                        start=True, stop=False), reads=["Hc", "Zr" + G_.sfx], writes=[pyk], inc=False)
            P.op("pe", lambda e, py=py, lo=lo: e.matmul(py[0:32, :], lhsT=Hsm[lo:lo + 64, :], rhs=G_.Zi[lo:lo + 64, :],
                                                        start=False, stop=True), reads=["Hs", "Zi" + G_.sfx], writes=[pyk])
            sg = stg2[hf]
            P.op("act", lambda e, py=py, sg=sg: e.copy(out=sg[0:32, :], in_=py[0:32, :]), reads=[pyk], writes=["stg2_%d" % hf])
            P.dma("pool", lambda e, sg=sg, hf=hf: e.dma_start(
                out=yc_a[0:17, G_.c0 + hf:G_.c0 + 8:2, :], in_=sg[0:17, :].rearrange("p (c r) -> p c r", c=4)),
                reads=["stg2_%d" % hf], writes=["ycd_%d_%d" % (G_.c0, hf)])

    def combine(cc):
        l1 = l1cB[cc % 2]
        l1k = "l1c_%d" % (cc % 2)
        P.dma("sp", lambda e: e.dma_start(out=accA, in_=ycd[cc * 128:(cc + 1) * 128, 0:NJ]),
              reads=["ycd_%d_%d" % (cc * 128 + g8 * 8, hf) for g8 in range(16) for hf in range(2)], writes=["accA"])
        P.dma("sp", lambda e: e.dma_start(out=b0t, in_=b0[cc * 128:(cc + 1) * 128, E0:E0 + NJ]), reads=["b0"], writes=["b0t"])
        udk = ["ud_%d_%d" % (cc, st_) for st_ in range(L // TOK)]
        P.dma("sp", lambda e: e.dma_start(out=uot[:, 0:1], in_=ud[cc * 128:(cc + 1) * 128, L - 1:L], allow_slow_non_contiguous=True), reads=udk, writes=["uot"])
        P.dma("sp", lambda e: e.dma_start(out=uot[:, 1:NJ], in_=ud[cc * 128:(cc + 1) * 128, 0:NJ - 1]), reads=udk, writes=["uot2"])
        P.op("dve", lambda e: e.tensor_scalar(out=accA, in0=accA, scalar1=l1[:, 33:34], scalar2=None, op0=ALU.mult),
             reads=["accA", l1k], writes=["accA"])
        P.op("dve", lambda e: e.scalar_tensor_tensor(out=accA, in0=uot, scalar=dsk_s[:, cc:cc + 1], in1=accA,
                                                     op0=ALU.mult, op1=ALU.add), reads=["uot", "uot2", "dsk", "accA"], writes=["accA"])
        P.op("dve", lambda e: e.tensor_tensor(out=accA, in0=accA, in1=b0t, op=ALU.mult), reads=["accA", "b0t"], writes=["accA"])
        P.dma("sp", lambda e: e.dma_start(out=ybs[cc * 128:(cc + 1) * 128, E0:E0 + NJ], in_=accA), reads=["accA"], writes=["ybs_%d" % cc])

    NG = 128
    bank_n[0] = 8
    taps_slice(0, range(NPC))
    taps_finish(0)
    load_sub(0, 0)
    grp = {}
    for t in range(NG + 4):
        if t < NG:
            grp[t] = mk(t)
            stA(grp[t])
        if 0 <= t - 1 < NG:
            stB(grp[t - 1])
        if 0 <= t - 2 < NG:
            stC(grp[t - 2])
        if 0 <= t - 3 < NG:
            stD(grp[t - 3])
        if 0 <= t - 4 < NG:
            stE(grp[t - 4])
            if (t - 4) % 16 == 15:
                combine((t - 4) // 16)
            del grp[t - 4]
        if t < NG:
            cc, rem = divmod(t, 16)
            if cc + 1 < 8 and rem < 11:
                lo_ = rem * 3
                hi_ = min(lo_ + 3, NPC)
                taps_slice(cc + 1, range(lo_, hi_))
                if rem == 10:
                    taps_finish(cc + 1)
            if rem % 4 == 0:
                nt = t + 4
                if nt < NG:
                    load_sub(nt // 16, (nt % 16) // 4)
    P.full_barrier()
    bank_n[0] = NPS
    bank_p[0] = 0
    a32.reset(); a16.reset()

    xTB = [a32.get([128, KD, TM]) for _ in range(3)]
    zu = a32.get([128, 8, TM])
    gav = a32.get([128, DA])
    ssq = a32.get([128, 2])
    ya = a32.get([128, 8, TM])
    yb = a32.get([128, 8, TM])
    stmp = a32.get([128, 128])
    hTB2 = [a16.get([128, KD, TM]) for _ in range(2)]
    wblk = [a16.get([128, KD, 128]) for i in range(4)]
    wv = [a16.get([128, KD, 512]) for i in range(2)]
    NRES = 6
    wres = [a16.get([128, KD, 128]) for i in range(NRES)]
    zv = a16.get([128, DA])
    junk = a16.get([128, DA])
    mT = a16.get([128, KD, TM])
    wb_rr = [0]

    def load_wblk(src_ap):
        i = wb_rr[0]
        wb_rr[0] = (i + 1) % 4
        P.dma("pool", lambda e: e.dma_start(out=wblk[i][:], in_=src_ap), writes=["wblk%d" % i])
        return wblk[i], "wblk%d" % i

    def load_wv():
        for half in range(2):
            for j in range(4):
                P.dma("pool", lambda e, half=half, j=j: e.dma_start(out=wv[half][:, :, j * 128:(j + 1) * 128],
                                                                    in_=wblk_ap(w_in, DA + half * 512 + j * 128)),
                      writes=["wv%d_%d" % (half, j)])

    def mF1a(m):
        xT, xk = xTB[m % 3], "xT%d" % (m % 3)
        P.dma("sp", lambda e: e.dma_start(out=xT, in_=xTm_v[:, :, m * TM:(m + 1) * TM]), writes=[xk])

    def mF1b(m):
        xT, xk = xTB[m % 3], "xT%d" % (m % 3)
        hT, hk = hTB2[m % 2], "hT%d" % (m % 2)
        stats(xT, KD, TM, D, [xk])
        for k in range(KD):
            P.op("dve", lambda e, k=k: e.scalar_tensor_tensor(out=hT[:, k, :], in0=xT[:, k, :], scalar=g1_s[:, k:k + 1],
                                                              in1=rstd[:, 0:TM], op0=ALU.mult, op1=ALU.mult),
                 reads=[xk, "g1", "rstd"], writes=[hk])

    def mF2(m):
        hT, hk = hTB2[m % 2], "hT%d" % (m % 2)
        for h in range(8):
            if h < NRES:
                wt, wk = wres[h], "wres%d" % h
            else:
                wt, wk = load_wblk(wblk_ap(w_in, h * 128))
            pt, pk = next_ps()
            for k in range(KD):
                P.op("pe", lambda e, k=k, wt=wt, pt=pt: e.matmul(pt[:, 0:TM], lhsT=wt[:, k, :], rhs=hT[:, k, :],
                                                                 start=(k == 0), stop=(k == KD - 1)),
                     reads=[wk, hk], writes=[pk], inc=(k == KD - 1))
            P.op("act", lambda e, h=h, pt=pt: e.activation(out=zu[:, h, :], in_=pt[:, 0:TM], func=AF.Gelu),
                 reads=[pk], writes=["zu"])
        for tb in range(TM // 128):
            for half in range(2):
                pt, pk = next_ps()
                for k in range(KD):
                    P.op("pe", lambda e, k=k, pt=pt, tb=tb, half=half: e.matmul(
                        pt[:, :], lhsT=hT[:, k, tb * 128:(tb + 1) * 128], rhs=wv[half][:, k, :],
                        start=(k == 0), stop=(k == KD - 1)),
                        reads=["wv%d_%d" % (half, j_) for j_ in range(4)] + [hk], writes=[pk], inc=(k == KD - 1))
                P.op("act", lambda e, pt=pt, half=half: e.activation(out=gav[:, half * 512:(half + 1) * 512], in_=pt[:, :],
                                                                     func=AF.Gelu), reads=[pk], writes=["gav"])
            P.op("act", lambda e: e.activation(out=junk, in_=gav, func=AF.Square, accum_out=ssq[:, 0:1]),
                 reads=["gav"], writes=["junk", "ssq"])
            P.op("act", lambda e: e.activation(out=ssq[:, 1:2], in_=ssq[:, 0:1], func=AF.Sqrt, scale=1.0 / DA, bias=eps_s[:, 0:1]),
                 reads=["ssq", "eps"], writes=["ssq"])
            P.op("dve", lambda e: e.reciprocal(out=ssq[:, 1:2], in_=ssq[:, 1:2]), reads=["ssq"], writes=["ssq"])
            P.op("dve", lambda e: e.scalar_tensor_tensor(out=zv, in0=gav, scalar=ssq[:, 1:2], in1=gsgu_s[:],
                                                         op0=ALU.mult, op1=ALU.mult),
                 reads=["gav", "ssq", "gsgu"], writes=["zv"])
            for h in range(8):
                P.op("pe", lambda e, h=h: e.matmul(ps_h[:, 0:128], lhsT=zv[:, h * 128:(h + 1) * 128], rhs=wsT_s[:, h, :],
                                                   start=True, stop=True), reads=["zv", "wsT"], writes=["ps7"])
                P.op("dve", lambda e, h=h: e.tensor_tensor(out=stmp, in0=ps_h[:, 0:128], in1=bsr_s[:, h, :], op=ALU.add),
                     reads=["ps7", "bsr"], writes=["stmp"])
                P.op("dve", lambda e, h=h, tb=tb: e.tensor_tensor(out=ya[:, h, tb * 128:(tb + 1) * 128], in0=stmp,
                                                                  in1=zu[:, h, tb * 128:(tb + 1) * 128], op=ALU.mult),
                     reads=["stmp", "zu"], writes=["ya"])

    def mB1a(m):
        P.dma("sp", lambda e: e.dma_start(out=yb, in_=ybs_v[:, :, m * TM:(m + 1) * TM]), reads=["ybs_%d" % c_ for c_ in range(8)] + ybz_keys, writes=["yb"])

    def mB1b(m):
        stats(ya, 8, TM, DA, ["ya"])
        for h in range(8):
            P.op("dve", lambda e, h=h: e.scalar_tensor_tensor(out=mT[:, h, :], in0=ya[:, h, :], scalar=ga_s[:, h:h + 1],
                                                              in1=rstd[:, 0:TM], op0=ALU.mult, op1=ALU.mult),
                 reads=["ya", "ga", "rstd"], writes=["mT"])
        stats(yb, 8, TM, DB, ["yb"])
        for h in range(8):
            P.op("dve", lambda e, h=h: e.scalar_tensor_tensor(out=mT[:, 8 + h, :], in0=yb[:, h, :], scalar=gb_s[:, h:h + 1],
                                                              in1=rstd[:, 0:TM], op0=ALU.mult, op1=ALU.mult),
                 reads=["yb", "gb", "rstd"], writes=["mT"])

    def mB2(m):
        xT, xk = xTB[m % 3], "xT%d" % (m % 3)
        for ob in range(KD):
            wt, wk = load_wblk(wblk_ap(w_out, ob * 128))
            pt, pk = next_ps()
            for k in range(KD):
                P.op("pe", lambda e, k=k, wt=wt, pt=pt: e.matmul(pt[:, 0:TM], lhsT=wt[:, k, :], rhs=mT[:, k, :],
                                                                 start=(k == 0), stop=(k == KD - 1)),
                     reads=[wk, "mT"], writes=[pk], inc=(k == KD - 1))
            P.op("dve", lambda e, ob=ob, pt=pt: e.tensor_tensor(out=xT[:, ob, :], in0=pt[:, 0:TM], in1=xT[:, ob, :], op=ALU.add),
                 reads=[pk, xk], writes=[xk])
        P.dma("sp", lambda e: e.dma_start(out=x1s_v[:, :, m * TM:(m + 1) * TM], in_=xT), reads=[xk], writes=["x1s"])

    load_wv()
    for h in range(NRES):
        P.dma("pool", lambda e, h=h: e.dma_start(out=wres[h], in_=wblk_ap(w_in, h * 128)), writes=["wres%d" % h])
    mF1a(0)
    mF1b(0)
    for it in range(nmt + 1):
        if it + 1 < nmt:
            mF1a(it + 1)
        if it < nmt:
            mB1a(it)
            mF2(it)
        if it - 1 >= 0:
            mB2(it - 1)
        if it + 1 < nmt:
            mF1b(it + 1)
        if it < nmt:
            mB1b(it)

    TH = TF + 2
    P.full_barrier()
    a32.reset()
    a16.reset()
    x1fB = [a32.get([128, KD, TH]) for _ in range(2)]
    gp = a32.get([128, TH])
    gc = a32.get([128, TF])
    ge = a32.get([128, TF])
    h2 = a16.get([128, KD, TH])
    act = a16.get([128, KF, TF])
    wg = [a16.get([128, KD, 128]) for i in range(2)]
    wu = [a16.get([128, KD, 128]) for i in range(2)]
    wd = [a16.get([128, KF, 128]) for i in range(2)]
    yT_keys = []

    def fPr(ft):
        x1f, xk = x1fB[ft % 2], "x1f%d" % (ft % 2)
        c0 = HALO + ft * TF - 1
        P.dma("sp", lambda e: e.dma_start(out=x1f, in_=x1s_v[:, :, c0:c0 + TH]), reads=["x1s"], writes=[xk])
        P.dma("sp", lambda e: e.dma_start(out=mask_s[:, 0:TH], in_=mask[:, c0:c0 + TH]), writes=["mask"])
        for cix in (0, TH - 1):
            P.op("dve", lambda e, cix=cix: e.tensor_scalar(out=x1f[:, :, cix:cix + 1], in0=x1f[:, :, cix:cix + 1],
                                                           scalar1=mask_s[:, cix:cix + 1], scalar2=None, op0=ALU.mult),
                 reads=[xk, "mask"], writes=[xk])
        stats(x1f, KD, TH, D, [xk], pieces=[(0, 512), (512, TH)])
        for k in range(KD):
            P.op("dve", lambda e, k=k: e.scalar_tensor_tensor(out=h2[:, k, :], in0=x1f[:, k, :], scalar=g2_s[:, k:k + 1],
                                                              in1=rstd[:, 0:TH], op0=ALU.mult, op1=ALU.mult),
                 reads=[xk, "g2", "rstd"], writes=["h2"])

    def fU(ft, fbs):
        for fb in fbs:
            i = fb % 2
            P.dma("pool", lambda e, i=i, fb=fb: e.dma_start(out=wg[i][:], in_=wblk_ap(w_up, fb * 128)),
                  writes=["wg%d" % i])
            P.dma("pool", lambda e, i=i, fb=fb: e.dma_start(out=wu[i][:], in_=wblk_ap(w_up, DFF + fb * 128)),
                  writes=["wu%d" % i])
            pg, pgk = next_ps()
            pg2, pg2k = next_ps()
            pv, pvk = next_ps()
            HH = TH // 2
            for k in range(KD):
                P.op("pe", lambda e, k=k, i=i, pg=pg: e.matmul(pg[:, 0:HH], lhsT=wg[i][:, k, :], rhs=h2[:, k, 0:HH],
                                                               start=(k == 0), stop=(k == KD - 1)),
                     reads=["wg%d" % i, "h2"], writes=[pgk], inc=(k == KD - 1))
            for k in range(KD):
                P.op("pe", lambda e, k=k, i=i, pg2=pg2: e.matmul(pg2[:, 0:TH - HH], lhsT=wg[i][:, k, :], rhs=h2[:, k, HH:TH],
                                                                 start=(k == 0), stop=(k == KD - 1)),
                     reads=["wg%d" % i, "h2"], writes=[pg2k], inc=(k == KD - 1))
            for k in range(KD):
                P.op("pe", lambda e, k=k, i=i, pv=pv: e.matmul(pv[:, 0:TF], lhsT=wu[i][:, k, :], rhs=h2[:, k, 1:TF + 1],
                                                               start=(k == 0), stop=(k == KD - 1)),
                     reads=["wu%d" % i, "h2"], writes=[pvk], inc=(k == KD - 1))
            P.op("act", lambda e, pg=pg: e.copy(out=gp[:, 0:HH], in_=pg[:, 0:HH]), reads=[pgk], writes=["gp"])
            P.op("act", lambda e, pg2=pg2: e.copy(out=gp[:, HH:TH], in_=pg2[:, 0:TH - HH]), reads=[pg2k], writes=["gp"])
            P.op("dve", lambda e, fb=fb: e.tensor_scalar(out=gc, in0=gp[:, 1:TF + 1], scalar1=dww_s[:, fb, 1:2],
                                                         scalar2=dwb_s[:, fb:fb + 1], op0=ALU.mult, op1=ALU.add),
                 reads=["gp", "dww", "dwb"], writes=["gc"])
            P.op("dve", lambda e, fb=fb: e.scalar_tensor_tensor(out=gc, in0=gp[:, 0:TF], scalar=dww_s[:, fb, 0:1],
                                                                in1=gc, op0=ALU.mult, op1=ALU.add),
                 reads=["gp", "dww", "gc"], writes=["gc"])
            P.op("dve", lambda e, fb=fb: e.scalar_tensor_tensor(out=gc, in0=gp[:, 2:TF + 2], scalar=dww_s[:, fb, 2:3],
                                                                in1=gc, op0=ALU.mult, op1=ALU.add),
                 reads=["gp", "dww", "gc"], writes=["gc"])
            P.op("act", lambda e: e.activation(out=ge, in_=gc, func=AF.Gelu), reads=["gc"], writes=["ge"])
            P.op("dve", lambda e, fb=fb, pv=pv: e.tensor_tensor(out=act[:, fb, :], in0=pv[:, 0:TF], in1=ge, op=ALU.mult),
                 reads=[pvk, "ge"], writes=["act%d" % fb])

    def fD(ft, obs):
        x1f, xk = x1fB[ft % 2], "x1f%d" % (ft % 2)
        for ob in obs:
            i = ob % 2
            P.dma("pool", lambda e, i=i, ob=ob: e.dma_start(out=wd[i][:, 0:kf, :], in_=w_down[ob][:, 0:kf, :]),
                  writes=["wd%d" % i])
            pt, pk = next_ps()
            for k in range(kf):
                P.op("pe", lambda e, k=k, i=i, pt=pt: e.matmul(pt[:, 0:TF], lhsT=wd[i][:, k, :], rhs=act[:, k, :],
                                                               start=(k == 0), stop=(k == kf - 1)),
                     reads=["wd%d" % i, "act%d" % k], writes=[pk], inc=(k == kf - 1))
            P.op("dve", lambda e, ob=ob, pt=pt: e.tensor_tensor(out=x1f[:, ob, 1:TF + 1], in0=pt[:, 0:TF], in1=x1f[:, ob, 1:TF + 1],
                                                                op=ALU.add), reads=[pk, xk], writes=[xk])

    def fN(ft):
        x1f, xk = x1fB[ft % 2], "x1f%d" % (ft % 2)
        stats(x1f[:, :, 1:TF + 1], KD, TF, D, [xk])
        for k in range(KD):
            P.op("dve", lambda e, k=k: e.scalar_tensor_tensor(out=x1f[:, k, 1:TF + 1], in0=x1f[:, k, 1:TF + 1], scalar=gf_s[:, k:k + 1],
                                                              in1=rstd[:, 0:TF], op0=ALU.mult, op1=ALU.mult),
                 reads=[xk, "gf", "rstd"], writes=[xk])
        yT_keys.append("yT%d" % ft)
        P.dma("sp", lambda e: e.dma_start(out=yT_v[:, :, ft * TF:(ft + 1) * TF], in_=x1f[:, :, 1:TF + 1]), reads=[xk], writes=["yT%d" % ft])

    NU0 = min(4, kf)
    fPr(0)
    fU(0, range(0, NU0))
    for ft in range(nft):
        fU(ft, range(NU0, kf))
        fD(ft, range(0, 4))
        if ft + 1 < nft:
            fPr(ft + 1)
        fD(ft, range(4, KD))
        if ft + 1 < nft:
            fU(ft + 1, range(0, NU0))
        fN(ft)

    P.barrier_wait("sp", yT_keys)
    P.emit()
    P.close()
    return nc


def _tile_w(w):
    w = np.asarray(w, np.float32)
    K, C = w.shape
    return np.ascontiguousarray(w.reshape(K // 128, 128, C // 128, 128).transpose(2, 1, 0, 3))


def _pk(v, nk):
    return np.ascontiguousarray(np.asarray(v, np.float32).reshape(nk, 128).T)


def prep_inputs(inputs):
    x = np.asarray(inputs["x"], np.float32)
    shared = {
        "w_in": _tile_w(inputs["w_in"][0]),
        "w_out": _tile_w(inputs["w_out"][0]),
        "w_up": _tile_w(inputs["ffn_w_up"][0]),
        "w_down": _tile_w(inputs["ffn_w_down"][0]),
        "g1": _pk(inputs["norm1_g"][0], KD),
        "g2": _pk(inputs["norm2_g"][0], KD),
        "gf": _pk(inputs["final_g"], KD),
        "ga": _pk(inputs["outnorm_a_g"][0], 8),
        "gb": _pk(inputs["outnorm_b_g"][0], 8),
        "gsgu": np.ascontiguousarray(np.broadcast_to(np.asarray(inputs["sgu_norm_g"][0], np.float32)[None, :], (128, DA))),
        "bsr": np.ascontiguousarray(np.broadcast_to(np.asarray(inputs["sgu_b"][0], np.float32).reshape(1, 8 * 128), (128, 8 * 128))),
        "wsT": np.ascontiguousarray(np.transpose(np.asarray(inputs["sgu_w"][0], np.float32), (2, 0, 1)).reshape(128, 8 * 128)),
        "dww": np.ascontiguousarray(np.transpose(np.asarray(inputs["ffn_dw_w"][0], np.float32).reshape(3, KF, 128), (2, 1, 0)).reshape(128, KF * 3)),
        "dwb": _pk(inputs["ffn_dw_b"][0], KF),
    }
    t_lin = np.linspace(0.0, 1.0, L, dtype=np.float32)
    wv_ = (np.float32(2.0 * math.pi / L) * np.arange(L, dtype=np.float32))[:, None]
    fb_ = np.linspace(1e-4, 15, 16, dtype=np.float32)[None]
    zfull = np.concatenate([t_lin[:, None], np.cos(fb_ * wv_), -np.sin(fb_ * wv_)], axis=-1).astype(np.float32)
    max_decay = math.log(1e-2) / 0.3
    min_decay = math.log(1e-2) / 1.5
    deltas = np.abs(np.linspace(min_decay, max_decay, DB, dtype=np.float32))
    shared.update({
        "negd": _pk(-deltas, 8),
        "hw1": np.ascontiguousarray(inputs["hy_f_w1"][0], dtype=np.float32),
        "hw2": np.ascontiguousarray(inputs["hy_f_w2"][0], dtype=np.float32),
        "hw3": np.ascontiguousarray(inputs["hy_f_w3"][0], dtype=np.float32),
        "hb": np.ascontiguousarray(np.stack([inputs["hy_f_b1"][0], inputs["hy_f_b2"][0], inputs["hy_f_b3"][0]], axis=1), dtype=np.float32),
        "hfr": np.ascontiguousarray(np.asarray(inputs["hy_f_freq"][0], np.float32).reshape(64, 1)),
        "hwo2": np.ascontiguousarray(np.concatenate([inputs["hy_f_wout"][0][:, :DB], inputs["hy_f_wout"][0][:, DB:]], axis=0), dtype=np.float32),
        "hw3d": np.ascontiguousarray(np.concatenate([inputs["hy_f_w3"][0], inputs["hy_f_w3"][0]], axis=1), dtype=np.float32),
        "hfr2": np.ascontiguousarray(np.concatenate([inputs["hy_f_freq"][0], inputs["hy_f_freq"][0]]).reshape(128, 1), dtype=np.float32),
        "hb32": np.ascontiguousarray(np.concatenate([inputs["hy_f_b3"][0], inputs["hy_f_b3"][0]]).reshape(128, 1), dtype=np.float32),
        "cw": np.ascontiguousarray(np.transpose(np.asarray(inputs["hy_conv_w"][0], np.float32).reshape(3, 24, 128), (2, 1, 0)).reshape(128, 72)),
        "cb": _pk(inputs["hy_conv_b"][0], 24),
        "dsk": _pk(inputs["hy_d_skip"][0], 8),
    })
    NF = 2 * L
    ar = np.arange(128, dtype=np.float64)
    kA = np.arange(64, dtype=np.float64) + 0.5
    th1 = 2 * np.pi * np.outer(ar, kA) / 128.0
    thT = 2 * np.pi * np.outer(ar, kA) / NF
    th2 = 2 * np.pi * np.outer(ar, ar) / 128.0
    hc = np.zeros((64, 32)); hs = np.zeros((64, 32))
    hc[:, :17] = (2.0 / NF) * np.cos(th1[:17].T)
    hs[:, :17] = -(2.0 / NF) * np.sin(th1[:17].T)
    f32c = lambda a_: np.ascontiguousarray(a_, dtype=np.float32)
    shared.update({
        "fF1": f32c(np.concatenate([np.cos(th1), -np.sin(th1)], axis=1)),
        "fT1r": f32c(np.tile(np.cos(thT), (1, 8))), "fT1i": f32c(np.tile(-np.sin(thT), (1, 8))),
        "fF2r": f32c(np.cos(th2)), "fF2i": f32c(-np.sin(th2)), "fF2n": f32c(np.sin(th2)),
        "fG1": f32c(np.concatenate([np.cos(th2), np.sin(th2)], axis=1)),
        "fG2": f32c(np.concatenate([-np.sin(th2), np.cos(th2)], axis=1)),
        "fT2r": f32c(np.tile(np.concatenate([np.cos(thT.T), np.cos(thT.T)], axis=0), (1, 4))),
        "fT2i": f32c(np.tile(np.concatenate([np.sin(thT.T), np.sin(thT.T)], axis=0), (1, 4))),
        "fHc": f32c(np.concatenate([hc, hc], axis=0)), "fHs": f32c(np.concatenate([hs, hs], axis=0)),
    })
    xpad = [np.concatenate([np.zeros((1, D), np.float32), x[b], np.zeros((1, D), np.float32)], axis=0) for b in range(2)]

    def xTb_for(b, q):
        tiles_ = []
        for i in range(4):
            qi = (q + i) % 4
            tiles_.append(xpad[b][TOK * qi:TOK * qi + TOK + 2])
        return np.ascontiguousarray(np.concatenate(tiles_, axis=0).T)

    F1full = np.concatenate([np.cos(th1), -np.sin(th1)], axis=1)
    rot = {}
    for q in range(4):
        s0 = TOK * q - 1
        n = (s0 + np.arange(2 * L)) % (2 * L)
        fwd = n < L
        bwd = n > L
        pos = np.where(fwd, n, np.where(bwd, 2 * L - n, 0))
        sgn = np.where(np.arange(2 * L) < L, 1.0, -1.0).astype(np.float32)
        rot[q] = {
            "zq": np.ascontiguousarray(zfull[pos].T),
            "ttq": np.ascontiguousarray(np.broadcast_to(t_lin[pos][None, :], (128, 2 * L))),
            "m2q": np.ascontiguousarray(np.concatenate([np.broadcast_to((fwd.astype(np.float32) * sgn)[None, :], (64, 2 * L)),
                                                        np.broadcast_to((bwd.astype(np.float32) * sgn)[None, :], (64, 2 * L))], axis=0)),
        }
    in_maps = []
    for c in range(NCORES):
        b, q = divmod(c, 4)
        lo = TOK * q - HALO
        xe = np.zeros((TE, D), np.float32)
        m = np.zeros((TE,), np.float32)
        s0 = max(lo, 0)
        s1 = min(lo + TE, L)
        xe[s0 - lo:s1 - lo] = x[b, s0:s1]
        m[s0 - lo:s1 - lo] = 1.0
        d = dict(shared)
        d.update(rot[q])
        d["xTb"] = xTb_for(b, q)
        a_s = (np.arange(64) + 16 * q) % 64
        d["fF1d"] = np.ascontiguousarray(F1full[a_s], dtype=np.float32)
        d["xTm"] = np.ascontiguousarray(xe.T)
        d["mask"] = np.ascontiguousarray(np.broadcast_to(m[None, :], (128, TE)))
        in_maps.append(d)
    return in_maps


def kernel(**inputs):
    in_maps = prep_inputs(inputs)
    nc = build()
    res = run_bass_kernel_spmd(nc, in_maps, core_ids=list(range(NCORES)))
    out = np.empty((2, L, D), np.float32)
    for c in range(NCORES):
        b, q = divmod(c, 4)
        out[b, TOK * q:TOK * (q + 1), :] = res.results[c]["yT"].T
    return out
```

```python
import contextlib
import math
import numpy as np
import concourse.bass as bass
import concourse.mybir as mybir
from concourse.bass_utils import run_bass_kernel_spmd

F32 = mybir.dt.float32
BF16 = mybir.dt.bfloat16
I32 = mybir.dt.int32
ALU = mybir.AluOpType
AF = mybir.ActivationFunctionType

NCORES = 8
D = 2048
L = 8192
DA = 1024
DB = 1024
DFF = 5504
DIN = 5120
KD = D // 128
KF = DFF // 128
TOK = 2048
HALO = 128
TE = TOK + 2 * HALO
TM = 256
NMT = TE // TM
TF = 512
NFT = TOK // TF
EPS = 1e-6

ENGS = ("pe", "act", "dve", "pool", "sp")
N_DMA_SEMS = 40


class Prog:
    def __init__(self, nc):
        self.nc = nc
        self.stack = contextlib.ExitStack()
        self.ops = {e: [] for e in ENGS}
        self.cnt = {e: 0 for e in ENGS}
        self.res = {}
        self.known = {e: {} for e in ENGS}
        self.sems = {}
        for e in ENGS:
            self.sems[e] = self.stack.enter_context(nc.semaphore("s_" + e))
        self.dma_cum = {}
        self.dma_rr = {}
        for q in ("sp", "act", "pool"):
            n = N_DMA_SEMS if q != "act" else 4
            self.dma_rr[q] = [0, n]
            for k in range(n):
                self.dma_cum[(q, k)] = 0
                self.sems[("d", q, k)] = self.stack.enter_context(nc.semaphore("s_d%s%d" % (q, k)))

    def sb(self, name, shape, dt):
        return self.stack.enter_context(self.nc.sbuf_tensor(name, list(shape), dt))

    def ps(self, name, shape, dt=F32):
        return self.stack.enter_context(self.nc.psum_tensor(name, list(shape), dt))

    def _deps(self, eng, reads, writes):
        evs = []
        for k in reads:
            r = self.res.get(k)
            if r and r["w"] is not None:
                evs.append(r["w"])
        for k in writes:
            r = self.res.get(k)
            if r:
                if r["w"] is not None:
                    evs.append(r["w"])
                evs.extend(r["r"].items())
        waits = {}
        for (s, v) in evs:
            if eng == "pe" and s == "pe":
                continue
            if self.known[eng].get(s, 0) >= v:
                continue
            if waits.get(s, 0) < v:
                waits[s] = v
        for s, v in waits.items():
            self.known[eng][s] = v
        return list(waits.items())

    def _record(self, ev, reads, writes):
        for k in reads:
            r = self.res.setdefault(k, {"w": None, "r": {}})
            if r["r"].get(ev[0], 0) < ev[1]:
                r["r"][ev[0]] = ev[1]
        for k in writes:
            self.res[k] = {"w": ev, "r": {}}

    def op(self, eng, fn, reads=(), writes=(), inc=True):
        waits = self._deps(eng, reads, writes)
        if inc:
            self.cnt[eng] += 1
            ev = (eng, self.cnt[eng])
        else:
            ev = (eng, self.cnt[eng] + 1)
        self.ops[eng].append(("c", fn, waits, inc))
        self._record(ev, reads, writes)

    def dma(self, q, fn, reads=(), writes=(), inc=16):
        k = self.dma_rr[q][0]
        self.dma_rr[q][0] = (k + 1) % self.dma_rr[q][1]
        s = ("d", q, k)
        waits = dict(self._deps(q, reads, writes))
        cum = self.dma_cum[(q, k)]
        if cum > 0 and self.known[q].get(s, 0) < cum:
            waits[s] = cum
            self.known[q][s] = cum
        self.dma_cum[(q, k)] = cum + inc
        ev = (s, cum + inc)
        self.ops[q].append(("d", fn, list(waits.items()), (s, inc)))
        self._record(ev, reads, writes)

    def full_barrier(self):
        evs = [(e, self.cnt[e]) for e in ENGS if self.cnt[e] > 0]
        evs += [(("d", q_, k), v) for (q_, k), v in self.dma_cum.items() if v > 0]
        for e in ENGS:
            waits = []
            for s_, v in evs:
                if self.known[e].get(s_, 0) < v:
                    waits.append((s_, v))
                    self.known[e][s_] = v
            self.ops[e].append(("w", None, waits, False))
        self.res = {}

    def barrier_wait(self, eng, keys):
        waits = self._deps(eng, list(keys), list(keys))
        self.ops[eng].append(("w", None, waits, False))

    def emit(self):
        nc = self.nc
        handles = {"pe": "tensor", "act": "scalar", "dve": "vector", "pool": "gpsimd", "sp": "sync"}
        with nc.Block() as block:
            for e in ENGS:
                ops = self.ops[e]
                if not ops:
                    continue

                def body(eng, ops=ops, e=e):
                    for kind, fn, waits, inc in ops:
                        for s, v in waits:
                            eng.wait_ge(self.sems[s], v)
                        if kind == "w":
                            continue
                        ins = fn(eng)
                        if kind == "c":
                            if inc:
                                ins.then_inc(self.sems[e], 1)
                        else:
                            s, n = inc
                            ins.then_inc(self.sems[s], n)

                getattr(block, handles[e])(body)

    def close(self):
        self.stack.close()


class Arena:
    def __init__(self, t, n):
        self.t, self.n, self.off = t, n, 0

    def reset(self):
        self.off = 0

    def get(self, shape):
        n = int(np.prod(shape[1:]))
        assert self.off + n <= self.n, (self.off, n, self.n)
        ap = self.t[:, self.off:self.off + n]
        self.off += n
        if len(shape) == 3:
            ap = ap.rearrange("p (a b) -> p a b", a=shape[1])
        return ap


def build(nmt=NMT, nft=NFT, kf=KF, debug=False):
    nc = bass.Bass("TRN2", target_bir_lowering=False)

    def din(name, shape, dt=F32):
        return nc.dram_tensor(name, list(shape), dt, kind="ExternalInput").ap()

    xTm = din("xTm", [D, TE])
    mask = din("mask", [128, TE])
    w_in = din("w_in", [DIN // 128, 128, KD, 128])
    w_out = din("w_out", [D // 128, 128, KD, 128])
    w_up = din("w_up", [2 * DFF // 128, 128, KD, 128])
    w_down = din("w_down", [D // 128, 128, KF, 128])
    g1 = din("g1", [128, KD])
    g2 = din("g2", [128, KD])
    gf = din("gf", [128, KD])
    ga = din("ga", [128, 8])
    gb = din("gb", [128, 8])
    gsgu = din("gsgu", [128, DA])
    bsr = din("bsr", [128, 8 * 128])
    wsT = din("wsT", [128, 8 * 128])
    dww = din("dww", [128, KF * 3])
    dwb = din("dwb", [128, KF])
    xTb = din("xTb", [D, (L // TOK) * (TOK + 2)])
    zq = din("zq", [33, 2 * L])
    ttq = din("ttq", [128, 2 * L])
    m2q = din("m2q", [128, 2 * L])
    hw3d = din("hw3d", [64, 128])
    hfr2 = din("hfr2", [128, 1])
    hb32 = din("hb32", [128, 1])
    hwo2 = din("hwo2", [128, DB])
    negd = din("negd", [128, 8])
    hw1 = din("hw1", [33, 64])
    hw2 = din("hw2", [64, 64])
    hw3 = din("hw3", [64, 64])
    hb = din("hb", [64, 3])
    hfr = din("hfr", [64, 1])
    cw = din("cw", [128, 24 * 3])
    cb = din("cb", [128, 24])
    dsk = din("dsk", [128, 8])
    yT = nc.dram_tensor("yT", [D, TOK], F32, kind="ExternalOutput").ap()
    ud = nc.dram_tensor("ud", [DB, L], F32, kind="Internal").ap()
    uo = nc.dram_tensor("uo", [DB, TE], F32, kind="Internal").ap()
    b0 = nc.dram_tensor("b0", [DB, TE], F32, kind="Internal").ap()
    kd = nc.dram_tensor("kd", [DB, 2 * L], F32, kind="Internal").ap()
    ycd = nc.dram_tensor("ycd", [DB, 17 * 128], F32, kind="Internal").ap()
    fT1r = din("fT1r", [128, 512]); fT1i = din("fT1i", [128, 512])
    fT2r = din("fT2r", [128, 512]); fT2i = din("fT2i", [128, 512])
    fF1 = din("fF1", [128, 128])
    fF1d = din("fF1d", [64, 128])
    fF2r = din("fF2r", [128, 128]); fF2i = din("fF2i", [128, 128]); fF2n = din("fF2n", [128, 128])
    fG1 = din("fG1", [128, 256]); fG2 = din("fG2", [128, 256])
    fHc = din("fHc", [128, 32]); fHs = din("fHs", [128, 32])
    x1s = nc.dram_tensor("x1s", [D, TE], F32, kind="Internal").ap()
    ybs = nc.dram_tensor("ybs", [DB, TE], F32, kind="Internal").ap()

    P = Prog(nc)
    ones = P.sb("ones", [128, 128], F32)
    zero = P.sb("zero", [128, TM], F32)
    g1_s = P.sb("g1_s", [128, KD], F32)
    g2_s = P.sb("g2_s", [128, KD], F32)
    gf_s = P.sb("gf_s", [128, KD], F32)
    ga_s = P.sb("ga_s", [128, 8], F32)
    gb_s = P.sb("gb_s", [128, 8], F32)
    gsgu_s = P.sb("gsgu_s", [128, DA], F32)
    bsr_s = P.sb("bsr_s", [128, 8, 128], F32)
    wsT_s = P.sb("wsT_s", [128, 8, 128], BF16)
    dww_s = P.sb("dww_s", [128, KF, 3], F32)
    dwb_s = P.sb("dwb_s", [128, KF], F32)
    mask_s = P.sb("mask_s", [128, TF + 2], F32)
    NA32, NA16 = 19800, 52200
    a32 = Arena(P.sb("arena32", [128, NA32], F32), NA32)
    a16 = Arena(P.sb("arena16", [128, NA16], BF16), NA16)

    eps_s = P.sb("eps_s", [128, 1], F32)
    P.op("dve", lambda e: e.memset(eps_s[:], EPS), writes=["eps"])
    P.op("dve", lambda e: e.memset(ones[:], 1.0), writes=["ones"])
    P.op("dve", lambda e: e.memset(zero[:], 0.0), writes=["zero"])
    for (t, src, nm) in ((g1_s, g1, "g1"), (g2_s, g2, "g2"), (gf_s, gf, "gf"), (ga_s, ga, "ga"),
                         (gb_s, gb, "gb"), (gsgu_s, gsgu, "gsgu"), (dwb_s, dwb, "dwb")):
        P.dma("sp", lambda e, t=t, src=src: e.dma_start(out=t[:], in_=src), writes=[nm])
    P.dma("sp", lambda e: e.dma_start(out=bsr_s[:].rearrange("p h i -> p (h i)"), in_=bsr), writes=["bsr"])
    P.dma("sp", lambda e: e.dma_start(out=dww_s[:].rearrange("p k c -> p (k c)"), in_=dww), writes=["dww"])
    P.dma("pool", lambda e: e.dma_start(out=wsT_s[:].rearrange("p h i -> p (h i)"), in_=wsT), writes=["wsT"])

    NPS = 6
    pdb = [P.ps("pd%d" % i, [128, 1024]) for i in range(4)]
    psb = [pdb[i // 2][:, (i % 2) * 512:(i % 2 + 1) * 512] for i in range(8)]
    ps_st, ps_h = psb[6], psb[7]
    bank_n = [NPS]
    bank_p = [0]

    def next_ps():
        i = bank_p[0] % bank_n[0]
        bank_p[0] = i + 1
        return psb[i], "ps%d" % i

    def next_pd():
        i = bank_p[0] % bank_n[0]
        if i % 2:
            i = (i + 1) % bank_n[0]
        bank_p[0] = i + 2
        return pdb[i // 2][:, :], ["ps%d" % i, "ps%d" % (i + 1)]

    rstd = P.sb("rstd", [128, 520], F32)

    sqb = [P.sb("sqb%d" % i, [128, 520], F32) for i in range(2)]
    sq_rr = [0]

    def stats(src, nk, n, dim, srckeys, outkey="rstd", pieces=None, dst=None):
        pieces = pieces or [(0, n)]
        R = rstd if dst is None else dst
        for (a, b) in pieces:
            for k in range(nk):
                i = sq_rr[0]
                sq_rr[0] = 1 - i
                P.op("act", lambda e, i=i, k=k, a=a, b=b: e.activation(out=sqb[i][:, 0:b - a], in_=src[:, k, a:b], func=AF.Square),
                     reads=srckeys, writes=["sqb%d" % i])
                P.op("pe", lambda e, i=i, k=k, a=a, b=b: e.matmul(ps_st[:, 0:b - a], lhsT=ones[:], rhs=sqb[i][:, 0:b - a],
                                                                  start=(k == 0), stop=(k == nk - 1)),
                     reads=["sqb%d" % i, "ones"], writes=["ps6"])
            P.op("act", lambda e, a=a, b=b: e.activation(out=R[:, a:b], in_=ps_st[:, 0:b - a], func=AF.Sqrt,
                                                         scale=1.0 / dim, bias=eps_s[:, 0:1]),
                 reads=["ps6", "eps"], writes=[outkey])
        P.op("dve", lambda e: e.reciprocal(out=R[:, 0:n], in_=R[:, 0:n]), reads=[outkey], writes=[outkey])

    def wblk_ap(w, col):
        assert col % 128 == 0
        return w[col // 128]
    xTm_v = xTm.rearrange("(k p) t -> p k t", p=128)
    x1s_v = x1s.rearrange("(k p) t -> p k t", p=128)
    ybs_v = ybs.rearrange("(k p) t -> p k t", p=128)
    yT_v = yT.rearrange("(k p) t -> p k t", p=128)
    ybz_keys = []
    for c in range(DB // 128):
        for (a_, b_) in ((0, HALO - 1), (HALO + TOK + 1, TE)):
            key = "ybz_%d_%d" % (c, a_)
            ybz_keys.append(key)
            P.dma("sp", lambda e, c=c, a_=a_, b_=b_: e.dma_start(out=ybs[c * 128:(c + 1) * 128, a_:b_], in_=zero[:, 0:b_ - a_]),
                  reads=["zero"], writes=[key])

    NJ = TOK + 2
    E0 = HALO - 1
    TWO_PI = 2.0 * math.pi
    MAGIC = 12582912.0
    hw1_s = P.sb("hw1_s", [33, 64], F32)
    hw2_s = P.sb("hw2_s", [64, 64], F32)
    hw3_s = P.sb("hw3_s", [64, 64], F32)
    hb_s = P.sb("hb_s", [64, 3], F32)
    hfr_s = P.sb("hfr_s", [64, 1], F32)
    hfb_s = P.sb("hfb_s", [64, 3], F32)
    hwo_s = P.sb("hwo_s", [128, DB], BF16)
    hw3d_s = P.sb("hw3d_s", [64, 128], F32)
    hfr2_s = P.sb("hfr2_s", [128, 1], F32)
    hfb2_s = P.sb("hfb2_s", [128, 1], F32)
    cw_s = P.sb("cw_s", [128, 24, 3], F32)
    cb_s = P.sb("cb_s", [128, 24], F32)
    dsk_s = P.sb("dsk_s", [128, 8], F32)
    negd_s = P.sb("negd_s", [128, 8], F32)
    l1c = P.sb("l1c", [128, 40], F32)
    for (t, src, nm) in ((hw1_s, hw1, "hw1"), (hw2_s, hw2, "hw2"), (hw3_s, hw3, "hw3"), (hb_s, hb, "hb"), (hfr_s, hfr, "hfr"),
                         (cb_s, cb, "cb"), (dsk_s, dsk, "dsk"), (negd_s, negd, "negd")):
        P.dma("sp", lambda e, t=t, src=src: e.dma_start(out=t[:], in_=src), writes=[nm])
    P.dma("sp", lambda e: e.dma_start(out=cw_s[:].rearrange("p k c -> p (k c)"), in_=cw), writes=["cw"])
    P.dma("pool", lambda e: e.dma_start(out=hwo_s[:], in_=hwo2), writes=["hwo"])
    P.dma("sp", lambda e: e.dma_start(out=hw3d_s[:], in_=hw3d), writes=["hw3d"])
    P.dma("sp", lambda e: e.dma_start(out=hfr2_s[:], in_=hfr2), writes=["hfr2"])
    P.dma("sp", lambda e: e.dma_start(out=hfb2_s[:], in_=hb32), writes=["hfb2"])
    P.op("dve", lambda e: e.tensor_scalar(out=hfb2_s[:], in0=hfb2_s[:], scalar1=hfr2_s[:, 0:1], scalar2=None, op0=ALU.mult),
         reads=["hfb2", "hfr2"], writes=["hfb2"])
    P.op("dve", lambda e: e.tensor_scalar(out=hfb_s[:], in0=hb_s[:], scalar1=hfr_s[:, 0:1], scalar2=None, op0=ALU.mult),
         reads=["hb", "hfr"], writes=["hfb"])

    a16f = Arena(a16.t[:, :].bitcast(F32), NA16 // 2)

    def conv3(o, xin, kidx, W, keys_in, key_out):
        P.op("dve", lambda e: e.tensor_scalar(out=o[:, 0:W], in0=xin[:, 0:W], scalar1=cw_s[:, kidx, 1:2], scalar2=cb_s[:, kidx:kidx + 1],
                                              op0=ALU.mult, op1=ALU.add), reads=keys_in + ["cw", "cb"], writes=[key_out])
        P.op("dve", lambda e: e.scalar_tensor_tensor(out=o[:, 1:W], in0=xin[:, 0:W - 1], scalar=cw_s[:, kidx, 0:1], in1=o[:, 1:W],
                                                     op0=ALU.mult, op1=ALU.add), reads=keys_in + ["cw", key_out], writes=[key_out])
        P.op("dve", lambda e: e.scalar_tensor_tensor(out=o[:, 0:W - 1], in0=xin[:, 1:W], scalar=cw_s[:, kidx, 2:3], in1=o[:, 0:W - 1],
                                                     op0=ALU.mult, op1=ALU.add), reads=keys_in + ["cw", key_out], writes=[key_out])

    xTb_v = xTb.rearrange("(k p) t -> p k t", p=128)
    P.full_barrier()
    a32.reset(); a16.reset()
    WS = 2304
    xch = [a32.get([128, KD, 256]) for _ in range(2)]
    prb = [a32.get([128, WS]) for _ in range(2)]
    cvb = [a32.get([128, WS]) for _ in range(2)]
    hTs = a16.get([128, KD, WS])
    rstdS = a32.get([128, WS])
    wbl = [a16.get([128, KD, 128]) for _ in range(4)]
    w_rr = [0]
    x_rr = [0]

    def build_hT(src_v, col0, ncols):
        c = 0
        while c < ncols:
            w = min(256, ncols - c)
            i = x_rr[0]; x_rr[0] = 1 - i
            xt, xk = xch[i], "xch%d" % i
            P.dma("sp", lambda e, xt=xt, c=c, w=w: e.dma_start(out=xt[:, :, 0:w], in_=src_v[:, :, col0 + c:col0 + c + w]), writes=[xk])
            for k in range(KD):
                P.op("dve", lambda e, k=k, xt=xt, c=c, w=w: e.tensor_scalar(
                    out=hTs[:, k, c:c + w], in0=xt[:, k, 0:w], scalar1=g1_s[:, k:k + 1], scalar2=None, op0=ALU.mult),
                    reads=[xk, "g1"], writes=["hTs"])
            stats(xt, KD, w, D, [xk], outkey="rstdS_%d" % c, dst=rstdS[:, c:c + w])
            c += w

    def apply_rstd(buf, key, ncols):
        ks = ["rstdS_%d" % c_ for c_ in range(0, ncols, 256)]
        P.op("dve", lambda e: e.tensor_tensor(out=buf[:, 0:ncols], in0=buf[:, 0:ncols], in1=rstdS[:, 0:ncols], op=ALU.mult),
             reads=[key] + ks, writes=[key])

    def proj_block(ncols, wcol, dst, dkey):
        i = w_rr[0]; w_rr[0] = (i + 1) % 4
        P.dma("pool", lambda e: e.dma_start(out=wbl[i], in_=wblk_ap(w_in, wcol)), writes=["wbl%d" % i])
        c = 0
        nch_ = -(-ncols // 512)
        cw_ = -(-ncols // nch_)
        while c < ncols:
            w = min(cw_, ncols - c)
            pt, pk = next_ps()
            for k in range(KD):
                P.op("pe", lambda e, k=k, pt=pt, c=c, w=w: e.matmul(pt[:, 0:w], lhsT=wbl[i][:, k, :], rhs=hTs[:, k, c:c + w],
                                                                    start=(k == 0), stop=(k == KD - 1)),
                     reads=["wbl%d" % i, "hTs"], writes=[pk], inc=(k == KD - 1))
            P.op("act", lambda e, pt=pt, c=c, w=w: e.copy(out=dst[:, c:c + w], in_=pt[:, 0:w]), reads=[pk], writes=[dkey])
            c += w

    for st in range(L // TOK):
        build_hT(xTb_v, st * (TOK + 2), TOK + 2)
        for cc in range(8):
            proj_block(TOK + 2, 3 * DB + cc * 128, prb[0], "prb0")
            proj_block(TOK + 2, 4 * DB + cc * 128, prb[1], "prb1")
            apply_rstd(prb[0], "prb0", TOK + 2)
            apply_rstd(prb[1], "prb1", TOK + 2)
            conv3(cvb[0], prb[0], 8 + cc, TOK + 2, ["prb0"], "cvb0")
            conv3(cvb[1], prb[1], 16 + cc, TOK + 2, ["prb1"], "cvb1")
            P.op("dve", lambda e: e.tensor_tensor(out=cvb[0][:, 1:TOK + 1], in0=cvb[0][:, 1:TOK + 1], in1=cvb[1][:, 1:TOK + 1], op=ALU.mult),
                 reads=["cvb0", "cvb1"], writes=["cvb0"])
            P.dma("sp", lambda e, cc=cc, st=st: e.dma_start(out=ud[cc * 128:(cc + 1) * 128, st * TOK:(st + 1) * TOK],
                                                            in_=cvb[0][:, 1:TOK + 1]), reads=["cvb0"], writes=["ud_%d_%d" % (cc, st)])
    build_hT(xTm_v, 0, TE)
    for cc in range(8):
        proj_block(TE, 2 * DB + cc * 128, prb[0], "prb0")
        apply_rstd(prb[0], "prb0", TE)
        conv3(cvb[0], prb[0], cc, TE, ["prb0"], "cvb0")
        P.dma("sp", lambda e, cc=cc: e.dma_start(out=b0[cc * 128:(cc + 1) * 128, :], in_=cvb[0][:, 0:TE]), reads=["cvb0"], writes=["b0"])
    P.full_barrier()

    NFFT = 2 * L
    a32.reset(); a16.reset()
    t512 = [a32.get([128, 512]) for _ in range(3)]
    kch = [a32.get([128, 512]) for _ in range(2)]
    stg2 = [a32.get([128, 512]) for _ in range(2)]
    accA = a32.get([128, NJ]); b0t = a32.get([128, NJ])
    za = [a32.get([128, 512]) for _ in range(2)]
    zr = a32.get([128, 512])

    def bf512():
        return a32.get([128, 256]).bitcast(BF16)

    EV = {nm: [(bf512(), bf512()) for _ in range(2)] for nm in ("A", "B", "C", "D")}
    TMP = [[bf512() for _ in range(4)] for _ in range(2)]
    tmp_rr = [0]
    h3T = a16.get([128, 2 * L])
    Ub = [a16.get([128, 32, 128]) for _ in range(2)]
    Kb = [a16.get([128, 32, 128]) for _ in range(2)]
    AfrB = [a16.get([128, 512]) for _ in range(2)]; AfiB = [a16.get([128, 512]) for _ in range(2)]
    AdrB = [a16.get([128, 512]) for _ in range(2)]; AdiB = [a16.get([128, 512]) for _ in range(2)]
    YrB = [a16.get([128, 8, 64]) for _ in range(2)]; YiB = [a16.get([128, 8, 64]) for _ in range(2)]
    ZrB = [a16.get([128, 512]) for _ in range(2)]; ZiB = [a16.get([128, 512]) for _ in range(2)]
    KrB = [a16.get([128, 512]) for _ in range(2)]; KiB = [a16.get([128, 512]) for _ in range(2)]
    T1r8 = a16.get([128, 512]); T1i8 = a16.get([128, 512])
    T2r4 = a16.get([128, 512]); T2i4 = a16.get([128, 512])
    F1m = a16.get([128, 128])
    F1d = a16.get([128, 128])
    uot = a16.get([128, 2 * NJ]).bitcast(F32)
    F2r_m = a16.get([128, 128]); F2i_m = a16.get([128, 128]); F2n_m = a16.get([128, 128])
    G1m = a16.get([128, 256]); G2m = a16.get([128, 256])
    Hcm = a16.get([128, 32]); Hsm = a16.get([128, 32])
    for (dst_, src_, nm) in ((T1r8, fT1r, "T1r"), (T1i8, fT1i, "T1i"), (T2r4, fT2r, "T2r"), (T2i4, fT2i, "T2i"), (F1m, fF1, "F1m"), (F1d[0:64, :], fF1d, "F1d"), (F2r_m, fF2r, "F2r"), (F2i_m, fF2i, "F2i"), (F2n_m, fF2n, "F2n"),
                             (G1m, fG1, "G1"), (G2m, fG2, "G2"), (Hcm, fHc, "Hc"), (Hsm, fHs, "Hs")):
        P.dma("pool", lambda e, dst_=dst_, src_=src_: e.dma_start(out=dst_, in_=src_), writes=[nm])

    NPC = 2 * L // 512
    zaS = [za, [a32.get([128, 512]) for _ in range(2)]]
    zrS = [zr, a32.get([128, 512])]
    zinS = [t512[0], t512[1]]
    m2S = [t512[2], kch[0]]
    for pc0 in range(0, NPC, 2):
        st_ = []
        for s_i in range(2):
            pc = pc0 + s_i
            zc, zk = zinS[s_i], "zin%d" % s_i
            P.dma("sp", lambda e, pc=pc, zc=zc: e.dma_start(out=zc[0:33, :], in_=zq[:, pc * 512:(pc + 1) * 512]), writes=[zk])
            P.dma("sp", lambda e, pc=pc, s_i=s_i: e.dma_start(out=m2S[s_i], in_=m2q[:, pc * 512:(pc + 1) * 512]), writes=["m2_%d" % s_i])
            st_.append([zc, zk, 33])
        for li, wl in enumerate((hw1_s, hw2_s, hw3d_s)):
            nr = 64 if li < 2 else 128
            sc1 = hfr_s[:, 0:1] if li < 2 else hfr2_s[:, 0:1]
            sc2 = hfb_s[:, li:li + 1] if li < 2 else hfb2_s[:, 0:1]
            pts = []
            for s_i in range(2):
                cur, curk, kdim = st_[s_i]
                pt, pk = next_ps()
                pts.append((pt, pk))
                P.op("pe", lambda e, pt=pt, cur=cur, kdim=kdim, wl=wl, nr=nr: e.matmul(pt[0:nr, :], lhsT=wl[0:kdim, :], rhs=cur[0:kdim, :],
                                                                         start=True, stop=True),
                     reads=[curk, "hw%d" % (li + 1), "hw3d"], writes=[pk])
            aa = [(zaS[s_i][li % 2], "za%d_%d" % (s_i, li % 2)) for s_i in range(2)]
            zz = [(zrS[s_i], "zr%d" % s_i) for s_i in range(2)]
            for s_i in range(2):
                (pt, pk), (a_, ak) = pts[s_i], aa[s_i]
                P.op("dve", lambda e, pt=pt, a_=a_, nr=nr, sc1=sc1, sc2=sc2: e.tensor_scalar(out=a_[0:nr, :], in0=pt[0:nr, :], scalar1=sc1, scalar2=sc2,
                                                                    op0=ALU.mult, op1=ALU.add),
                     reads=[pk, "hfr", "hfb", "hfr2", "hfb2"], writes=[ak])
            for s_i in range(2):
                (a_, ak), (z_, zk_) = aa[s_i], zz[s_i]
                P.op("dve", lambda e, a_=a_, z_=z_, nr=nr: e.tensor_scalar(out=z_[0:nr, :], in0=a_[0:nr, :], scalar1=1.0 / TWO_PI, scalar2=MAGIC,
                                                                    op0=ALU.mult, op1=ALU.add), reads=[ak], writes=[zk_])
            for s_i in range(2):
                (z_, zk_) = zz[s_i]
                P.op("dve", lambda e, z_=z_, nr=nr: e.tensor_scalar(out=z_[0:nr, :], in0=z_[0:nr, :], scalar1=MAGIC, scalar2=-TWO_PI,
                                                             op0=ALU.subtract, op1=ALU.mult), reads=[zk_], writes=[zk_])
            for s_i in range(2):
                (a_, ak), (z_, zk_) = aa[s_i], zz[s_i]
                P.op("dve", lambda e, a_=a_, z_=z_, nr=nr: e.tensor_tensor(out=a_[0:nr, :], in0=a_[0:nr, :], in1=z_[0:nr, :], op=ALU.add),
                     reads=[ak, zk_], writes=[ak])
            for s_i in range(2):
                (a_, ak) = aa[s_i]
                P.op("dve", lambda e, a_=a_, nr=nr: e.tensor_scalar(out=a_[0:nr, :], in0=a_[0:nr, :], scalar1=-math.pi, scalar2=math.pi,
                                                             op0=ALU.max, op1=ALU.min), reads=[ak], writes=[ak])
            for s_i in range(2):
                (a_, ak) = aa[s_i]
                P.op("act", lambda e, a_=a_, nr=nr: e.activation(out=a_[0:nr, :], in_=a_[0:nr, :], func=AF.Sin), reads=[ak], writes=[ak])
                st_[s_i] = [a_, ak, 64]
            if li == 2:
                for s_i in range(2):
                    (a_, ak) = aa[s_i]
                    pc = pc0 + s_i
                    P.op("dve", lambda e, a_=a_, pc=pc, s_i=s_i: e.tensor_tensor(out=h3T[:, pc * 512:(pc + 1) * 512], in0=a_[:, :], in1=m2S[s_i],
                                                                                 op=ALU.mult),
                         reads=[ak, "m2_%d" % s_i], writes=["h3T"])
    P.full_barrier()

    ud_a = ud.rearrange("c (a r) -> a c r", r=128)
    kd_a = kd.rearrange("c (a r) -> a c r", r=128)
    yc_a = ycd.rearrange("c (a r) -> a c r", r=128)

    def cmul(o_re, o_im, p_re, p_im, t_re, t_im, rk, wk):
        i = tmp_rr[0]
        tmp_rr[0] = 1 - i
        t1, t2, t3, t4 = TMP[i]
        k1, k2, k3, k4 = ["tmp%d_%d" % (i, j) for j in range(4)]
        P.op("dve", lambda e: e.tensor_tensor(out=t1, in0=p_re, in1=t_re, op=ALU.mult), reads=rk, writes=[k1])
        P.op("dve", lambda e: e.tensor_tensor(out=t2, in0=p_im, in1=t_im, op=ALU.mult), reads=rk, writes=[k2])
        P.op("dve", lambda e: e.tensor_tensor(out=t3, in0=p_re, in1=t_im, op=ALU.mult), reads=rk, writes=[k3])
        P.op("dve", lambda e: e.tensor_tensor(out=t4, in0=p_im, in1=t_re, op=ALU.mult), reads=rk, writes=[k4])
        P.op("dve", lambda e: e.tensor_tensor(out=o_re, in0=t1, in1=t2, op=ALU.subtract), reads=[k1, k2], writes=wk[0:1])
        P.op("dve", lambda e: e.tensor_tensor(out=o_im, in0=t3, in1=t4, op=ALU.add), reads=[k3, k4], writes=wk[1:2])

    def evac(stage, gi, src_re, src_im, srck, shape3=None):
        er, ei = EV[stage][gi]
        kr, ki = "ev%s%d_r" % (stage, gi), "ev%s%d_i" % (stage, gi)
        o_r = er if shape3 is None else v3(er, *shape3)
        o_i = ei if shape3 is None else v3(ei, *shape3)
        P.op("act", lambda e: e.copy(out=o_r, in_=src_re), reads=srck, writes=[kr])
        P.op("act", lambda e: e.copy(out=o_i, in_=src_im), reads=srck, writes=[ki])
        return er, ei, [kr, ki]

    def v3(ap2d, a, b):
        return ap2d.rearrange("p (a b) -> p a b", a=a)

    def fwd(src, nrow, srck, A_re, A_im, akeys, via_pool=None):
        pd_, pdk = next_pd()
        for j in range(8):
            P.op("pe", lambda e, j=j: e.matmul(pd_[:, j * 128:(j + 1) * 128], lhsT=src[0:nrow, j, :], rhs=F1m[0:nrow, :],
                                               start=True, stop=True), reads=[srck, "F1m"], writes=pdk, inc=(j == 7))
        if via_pool is None:
            pv = pd_.rearrange("p (c t k) -> p c t k", c=8, t=2)
            cmul(v3(A_re, 8, 64), v3(A_im, 8, 64), pv[:, :, 0, :], pv[:, :, 1, :], v3(T1r8, 8, 64), v3(T1i8, 8, 64),
                 pdk + ["T1r", "T1i"], akeys)
        else:
            sf, sfk = via_pool
            P.op("act", lambda e: e.copy(out=sf, in_=pd_), reads=pdk, writes=[sfk])
            pv = sf.rearrange("p (c t k) -> p c t k", c=8, t=2)
            cmul(v3(A_re, 8, 64), v3(A_im, 8, 64), pv[:, :, 0, :], pv[:, :, 1, :], v3(T1r8, 8, 64), v3(T1i8, 8, 64),
                 [sfk, "T1r", "T1i"], akeys, eng="pool")
        xr, xrk = next_ps()
        xi, xik = next_ps()
        P.op("pe", lambda e: e.matmul(xr[:, :], lhsT=F2r_m, rhs=A_re, start=True, stop=False), reads=["F2r", akeys[0]], writes=[xrk], inc=False)
        P.op("pe", lambda e: e.matmul(xr[:, :], lhsT=F2n_m, rhs=A_im, start=False, stop=True), reads=["F2n", akeys[1]], writes=[xrk])
        P.op("pe", lambda e: e.matmul(xi[:, :], lhsT=F2i_m, rhs=A_re, start=True, stop=False), reads=["F2i", akeys[0]], writes=[xik], inc=False)
        P.op("pe", lambda e: e.matmul(xi[:, :], lhsT=F2r_m, rhs=A_im, start=False, stop=True), reads=["F2r", akeys[1]], writes=[xik])
        return xr, xi, [xrk, xik]

    l1cB = [l1c, P.sb("l1c2", [128, 40], F32)]

    kch.append(a32.get([128, 512]))

    def taps_slice(cc, pcs):
        l1 = l1cB[cc % 2]
        l1k = "l1c_%d" % (cc % 2)
        pcs = list(pcs)
        for g0 in range(0, len(pcs), 3):
            grp_ = pcs[g0:g0 + 3]
            pfs = []
            for pc in grp_:
                j = pc % 3
                sl = slice(pc * 512, (pc + 1) * 512)
                P.dma("sp", lambda e, j=j, sl=sl: e.dma_start(out=t512[j], in_=ttq[:, sl]), writes=["t512_%d" % j])
            for pc in grp_:
                j = pc % 3
                sl = slice(pc * 512, (pc + 1) * 512)
                pf, pfk = next_ps()
                pfs.append((pf, pfk))
                P.op("pe", lambda e, pf=pf, sl=sl: e.matmul(pf[:, :], lhsT=hwo_s[:, cc * 128:(cc + 1) * 128], rhs=h3T[:, sl],
                                                            start=True, stop=True), reads=["hwo", "h3T"], writes=[pfk])
                P.op("act", lambda e, j=j: e.activation(out=t512[j], in_=t512[j], func=AF.Exp, scale=negd_s[:, cc:cc + 1]),
                     reads=["t512_%d" % j, "negd"], writes=["t512_%d" % j])
            for pc, (pf, pfk) in zip(grp_, pfs):
                j = pc % 3
                P.op("dve", lambda e, pf=pf, j=j: e.tensor_tensor(out=kch[j], in0=pf[:, :], in1=t512[j], op=ALU.mult),
                     reads=[pfk, "t512_%d" % j], writes=["kch%d" % j])
            for pc in grp_:
                j = pc % 3
                sl = slice(pc * 512, (pc + 1) * 512)
                P.op("act", lambda e, j=j, pc=pc: e.activation(out=za[0], in_=kch[j], func=AF.Abs, accum_out=l1[:, pc:pc + 1]),
                     reads=["kch%d" % j], writes=["za0", l1k])
                P.dma("pool", lambda e, j=j, sl=sl: e.dma_start(out=kd[cc * 128:(cc + 1) * 128, sl], in_=kch[j]), reads=["kch%d" % j],
                      writes=["kd_%d_%d" % (cc, pc)])

    def taps_finish(cc):
        l1 = l1cB[cc % 2]
        l1k = "l1c_%d" % (cc % 2)
        P.op("dve", lambda e: e.tensor_reduce(out=l1[:, 32:33], in_=l1[:, 0:NPC], axis=mybir.AxisListType.X, op=ALU.add),
             reads=[l1k], writes=[l1k])
        P.op("dve", lambda e: e.tensor_scalar(out=l1[:, 32:33], in0=l1[:, 32:33], scalar1=EPS, scalar2=None, op0=ALU.add),
             reads=[l1k], writes=[l1k])
        P.op("dve", lambda e: e.reciprocal(out=l1[:, 33:34], in_=l1[:, 32:33]), reads=[l1k], writes=[l1k])

    def load_sub(cc, sc):
        slot = sc % 2
        cbase = cc * 128 + sc * 32
        P.dma("pool", lambda e: e.dma_start(out=Ub[slot][0:64, :, :], in_=ud_a[:, cbase:cbase + 32, :]),
              reads=["ud_%d_%d" % (cc, st_) for st_ in range(L // TOK)], writes=["Ub%d" % slot])
        P.dma("pool", lambda e: e.dma_start(out=Kb[slot], in_=kd_a[:, cbase:cbase + 32, :]),
              reads=["kd_%d_%d" % (cc, pc_) for pc_ in range(NPC)], writes=["Kb%d" % slot])

    class Grp:
        pass

    def mk(t):
        G_ = Grp()
        G_.cc, rem = divmod(t, 16)
        G_.sc, G_.g = divmod(rem, 4)
        G_.slot = G_.sc % 2
        G_.c0 = G_.cc * 128 + G_.sc * 32 + G_.g * 8
        gi = t % 2
        G_.gi = gi
        G_.sfx = "_%d" % gi
        G_.Afr, G_.Afi, G_.Adr, G_.Adi = AfrB[gi], AfiB[gi], AdrB[gi], AdiB[gi]
        G_.Yr, G_.Yi, G_.Zr, G_.Zi, G_.Kr, G_.Ki = YrB[gi], YiB[gi], ZrB[gi], ZiB[gi], KrB[gi], KiB[gi]
        return G_

    def s1(src, nrow, srck):
        pd_, pdk = next_pd()
        fm, fk = (F1m, "F1m") if nrow == 128 else (F1d, "F1d")
        for j in range(8):
            P.op("pe", lambda e, j=j: e.matmul(pd_[:, j * 128:(j + 1) * 128], lhsT=src[0:nrow, j, :], rhs=fm[0:nrow, :],
                                               start=True, stop=True), reads=[srck, fk], writes=pdk, inc=(j == 7))
        return pd_, pdk

    def s2(A_re, A_im, akeys):
        xr, xrk = next_ps()
        xi, xik = next_ps()
        P.op("pe", lambda e: e.matmul(xr[:, :], lhsT=F2r_m, rhs=A_re, start=True, stop=False), reads=["F2r", akeys[0]], writes=[xrk], inc=False)
        P.op("pe", lambda e: e.matmul(xr[:, :], lhsT=F2n_m, rhs=A_im, start=False, stop=True), reads=["F2n", akeys[1]], writes=[xrk])
        P.op("pe", lambda e: e.matmul(xi[:, :], lhsT=F2i_m, rhs=A_re, start=True, stop=False), reads=["F2i", akeys[0]], writes=[xik], inc=False)
        P.op("pe", lambda e: e.matmul(xi[:, :], lhsT=F2r_m, rhs=A_im, start=False, stop=True), reads=["F2r", akeys[1]], writes=[xik])
        return xr, xi, [xrk, xik]

    def stA(G_):
        pd_, pdk = s1(Kb[G_.slot][:, G_.g * 8:(G_.g + 1) * 8, :], 128, "Kb%d" % G_.slot)
        pv = pd_.rearrange("p (c t k) -> p c t k", c=8, t=2)
        er, ei, ek = evac("A", G_.gi, pv[:, :, 0, :], pv[:, :, 1, :], pdk, (8, 64))
        cmul(G_.Afr, G_.Afi, er, ei, T1r8, T1i8, ek + ["T1r", "T1i"], ["Afr" + G_.sfx, "Afi" + G_.sfx])

    def stB(G_):
        xr, xi, xk = s2(G_.Afr, G_.Afi, ["Afr" + G_.sfx, "Afi" + G_.sfx])
        P.op("act", lambda e: e.copy(out=G_.Kr, in_=xr[:, :]), reads=xk[0:1], writes=["Kr" + G_.sfx])
        P.op("act", lambda e: e.copy(out=G_.Ki, in_=xi[:, :]), reads=xk[1:2], writes=["Ki" + G_.sfx])
        pd_, pdk = s1(Ub[G_.slot][:, G_.g * 8:(G_.g + 1) * 8, :], 64, "Ub%d" % G_.slot)
        pv = pd_.rearrange("p (c t k) -> p c t k", c=8, t=2)
        er, ei, ek = evac("B", G_.gi, pv[:, :, 0, :], pv[:, :, 1, :], pdk, (8, 64))
        cmul(G_.Adr, G_.Adi, er, ei, T1r8, T1i8, ek + ["T1r", "T1i"], ["Adr" + G_.sfx, "Adi" + G_.sfx])

    def stC(G_):
        xr, xi, xk = s2(G_.Adr, G_.Adi, ["Adr" + G_.sfx, "Adi" + G_.sfx])
        er, ei, ek = evac("C", G_.gi, xr[:, :], xi[:, :], xk)
        cmul(G_.Yr.rearrange("p c k -> p (c k)"), G_.Yi.rearrange("p c k -> p (c k)"), er, ei, G_.Kr, G_.Ki,
             ek + ["Kr" + G_.sfx, "Ki" + G_.sfx], ["Yr" + G_.sfx, "Yi" + G_.sfx])

    def stD(G_):
        pz, pzk = next_pd()
        for j in range(8):
            p_, hf = divmod(j, 2)
            o_ = pz[hf * 64:(hf + 1) * 64, p_ * 256:(p_ + 1) * 256]
            P.op("pe", lambda e, o_=o_, j=j, hf=hf: e.matmul(o_, lhsT=G_.Yr[:, j, :], rhs=G1m, start=True, stop=False,
                                                             tile_position=(0, hf * 64)),
                 reads=["Yr" + G_.sfx, "G1"], writes=pzk, inc=False)
            P.op("pe", lambda e, o_=o_, j=j, hf=hf: e.matmul(o_, lhsT=G_.Yi[:, j, :], rhs=G2m, start=False, stop=True,
                                                             tile_position=(0, hf * 64)),
                 reads=["Yi" + G_.sfx, "G2"], writes=pzk, inc=(j == 7))
        zv_ = pz.rearrange("p (c t r) -> p c t r", c=4, t=2)
        er, ei, ek = evac("D", G_.gi, zv_[:, :, 0, :], zv_[:, :, 1, :], pzk, (4, 128))
        cmul(G_.Zr, G_.Zi, er, ei, T2r4, T2i4, ek + ["T2r", "T2i"], ["Zr" + G_.sfx, "Zi" + G_.sfx])

    def stE(G_):
        for hf in range(2):
            py, pyk = next_ps()
            lo = hf * 64
            P.op("pe", lambda e, py=py, lo=lo: e.matmul(py[0:32, :], lhsT=Hcm[lo:lo + 64, :], rhs=G_.Zr[lo:lo + 64, :],
                                                        start=True, stop=False), reads=["Hc", "Zr" + G_.sfx], writes=[pyk], inc=False)
            P.op("pe", lambda e, py=py, lo=lo: e.matmul(py[0:32, :], lhsT=Hsm[lo:lo + 64, :], rhs=G_.Zi[lo:lo + 64, :],
                                                        start=False, stop=True), reads=["Hs", "Zi" + G_.sfx], writes=[pyk])
            sg = stg2[hf]
            P.op("act", lambda e, py=py, sg=sg: e.copy(out=sg[0:32, :], in_=py[0:32, :]), reads=[pyk], writes=["stg2_%d" % hf])
            P.dma("pool", lambda e, sg=sg, hf=hf: e.dma_start(
                out=yc_a[0:17, G_.c0 + hf:G_.c0 + 8:2, :], in_=sg[0:17, :].rearrange("p (c r) -> p c r", c=4)),
                reads=["stg2_%d" % hf], writes=["ycd_%d_%d" % (G_.c0, hf)])

    def combine(cc):
        l1 = l1cB[cc % 2]
        l1k = "l1c_%d" % (cc % 2)
        P.dma("sp", lambda e: e.dma_start(out=accA, in_=ycd[cc * 128:(cc + 1) * 128, 0:NJ]),
              reads=["ycd_%d_%d" % (cc * 128 + g8 * 8, hf) for g8 in range(16) for hf in range(2)], writes=["accA"])
        P.dma("sp", lambda e: e.dma_start(out=b0t, in_=b0[cc * 128:(cc + 1) * 128, E0:E0 + NJ]), reads=["b0"], writes=["b0t"])
        udk = ["ud_%d_%d" % (cc, st_) for st_ in range(L // TOK)]
        P.dma("sp", lambda e: e.dma_start(out=uot[:, 0:1], in_=ud[cc * 128:(cc + 1) * 128, L - 1:L], allow_slow_non_contiguous=True), reads=udk, writes=["uot"])
        P.dma("sp", lambda e: e.dma_start(out=uot[:, 1:NJ], in_=ud[cc * 128:(cc + 1) * 128, 0:NJ - 1]), reads=udk, writes=["uot2"])
        P.op("dve", lambda e: e.tensor_scalar(out=accA, in0=accA, scalar1=l1[:, 33:34], scalar2=None, op0=ALU.mult),
             reads=["accA", l1k], writes=["accA"])
        P.op("dve", lambda e: e.scalar_tensor_tensor(out=accA, in0=uot, scalar=dsk_s[:, cc:cc + 1], in1=accA,
                                                     op0=ALU.mult, op1=ALU.add), reads=["uot", "uot2", "dsk", "accA"], writes=["accA"])
        P.op("dve", lambda e: e.tensor_tensor(out=accA, in0=accA, in1=b0t, op=ALU.mult), reads=["accA", "b0t"], writes=["accA"])
        P.dma("sp", lambda e: e.dma_start(out=ybs[cc * 128:(cc + 1) * 128, E0:E0 + NJ], in_=accA), reads=["accA"], writes=["ybs_%d" % cc])

    NG = 128
    bank_n[0] = 8
    taps_slice(0, range(NPC))
    taps_finish(0)
    load_sub(0, 0)
    grp = {}
    for t in range(NG + 4):
        if t < NG:
            grp[t] = mk(t)
            stA(grp[t])
        if 0 <= t - 1 < NG:
            stB(grp[t - 1])
        if 0 <= t - 2 < NG:
            stC(grp[t - 2])
        if 0 <= t - 3 < NG:
            stD(grp[t - 3])
        if 0 <= t - 4 < NG:
            stE(grp[t - 4])
            if (t - 4) % 16 == 15:
                combine((t - 4) // 16)
            del grp[t - 4]
        if t < NG:
            cc, rem = divmod(t, 16)
            if cc + 1 < 8 and rem < 11:
                lo_ = rem * 3
                hi_ = min(lo_ + 3, NPC)
                taps_slice(cc + 1, range(lo_, hi_))
                if rem == 10:
                    taps_finish(cc + 1)
            if rem % 4 == 0:
                nt = t + 4
                if nt < NG:
                    load_sub(nt // 16, (nt % 16) // 4)
    P.full_barrier()
    bank_n[0] = NPS
    bank_p[0] = 0
    a32.reset(); a16.reset()

    xTB = [a32.get([128, KD, TM]) for _ in range(3)]
    zu = a32.get([128, 8, TM])
    gav = a32.get([128, DA])
    ssq = a32.get([128, 2])
    ya = a32.get([128, 8, TM])
    yb = a32.get([128, 8, TM])
    stmp = a32.get([128, 128])
    hTB2 = [a16.get([128, KD, TM]) for _ in range(2)]
    wblk = [a16.get([128, KD, 128]) for i in range(4)]
    wv = [a16.get([128, KD, 512]) for i in range(2)]
    NRES = 6
    wres = [a16.get([128, KD, 128]) for i in range(NRES)]
    zv = a16.get([128, DA])
    junk = a16.get([128, DA])
    mT = a16.get([128, KD, TM])
    wb_rr = [0]

    def load_wblk(src_ap):
        i = wb_rr[0]
        wb_rr[0] = (i + 1) % 4
        P.dma("pool", lambda e: e.dma_start(out=wblk[i][:], in_=src_ap), writes=["wblk%d" % i])
        return wblk[i], "wblk%d" % i

    def load_wv():
        for half in range(2):
            for j in range(4):
                P.dma("pool", lambda e, half=half, j=j: e.dma_start(out=wv[half][:, :, j * 128:(j + 1) * 128],
                                                                    in_=wblk_ap(w_in, DA + half * 512 + j * 128)),
                      writes=["wv%d_%d" % (half, j)])

    def mF1a(m):
        xT, xk = xTB[m % 3], "xT%d" % (m % 3)
        P.dma("sp", lambda e: e.dma_start(out=xT, in_=xTm_v[:, :, m * TM:(m + 1) * TM]), writes=[xk])

    def mF1b(m):
        xT, xk = xTB[m % 3], "xT%d" % (m % 3)
        hT, hk = hTB2[m % 2], "hT%d" % (m % 2)
        stats(xT, KD, TM, D, [xk])
        for k in range(KD):
            P.op("dve", lambda e, k=k: e.scalar_tensor_tensor(out=hT[:, k, :], in0=xT[:, k, :], scalar=g1_s[:, k:k + 1],
                                                              in1=rstd[:, 0:TM], op0=ALU.mult, op1=ALU.mult),
                 reads=[xk, "g1", "rstd"], writes=[hk])

    def mF2(m):
        hT, hk = hTB2[m % 2], "hT%d" % (m % 2)
        for h in range(8):
            if h < NRES:
                wt, wk = wres[h], "wres%d" % h
            else:
                wt, wk = load_wblk(wblk_ap(w_in, h * 128))
            pt, pk = next_ps()
            for k in range(KD):
                P.op("pe", lambda e, k=k, wt=wt, pt=pt: e.matmul(pt[:, 0:TM], lhsT=wt[:, k, :], rhs=hT[:, k, :],
                                                                 start=(k == 0), stop=(k == KD - 1)),
                     reads=[wk, hk], writes=[pk], inc=(k == KD - 1))
            P.op("act", lambda e, h=h, pt=pt: e.activation(out=zu[:, h, :], in_=pt[:, 0:TM], func=AF.Gelu),
                 reads=[pk], writes=["zu"])
        for tb in range(TM // 128):
            for half in range(2):
                pt, pk = next_ps()
                for k in range(KD):
                    P.op("pe", lambda e, k=k, pt=pt, tb=tb, half=half: e.matmul(
                        pt[:, :], lhsT=hT[:, k, tb * 128:(tb + 1) * 128], rhs=wv[half][:, k, :],
                        start=(k == 0), stop=(k == KD - 1)),
                        reads=["wv%d_%d" % (half, j_) for j_ in range(4)] + [hk], writes=[pk], inc=(k == KD - 1))
                P.op("act", lambda e, pt=pt, half=half: e.activation(out=gav[:, half * 512:(half + 1) * 512], in_=pt[:, :],
                                                                     func=AF.Gelu), reads=[pk], writes=["gav"])
            P.op("act", lambda e: e.activation(out=junk, in_=gav, func=AF.Square, accum_out=ssq[:, 0:1]),
                 reads=["gav"], writes=["junk", "ssq"])
            P.op("act", lambda e: e.activation(out=ssq[:, 1:2], in_=ssq[:, 0:1], func=AF.Sqrt, scale=1.0 / DA, bias=eps_s[:, 0:1]),
                 reads=["ssq", "eps"], writes=["ssq"])
            P.op("dve", lambda e: e.reciprocal(out=ssq[:, 1:2], in_=ssq[:, 1:2]), reads=["ssq"], writes=["ssq"])
            P.op("dve", lambda e: e.scalar_tensor_tensor(out=zv, in0=gav, scalar=ssq[:, 1:2], in1=gsgu_s[:],
                                                         op0=ALU.mult, op1=ALU.mult),
                 reads=["gav", "ssq", "gsgu"], writes=["zv"])
            for h in range(8):
                P.op("pe", lambda e, h=h: e.matmul(ps_h[:, 0:128], lhsT=zv[:, h * 128:(h + 1) * 128], rhs=wsT_s[:, h, :],
                                                   start=True, stop=True), reads=["zv", "wsT"], writes=["ps7"])
                P.op("dve", lambda e, h=h: e.tensor_tensor(out=stmp, in0=ps_h[:, 0:128], in1=bsr_s[:, h, :], op=ALU.add),
                     reads=["ps7", "bsr"], writes=["stmp"])
                P.op("dve", lambda e, h=h, tb=tb: e.tensor_tensor(out=ya[:, h, tb * 128:(tb + 1) * 128], in0=stmp,
                                                                  in1=zu[:, h, tb * 128:(tb + 1) * 128], op=ALU.mult),
                     reads=["stmp", "zu"], writes=["ya"])

    def mB1a(m):
        P.dma("sp", lambda e: e.dma_start(out=yb, in_=ybs_v[:, :, m * TM:(m + 1) * TM]), reads=["ybs_%d" % c_ for c_ in range(8)] + ybz_keys, writes=["yb"])

    def mB1b(m):
        stats(ya, 8, TM, DA, ["ya"])
        for h in range(8):
            P.op("dve", lambda e, h=h: e.scalar_tensor_tensor(out=mT[:, h, :], in0=ya[:, h, :], scalar=ga_s[:, h:h + 1],
                                                              in1=rstd[:, 0:TM], op0=ALU.mult, op1=ALU.mult),
                 reads=["ya", "ga", "rstd"], writes=["mT"])
        stats(yb, 8, TM, DB, ["yb"])
        for h in range(8):
            P.op("dve", lambda e, h=h: e.scalar_tensor_tensor(out=mT[:, 8 + h, :], in0=yb[:, h, :], scalar=gb_s[:, h:h + 1],
                                                              in1=rstd[:, 0:TM], op0=ALU.mult, op1=ALU.mult),
                 reads=["yb", "gb", "rstd"], writes=["mT"])

    def mB2(m):
        xT, xk = xTB[m % 3], "xT%d" % (m % 3)
        for ob in range(KD):
            wt, wk = load_wblk(wblk_ap(w_out, ob * 128))
            pt, pk = next_ps()
            for k in range(KD):
                P.op("pe", lambda e, k=k, wt=wt, pt=pt: e.matmul(pt[:, 0:TM], lhsT=wt[:, k, :], rhs=mT[:, k, :],
                                                                 start=(k == 0), stop=(k == KD - 1)),
                     reads=[wk, "mT"], writes=[pk], inc=(k == KD - 1))
            P.op("dve", lambda e, ob=ob, pt=pt: e.tensor_tensor(out=xT[:, ob, :], in0=pt[:, 0:TM], in1=xT[:, ob, :], op=ALU.add),
                 reads=[pk, xk], writes=[xk])
        P.dma("sp", lambda e: e.dma_start(out=x1s_v[:, :, m * TM:(m + 1) * TM], in_=xT), reads=[xk], writes=["x1s"])

    load_wv()
    for h in range(NRES):
        P.dma("pool", lambda e, h=h: e.dma_start(out=wres[h], in_=wblk_ap(w_in, h * 128)), writes=["wres%d" % h])
    mF1a(0)
    mF1b(0)
    for it in range(nmt + 1):
        if it + 1 < nmt:
            mF1a(it + 1)
        if it < nmt:
            mB1a(it)
            mF2(it)
        if it - 1 >= 0:
            mB2(it - 1)
        if it + 1 < nmt:
            mF1b(it + 1)
        if it < nmt:
            mB1b(it)

    TH = TF + 2
    P.full_barrier()
    a32.reset()
    a16.reset()
    x1fB = [a32.get([128, KD, TH]) for _ in range(2)]
    gp = a32.get([128, TH])
    gc = a32.get([128, TF])
    ge = a32.get([128, TF])
    h2 = a16.get([128, KD, TH])
    act = a16.get([128, KF, TF])
    wg = [a16.get([128, KD, 128]) for i in range(2)]
    wu = [a16.get([128, KD, 128]) for i in range(2)]
    wd = [a16.get([128, KF, 128]) for i in range(2)]
    yT_keys = []

    def fPr(ft):
        x1f, xk = x1fB[ft % 2], "x1f%d" % (ft % 2)
        c0 = HALO + ft * TF - 1
        P.dma("sp", lambda e: e.dma_start(out=x1f, in_=x1s_v[:, :, c0:c0 + TH]), reads=["x1s"], writes=[xk])
        P.dma("sp", lambda e: e.dma_start(out=mask_s[:, 0:TH], in_=mask[:, c0:c0 + TH]), writes=["mask"])
        for cix in (0, TH - 1):
            P.op("dve", lambda e, cix=cix: e.tensor_scalar(out=x1f[:, :, cix:cix + 1], in0=x1f[:, :, cix:cix + 1],
                                                           scalar1=mask_s[:, cix:cix + 1], scalar2=None, op0=ALU.mult),
                 reads=[xk, "mask"], writes=[xk])
        stats(x1f, KD, TH, D, [xk], pieces=[(0, 512), (512, TH)])
        for k in range(KD):
            P.op("dve", lambda e, k=k: e.scalar_tensor_tensor(out=h2[:, k, :], in0=x1f[:, k, :], scalar=g2_s[:, k:k + 1],
                                                              in1=rstd[:, 0:TH], op0=ALU.mult, op1=ALU.mult),
                 reads=[xk, "g2", "rstd"], writes=["h2"])

    def fU(ft, fbs):
        for fb in fbs:
            i = fb % 2
            P.dma("pool", lambda e, i=i, fb=fb: e.dma_start(out=wg[i][:], in_=wblk_ap(w_up, fb * 128)),
                  writes=["wg%d" % i])
            P.dma("pool", lambda e, i=i, fb=fb: e.dma_start(out=wu[i][:], in_=wblk_ap(w_up, DFF + fb * 128)),
                  writes=["wu%d" % i])
            pg, pgk = next_ps()
            pg2, pg2k = next_ps()
            pv, pvk = next_ps()
            HH = TH // 2
            for k in range(KD):
                P.op("pe", lambda e, k=k, i=i, pg=pg: e.matmul(pg[:, 0:HH], lhsT=wg[i][:, k, :], rhs=h2[:, k, 0:HH],
                                                               start=(k == 0), stop=(k == KD - 1)),
                     reads=["wg%d" % i, "h2"], writes=[pgk], inc=(k == KD - 1))
            for k in range(KD):
                P.op("pe", lambda e, k=k, i=i, pg2=pg2: e.matmul(pg2[:, 0:TH - HH], lhsT=wg[i][:, k, :], rhs=h2[:, k, HH:TH],
                                                                 start=(k == 0), stop=(k == KD - 1)),
                     reads=["wg%d" % i, "h2"], writes=[pg2k], inc=(k == KD - 1))
            for k in range(KD):
                P.op("pe", lambda e, k=k, i=i, pv=pv: e.matmul(pv[:, 0:TF], lhsT=wu[i][:, k, :], rhs=h2[:, k, 1:TF + 1],
                                                               start=(k == 0), stop=(k == KD - 1)),
                     reads=["wu%d" % i, "h2"], writes=[pvk], inc=(k == KD - 1))
            P.op("act", lambda e, pg=pg: e.copy(out=gp[:, 0:HH], in_=pg[:, 0:HH]), reads=[pgk], writes=["gp"])
            P.op("act", lambda e, pg2=pg2: e.copy(out=gp[:, HH:TH], in_=pg2[:, 0:TH - HH]), reads=[pg2k], writes=["gp"])
            P.op("dve", lambda e, fb=fb: e.tensor_scalar(out=gc, in0=gp[:, 1:TF + 1], scalar1=dww_s[:, fb, 1:2],
                                                         scalar2=dwb_s[:, fb:fb + 1], op0=ALU.mult, op1=ALU.add),
                 reads=["gp", "dww", "dwb"], writes=["gc"])
            P.op("dve", lambda e, fb=fb: e.scalar_tensor_tensor(out=gc, in0=gp[:, 0:TF], scalar=dww_s[:, fb, 0:1],
                                                                in1=gc, op0=ALU.mult, op1=ALU.add),
                 reads=["gp", "dww", "gc"], writes=["gc"])
            P.op("dve", lambda e, fb=fb: e.scalar_tensor_tensor(out=gc, in0=gp[:, 2:TF + 2], scalar=dww_s[:, fb, 2:3],
                                                                in1=gc, op0=ALU.mult, op1=ALU.add),
                 reads=["gp", "dww", "gc"], writes=["gc"])
            P.op("act", lambda e: e.activation(out=ge, in_=gc, func=AF.Gelu), reads=["gc"], writes=["ge"])
            P.op("dve", lambda e, fb=fb, pv=pv: e.tensor_tensor(out=act[:, fb, :], in0=pv[:, 0:TF], in1=ge, op=ALU.mult),
                 reads=[pvk, "ge"], writes=["act%d" % fb])

    def fD(ft, obs):
        x1f, xk = x1fB[ft % 2], "x1f%d" % (ft % 2)
        for ob in obs:
            i = ob % 2
            P.dma("pool", lambda e, i=i, ob=ob: e.dma_start(out=wd[i][:, 0:kf, :], in_=w_down[ob][:, 0:kf, :]),
                  writes=["wd%d" % i])
            pt, pk = next_ps()
            for k in range(kf):
                P.op("pe", lambda e, k=k, i=i, pt=pt: e.matmul(pt[:, 0:TF], lhsT=wd[i][:, k, :], rhs=act[:, k, :],
                                                               start=(k == 0), stop=(k == kf - 1)),
                     reads=["wd%d" % i, "act%d" % k], writes=[pk], inc=(k == kf - 1))
            P.op("dve", lambda e, ob=ob, pt=pt: e.tensor_tensor(out=x1f[:, ob, 1:TF + 1], in0=pt[:, 0:TF], in1=x1f[:, ob, 1:TF + 1],
                                                                op=ALU.add), reads=[pk, xk], writes=[xk])

    def fN(ft):
        x1f, xk = x1fB[ft % 2], "x1f%d" % (ft % 2)
        stats(x1f[:, :, 1:TF + 1], KD, TF, D, [xk])
        for k in range(KD):
            P.op("dve", lambda e, k=k: e.scalar_tensor_tensor(out=x1f[:, k, 1:TF + 1], in0=x1f[:, k, 1:TF + 1], scalar=gf_s[:, k:k + 1],
                                                              in1=rstd[:, 0:TF], op0=ALU.mult, op1=ALU.mult),
                 reads=[xk, "gf", "rstd"], writes=[xk])
        yT_keys.append("yT%d" % ft)
        P.dma("sp", lambda e: e.dma_start(out=yT_v[:, :, ft * TF:(ft + 1) * TF], in_=x1f[:, :, 1:TF + 1]), reads=[xk], writes=["yT%d" % ft])

    NU0 = min(4, kf)
    fPr(0)
    fU(0, range(0, NU0))
    for ft in range(nft):
        fU(ft, range(NU0, kf))
        fD(ft, range(0, 4))
        if ft + 1 < nft:
            fPr(ft + 1)
        fD(ft, range(4, KD))
        if ft + 1 < nft:
            fU(ft + 1, range(0, NU0))
        fN(ft)

    P.barrier_wait("sp", yT_keys)
    P.emit()
    P.close()
    return nc


def _tile_w(w):
    w = np.asarray(w, np.float32)
    K, C = w.shape
    return np.ascontiguousarray(w.reshape(K // 128, 128, C // 128, 128).transpose(2, 1, 0, 3))


def _pk(v, nk):
    return np.ascontiguousarray(np.asarray(v, np.float32).reshape(nk, 128).T)


def prep_inputs(inputs):
    x = np.asarray(inputs["x"], np.float32)
    shared = {
        "w_in": _tile_w(inputs["w_in"][0]),
        "w_out": _tile_w(inputs["w_out"][0]),
        "w_up": _tile_w(inputs["ffn_w_up"][0]),
        "w_down": _tile_w(inputs["ffn_w_down"][0]),
        "g1": _pk(inputs["norm1_g"][0], KD),
        "g2": _pk(inputs["norm2_g"][0], KD),
        "gf": _pk(inputs["final_g"], KD),
        "ga": _pk(inputs["outnorm_a_g"][0], 8),
        "gb": _pk(inputs["outnorm_b_g"][0], 8),
        "gsgu": np.ascontiguousarray(np.broadcast_to(np.asarray(inputs["sgu_norm_g"][0], np.float32)[None, :], (128, DA))),
        "bsr": np.ascontiguousarray(np.broadcast_to(np.asarray(inputs["sgu_b"][0], np.float32).reshape(1, 8 * 128), (128, 8 * 128))),
        "wsT": np.ascontiguousarray(np.transpose(np.asarray(inputs["sgu_w"][0], np.float32), (2, 0, 1)).reshape(128, 8 * 128)),
        "dww": np.ascontiguousarray(np.transpose(np.asarray(inputs["ffn_dw_w"][0], np.float32).reshape(3, KF, 128), (2, 1, 0)).reshape(128, KF * 3)),
        "dwb": _pk(inputs["ffn_dw_b"][0], KF),
    }
    t_lin = np.linspace(0.0, 1.0, L, dtype=np.float32)
    wv_ = (np.float32(2.0 * math.pi / L) * np.arange(L, dtype=np.float32))[:, None]
    fb_ = np.linspace(1e-4, 15, 16, dtype=np.float32)[None]
    zfull = np.concatenate([t_lin[:, None], np.cos(fb_ * wv_), -np.sin(fb_ * wv_)], axis=-1).astype(np.float32)
    max_decay = math.log(1e-2) / 0.3
    min_decay = math.log(1e-2) / 1.5
    deltas = np.abs(np.linspace(min_decay, max_decay, DB, dtype=np.float32))
    shared.update({
        "negd": _pk(-deltas, 8),
        "hw1": np.ascontiguousarray(inputs["hy_f_w1"][0], dtype=np.float32),
        "hw2": np.ascontiguousarray(inputs["hy_f_w2"][0], dtype=np.float32),
        "hw3": np.ascontiguousarray(inputs["hy_f_w3"][0], dtype=np.float32),
        "hb": np.ascontiguousarray(np.stack([inputs["hy_f_b1"][0], inputs["hy_f_b2"][0], inputs["hy_f_b3"][0]], axis=1), dtype=np.float32),
        "hfr": np.ascontiguousarray(np.asarray(inputs["hy_f_freq"][0], np.float32).reshape(64, 1)),
        "hwo2": np.ascontiguousarray(np.concatenate([inputs["hy_f_wout"][0][:, :DB], inputs["hy_f_wout"][0][:, DB:]], axis=0), dtype=np.float32),
        "hw3d": np.ascontiguousarray(np.concatenate([inputs["hy_f_w3"][0], inputs["hy_f_w3"][0]], axis=1), dtype=np.float32),
        "hfr2": np.ascontiguousarray(np.concatenate([inputs["hy_f_freq"][0], inputs["hy_f_freq"][0]]).reshape(128, 1), dtype=np.float32),
        "hb32": np.ascontiguousarray(np.concatenate([inputs["hy_f_b3"][0], inputs["hy_f_b3"][0]]).reshape(128, 1), dtype=np.float32),
        "cw": np.ascontiguousarray(np.transpose(np.asarray(inputs["hy_conv_w"][0], np.float32).reshape(3, 24, 128), (2, 1, 0)).reshape(128, 72)),
        "cb": _pk(inputs["hy_conv_b"][0], 24),
        "dsk": _pk(inputs["hy_d_skip"][0], 8),
    })
    NF = 2 * L
    ar = np.arange(128, dtype=np.float64)
    kA = np.arange(64, dtype=np.float64) + 0.5
    th1 = 2 * np.pi * np.outer(ar, kA) / 128.0
    thT = 2 * np.pi * np.outer(ar, kA) / NF
    th2 = 2 * np.pi * np.outer(ar, ar) / 128.0
    hc = np.zeros((64, 32)); hs = np.zeros((64, 32))
    hc[:, :17] = (2.0 / NF) * np.cos(th1[:17].T)
    hs[:, :17] = -(2.0 / NF) * np.sin(th1[:17].T)
    f32c = lambda a_: np.ascontiguousarray(a_, dtype=np.float32)
    shared.update({
        "fF1": f32c(np.concatenate([np.cos(th1), -np.sin(th1)], axis=1)),
        "fT1r": f32c(np.tile(np.cos(thT), (1, 8))), "fT1i": f32c(np.tile(-np.sin(thT), (1, 8))),
        "fF2r": f32c(np.cos(th2)), "fF2i": f32c(-np.sin(th2)), "fF2n": f32c(np.sin(th2)),
        "fG1": f32c(np.concatenate([np.cos(th2), np.sin(th2)], axis=1)),
        "fG2": f32c(np.concatenate([-np.sin(th2), np.cos(th2)], axis=1)),
        "fT2r": f32c(np.tile(np.concatenate([np.cos(thT.T), np.cos(thT.T)], axis=0), (1, 4))),
        "fT2i": f32c(np.tile(np.concatenate([np.sin(thT.T), np.sin(thT.T)], axis=0), (1, 4))),
        "fHc": f32c(np.concatenate([hc, hc], axis=0)), "fHs": f32c(np.concatenate([hs, hs], axis=0)),
    })
    xpad = [np.concatenate([np.zeros((1, D), np.float32), x[b], np.zeros((1, D), np.float32)], axis=0) for b in range(2)]

    def xTb_for(b, q):
        tiles_ = []
        for i in range(4):
            qi = (q + i) % 4
            tiles_.append(xpad[b][TOK * qi:TOK * qi + TOK + 2])
        return np.ascontiguousarray(np.concatenate(tiles_, axis=0).T)

    F1full = np.concatenate([np.cos(th1), -np.sin(th1)], axis=1)
    rot = {}
    for q in range(4):
        s0 = TOK * q - 1
        n = (s0 + np.arange(2 * L)) % (2 * L)
        fwd = n < L
        bwd = n > L
        pos = np.where(fwd, n, np.where(bwd, 2 * L - n, 0))
        sgn = np.where(np.arange(2 * L) < L, 1.0, -1.0).astype(np.float32)
        rot[q] = {
            "zq": np.ascontiguousarray(zfull[pos].T),
            "ttq": np.ascontiguousarray(np.broadcast_to(t_lin[pos][None, :], (128, 2 * L))),
            "m2q": np.ascontiguousarray(np.concatenate([np.broadcast_to((fwd.astype(np.float32) * sgn)[None, :], (64, 2 * L)),
                                                        np.broadcast_to((bwd.astype(np.float32) * sgn)[None, :], (64, 2 * L))], axis=0)),
        }
    in_maps = []
    for c in range(NCORES):
        b, q = divmod(c, 4)
        lo = TOK * q - HALO
        xe = np.zeros((TE, D), np.float32)
        m = np.zeros((TE,), np.float32)
        s0 = max(lo, 0)
        s1 = min(lo + TE, L)
        xe[s0 - lo:s1 - lo] = x[b, s0:s1]
        m[s0 - lo:s1 - lo] = 1.0
        d = dict(shared)
        d.update(rot[q])
        d["xTb"] = xTb_for(b, q)
        a_s = (np.arange(64) + 16 * q) % 64
        d["fF1d"] = np.ascontiguousarray(F1full[a_s], dtype=np.float32)
        d["xTm"] = np.ascontiguousarray(xe.T)
        d["mask"] = np.ascontiguousarray(np.broadcast_to(m[None, :], (128, TE)))
        in_maps.append(d)
    return in_maps


def kernel(**inputs):
    in_maps = prep_inputs(inputs)
    nc = build()
    res = run_bass_kernel_spmd(nc, in_maps, core_ids=list(range(NCORES)))
    out = np.empty((2, L, D), np.float32)
    for c in range(NCORES):
        b, q = divmod(c, 4)
        out[b, TOK * q:TOK * (q + 1), :] = res.results[c]["yT"].T
    return out
```
